# Optimizing a Trainium2 kernel written in Bass

```python
import math
import jax, jax.numpy as jnp
from jax import lax
import numpy as np

D_MODEL = 1024
BATCH = 32
SEQ = 256
DEPTH = 4
DEC_BATCH = 8
DEC_SEQ = 4096
PAST_LEN = 256

GRID_W = 64
D_MIX = D_MODEL
D_CONV = D_MIX // 2
D_RET = D_MIX - D_CONV
N_RET_HEADS = 4
HEAD_DK = D_RET // N_RET_HEADS
HEAD_DV = D_RET // N_RET_HEADS
CONV_K = 31
CONV_PAD = (CONV_K - 1) // 2
CHUNK = 128
D_FF = 2816
N_SUB = 3
D_IN = 2 * D_CONV + 4 * D_RET
ALPHA = (2.0 * DEPTH) ** 0.25
BETA = (8.0 * DEPTH) ** -0.25
ROPE_BASE = 10000.0
EPS = 1e-5

kernel_name = 'hybrid_conv_retention_flow_step'


def layer_norm(x, g=None, b=None):
    xf = x.astype(jnp.float32)
    mu = jnp.mean(xf, -1, keepdims=True)
    var = jnp.mean(jnp.square(xf - mu), -1, keepdims=True)
    y = (xf - mu) * lax.rsqrt(var + EPS)
    if g is not None:
        y = y * g.astype(jnp.float32) + b.astype(jnp.float32)
    return y.astype(x.dtype)


def swiglu(h, w1, w2):
    a, b = jnp.split(h @ w1, 2, axis=-1)
    return (jax.nn.silu(a) * b) @ w2


def depthwise_conv(u, w, b):
    y = lax.conv_general_dilated(u, w.astype(u.dtype)[:, None, :], window_strides=(1,),
                                 padding=[(CONV_PAD, CONV_PAD)],
                                 dimension_numbers=('NWC', 'WIO', 'NWC'),
                                 feature_group_count=u.shape[-1])
    return y + b.astype(u.dtype)


def grid_rope_angles(rows):
    t = jnp.arange(rows * GRID_W)
    r = (t // GRID_W).astype(jnp.float32)
    col = (t % GRID_W).astype(jnp.float32)
    nf = HEAD_DK // 4
    inv = ROPE_BASE ** (-jnp.arange(nf, dtype=jnp.float32) / nf)
    return r[:, None] * inv, col[:, None] * inv


def rotate(x, ang):
    x1, x2 = jnp.split(x, 2, axis=-1)
    cos, sin = jnp.cos(ang), jnp.sin(ang)
    return jnp.concatenate([x1 * cos - x2 * sin, x1 * sin + x2 * cos], -1)


def apply_grid_rope(x, rope):
    ang_row, ang_col = rope
    xr, xc = jnp.split(x, 2, axis=-1)
    return jnp.concatenate([rotate(xr, ang_row), rotate(xc, ang_col)], -1)


def retention_chunkwise(q, k, v, log_g, r0):
    b, h, l, dk = q.shape
    dv = v.shape[-1]
    n = l // CHUNK
    qc = q.reshape(b, h, n, CHUNK, dk)
    kc = k.reshape(b, h, n, CHUNK, dk)
    vc = v.reshape(b, h, n, CHUNK, dv)
    idx = jnp.arange(CHUNK, dtype=jnp.float32)
    dist = idx[:, None] - idx[None, :]
    intra_decay = jnp.where(dist >= 0, jnp.exp(log_g[:, None, None] * jnp.maximum(dist, 0.0)), 0.0)
    scores = jnp.einsum('bhncd,bhnsd->bhncs', qc, kc) * intra_decay[None, :, None]
    o_intra = jnp.einsum('bhncs,bhnsv->bhncv', scores, vc)
    k_decay = jnp.exp(log_g[:, None] * (CHUNK - 1.0 - idx))
    q_decay = jnp.exp(log_g[:, None] * (idx + 1.0))
    chunk_kv = jnp.einsum('bhncd,hc,bhncv->nbhdv', kc, k_decay, vc)
    chunk_decay = jnp.exp(log_g * CHUNK)[None, :, None, None]

    def step(r, kv):
        return chunk_decay * r + kv, r

    r_final, r_prev = lax.scan(step, r0, chunk_kv)
    o_inter = jnp.einsum('bhncd,hc,nbhdv->bhncv', qc, q_decay, r_prev)
    return (o_intra + o_inter).reshape(b, h, l, dv), r_final


def mixer(h, w_in, w_out, conv_w, conv_b, conv_ln_g, conv_ln_b, log_gamma, r0_f, r0_b, rope):
    bsz, l, _ = h.shape
    p = h @ w_in
    ca, cg, q, k, v, g = jnp.split(
        p, [D_CONV, 2 * D_CONV, 2 * D_CONV + D_RET, 2 * D_CONV + 2 * D_RET, 2 * D_CONV + 3 * D_RET], axis=-1)
    u = ca * jax.nn.sigmoid(cg)
    if rope is None:
        u = depthwise_conv(u, conv_w, conv_b)
    else:
        rows = l // GRID_W
        u = depthwise_conv(u.reshape(bsz * rows, GRID_W, D_CONV), conv_w, conv_b).reshape(bsz, l, D_CONV)
    u = jax.nn.silu(layer_norm(u, conv_ln_g, conv_ln_b))

    def to_heads(t):
        return t.reshape(bsz, l, N_RET_HEADS, -1).transpose(0, 2, 1, 3).astype(jnp.float32)

    qh, kh, vh = to_heads(q), to_heads(k), to_heads(v)
    if rope is not None:
        qh = apply_grid_rope(qh, rope)
        kh = apply_grid_rope(kh, rope)
    qh = qh * (HEAD_DK ** -0.5)
    o_f, r_f = retention_chunkwise(qh, kh, vh, log_gamma[0], r0_f.astype(jnp.float32))
    o_b, r_b = retention_chunkwise(jnp.flip(qh, 2), jnp.flip(kh, 2), jnp.flip(vh, 2),
                                   log_gamma[1], r0_b.astype(jnp.float32))
    o = layer_norm(o_f + jnp.flip(o_b, 2))
    o = o.transpose(0, 2, 1, 3).reshape(bsz, l, D_RET).astype(h.dtype) * jax.nn.silu(g)
    y = jnp.concatenate([u, o], axis=-1) @ w_out
    return y, r_f, r_b


def layer(x, cond, w_ada, b_ada, ln_g, ln_b, ffn_w1, ffn_w2, w_in, w_out, conv_w, conv_b,
          conv_ln_g, conv_ln_b, log_gamma, r0_f, r0_b, rope):
    mod = (jax.nn.silu(cond) @ w_ada + b_ada)[:, None, :]
    m = jnp.split(mod, 3 * N_SUB, axis=-1)

    def modulate(t, i):
        return t * (1.0 + m[3 * i + 1]) + m[3 * i]

    x = layer_norm(ALPHA * x + 0.5 * m[2] * swiglu(modulate(x, 0), ffn_w1[0], ffn_w2[0]), ln_g[0], ln_b[0])
    y, r_f, r_b = mixer(modulate(x, 1), w_in, w_out, conv_w, conv_b, conv_ln_g, conv_ln_b,
                        log_gamma, r0_f, r0_b, rope)
    x = layer_norm(ALPHA * x + m[5] * y, ln_g[1], ln_b[1])
    x = layer_norm(ALPHA * x + 0.5 * m[8] * swiglu(modulate(x, 2), ffn_w1[1], ffn_w2[1]), ln_g[2], ln_b[2])
    return x, r_f, r_b


def setup_inputs(seed: int = 0) -> dict:
    key = jax.random.key(seed)
    ks = jax.random.split(key, 24)
    nrm = jax.random.normal
    f32 = jnp.float32
    base_logit = jnp.log(2.0 ** (5.0 + jnp.arange(N_RET_HEADS, dtype=f32)) - 1.0)
    return {
        'x_prompt': nrm(ks[0], (BATCH, SEQ, D_MODEL), f32),
        'x_sample': nrm(ks[1], (DEC_BATCH, DEC_SEQ, D_MODEL), f32),
        'state_ret_fwd': nrm(ks[2], (DEC_BATCH, DEPTH, N_RET_HEADS, HEAD_DK, HEAD_DV), f32),
        'state_ret_bwd': nrm(ks[3], (DEC_BATCH, DEPTH, N_RET_HEADS, HEAD_DK, HEAD_DV), f32),
        'c': nrm(ks[4], (DEC_BATCH, D_MODEL), f32),
        'c_ctx': nrm(ks[5], (D_MODEL,), f32),
        'w_ada': nrm(ks[6], (DEPTH, D_MODEL, 3 * N_SUB * D_MODEL), f32) * D_MODEL ** -0.5,
        'b_ada': 0.01 * nrm(ks[7], (DEPTH, 3 * N_SUB * D_MODEL), f32),
        'ln_g': 1.0 + 0.01 * nrm(ks[8], (DEPTH, N_SUB, D_MODEL), f32),
        'ln_b': 0.01 * nrm(ks[9], (DEPTH, N_SUB, D_MODEL), f32),
        'ffn_w1': nrm(ks[10], (DEPTH, 2, D_MODEL, 2 * D_FF), f32) * D_MODEL ** -0.5,
        'ffn_w2': nrm(ks[11], (DEPTH, 2, D_FF, D_MODEL), f32) * (BETA * D_FF ** -0.5),
        'w_in': nrm(ks[12], (DEPTH, D_MODEL, D_IN), f32) * D_MODEL ** -0.5,
        'w_out': nrm(ks[13], (DEPTH, D_MIX, D_MODEL), f32) * (BETA * D_MIX ** -0.5),
        'conv_w': nrm(ks[14], (DEPTH, CONV_K, D_CONV), f32) * CONV_K ** -0.5,
        'conv_b': 0.01 * nrm(ks[15], (DEPTH, D_CONV), f32),
        'conv_ln_g': 1.0 + 0.01 * nrm(ks[16], (DEPTH, D_CONV), f32),
        'conv_ln_b': 0.01 * nrm(ks[17], (DEPTH, D_CONV), f32),
        'ret_decay_logit': base_logit + 0.1 * nrm(ks[18], (DEPTH, 2, N_RET_HEADS), f32),
    }


def reference(x_prompt, x_sample, state_ret_fwd, state_ret_bwd, c, c_ctx, w_ada, b_ada, ln_g, ln_b,
              ffn_w1, ffn_w2, w_in, w_out, conv_w, conv_b, conv_ln_g, conv_ln_b, ret_decay_logit):
    rope = grid_rope_angles(x_sample.shape[1] // GRID_W)
    zero_state = jnp.zeros((x_prompt.shape[0], N_RET_HEADS, HEAD_DK, HEAD_DV), jnp.float32)
    ctx = x_prompt
    lat = x_sample
    ctx_cond = c_ctx[None, :]
    new_fwd = []
    new_bwd = []
    for l in range(DEPTH):
        log_gamma = jax.nn.log_sigmoid(ret_decay_logit[l].astype(jnp.float32))
        ctx, r_f, r_b = layer(ctx, ctx_cond, w_ada[l], b_ada[l], ln_g[l], ln_b[l], ffn_w1[l], ffn_w2[l],
                              w_in[l], w_out[l], conv_w[l], conv_b[l], conv_ln_g[l], conv_ln_b[l],
                              log_gamma, zero_state, zero_state, None)
        new_fwd.append(r_f)
        new_bwd.append(r_b)
        lat, _, _ = layer(lat, c, w_ada[l], b_ada[l], ln_g[l], ln_b[l], ffn_w1[l], ffn_w2[l],
                          w_in[l], w_out[l], conv_w[l], conv_b[l], conv_ln_g[l], conv_ln_b[l],
                          log_gamma, state_ret_fwd[:, l], state_ret_bwd[:, l], rope)
    new_state_ret_fwd = jnp.stack(new_fwd, axis=1)
    new_state_ret_bwd = jnp.stack(new_bwd, axis=1)
    return (ctx, lat, new_state_ret_fwd, new_state_ret_bwd)
```

```python
import contextlib
import math
import numpy as np
import concourse.bass as bass
import concourse.mybir as mybir
from concourse.bass_utils import run_bass_kernel_spmd

F32 = mybir.dt.float32
BF16 = mybir.dt.bfloat16
AF = mybir.ActivationFunctionType
ALU = mybir.AluOpType

D = 1024
DFF = 2816
NHC = 22
DEPTH = 4
ALPHA = (2.0 * DEPTH) ** 0.25
EPS = 1e-5
CK = 31
QSCALE = 128.0 ** -0.5
TWO_PI = 2.0 * math.pi
PREFETCH = True


class Res:
    __slots__ = ("name", "w", "rs")

    def __init__(self, name):
        self.name = name
        self.w = None
        self.rs = {}


class Eng:
    def __init__(self, name, h, sem, is_pe=False):
        self.name = name
        self.h = h
        self.sem = sem
        self.n = 0
        self.seen = {}
        self.is_pe = is_pe
        self.prog = []


class FW:
    def __init__(self, nc, stack):
        self.nc = nc
        self.stack = stack
        mk = lambda nm, h, pe=False: Eng(nm, h, stack.enter_context(nc.semaphore("s_" + nm)), pe)
        self.pe = mk("pe", nc.tensor, True)
        self.act = mk("act", nc.scalar)
        self.dve = mk("dve", nc.vector)
        self.pool = mk("pool", nc.gpsimd)
        self.sp = mk("sp", nc.sync)
        self.dma_sems = {}
        self.nres = 0

    def res(self, name=None):
        self.nres += 1
        return Res(name or f"r{self.nres}")

    def _wait(self, eng, dep):
        key, (sem, idx, is_eng) = dep
        if is_eng and key == eng.name and eng.is_pe:
            return
        if eng.seen.get(key, 0) >= idx:
            return
        eng.prog.append(lambda h=eng.h, s=sem, v=idx: h.wait_ge(s, v))
        eng.seen[key] = idx

    def _deps(self, reads, writes):
        deps = []
        for r in reads:
            if r.w is not None:
                deps.append(r.w)
        for w in writes:
            if w.w is not None:
                deps.append(w.w)
            deps.extend(w.rs.items())
        return deps

    def _mark(self, me, reads, writes):
        key, val = me
        for r in reads:
            r.rs[key] = val
        for w in writes:
            w.w = me
            w.rs = {}

    def op(self, eng, fn, reads=(), writes=()):
        for d in self._deps(reads, writes):
            self._wait(eng, d)
        eng.n += 1
        eng.prog.append(lambda fn=fn, s=eng.sem: fn().then_inc(s, 1))
        me = (eng.name, (eng.sem, eng.n, True))
        self._mark(me, reads, writes)
        return me

    def dma(self, qeng, out, in_, reads=(), writes=(), key=None):
        for d in self._deps(reads, writes):
            self._wait(qeng, d)
        if key not in self.dma_sems:
            self.dma_sems[key] = [self.stack.enter_context(self.nc.semaphore("d_%d" % len(self.dma_sems))), 0]
        ent = self.dma_sems[key]
        ent[1] += 16
        qeng.prog.append(lambda h=qeng.h, o=out, i=in_, s=ent[0]: h.dma_start(out=o, in_=i).then_inc(s, 16))
        me = (("dma", key), (ent[0], ent[1], False))
        self._mark(me, reads, writes)
        return me

    def finish(self, eng):
        for key, (sem, val) in self.dma_sems.items():
            if val:
                eng.prog.append(lambda h=eng.h, s=sem, v=val: h.wait_ge(s, v))
        for e2 in (self.pe, self.act, self.dve, self.pool, self.sp):
            if e2 is not eng and e2.n:
                eng.prog.append(lambda h=eng.h, s=e2.sem, v=e2.n: h.wait_ge(s, v))

    def run(self):
        with self.nc.Block() as block:
            @block.tensor
            def _(e):
                for t in self.pe.prog:
                    t()

            @block.scalar
            def _(e):
                for t in self.act.prog:
                    t()

            @block.vector
            def _(e):
                for t in self.dve.prog:
                    t()

            @block.gpsimd
            def _(e):
                for t in self.pool.prog:
                    t()

            @block.sync
            def _(e):
                for t in self.sp.prog:
                    t()


def build(NL=4, ST=4096, NP=4, stages=None, dbg=False):
    NTOK = ST + NP * 256
    NCH = NTOK // 128
    nc = bass.Bass("TRN2", target_bir_lowering=False)
    din = lambda n, s, dt=F32: nc.dram_tensor(n, s, dt, kind="ExternalInput").ap()
    dout = lambda n, s: nc.dram_tensor(n, s, F32, kind="ExternalOutput").ap()
    dint = lambda n, s, dt=F32: nc.dram_tensor(n, s, dt, kind="Internal").ap()
    xin = din("xin", [NTOK, D])
    stf = din("stf", [NL, 4, 128, 128])
    stb = din("stb", [NL, 4, 128, 128])
    cond = din("cond", [2, D])
    w_ada = din("w_ada", [NL, D, 9 * D])
    b_ada = din("b_ada", [NL, 9 * D])
    ln_g = din("ln_g", [NL, 3, D])
    ln_b = din("ln_b", [NL, 3, D])
    ffn_w1 = din("ffn_w1", [NL, 2, D, 2 * DFF])
    ffn_w2 = din("ffn_w2", [NL, 2, DFF, D])
    w_in = din("w_in", [NL, D, 3072])
    w_out = din("w_out", [NL, D, D])
    conv_w = din("conv_w", [NL, CK, 512])
    conv_b = din("conv_b", [NL, 512])
    conv_ln_g = din("conv_ln_g", [NL, 512])
    conv_ln_b = din("conv_ln_b", [NL, 512])
    dlogit = din("dlogit", [NL, 8])
    yout = dout("yout", [NTOK, D])
    nsf = dout("nsf", [max(NP, 1), NL, 4, 128, 128])
    nsb = dout("nsb", [max(NP, 1), NL, 4, 128, 128])
    xs = dint("xs", [NTOK, D])
    mods = dint("mods", [NL, 2, 9 * D])
    rope = (dout if dbg else dint)("rope", [2, 128, max(ST, 256)])
    rbd = dint("rbd", [NCH, 128, 512], BF16)

    tiles = []
    for t in range(ST // 256):
        tiles.append(dict(kind="S", tok0=t * 256, cond=0, first=(t == 0), idx=len(tiles)))
    for n in range(NP):
        tiles.append(dict(kind="P", tok0=ST + n * 256, cond=1, seq=n, idx=len(tiles)))
    NT = len(tiles)

    with contextlib.ExitStack() as st, nc.allow_non_contiguous_dma(reason="small vector gathers"):
        fw = FW(nc, st)
        pe, act, dve, pool, sp = fw.pe, fw.act, fw.dve, fw.pool, fw.sp
        sbt = lambda n, s, dt=F32: st.enter_context(nc.sbuf_tensor(n, s, dt))
        R = fw.res

        arena = sbt("arena", [128, 33 * 1024])
        aux = sbt("aux", [128, 4864])
        xbuf = sbt("xbuf", [128, 2, 2, D])
        hT = sbt("hT", [128, 2, 8, 256], BF16)
        zb = sbt("zb", [128, 2, D])
        gate = sbt("gate", [128, 2, D])
        lng = sbt("lng", [128, D])
        lnb = sbt("lnb", [128, D])
        rbt = sbt("rbt", [128, 2, 2, 512], BF16)
        ident = sbt("ident", [128, 128])
        identb = sbt("identb", [128, 128], BF16)
        sc1 = sbt("sc1", [128, 2, 8])
        shf = sbt("shf", [128, 2, 8])
        stats = sbt("stats", [128, 4, 12])
        mv = sbt("mv", [128, 4, 2])
        rstd = sbt("rstd", [128, 4, 2])
        gst = sbt("gst", [128, 2, 24])
        gmv = sbt("gmv", [128, 2, 8])
        grs = sbt("grs", [128, 2, 8])
        lgt = sbt("lgt", [128, 8])
        kdf = sbt("kdf", [128, 4])
        kdb = sbt("kdb", [128, 4])
        gC = sbt("gC", [128, 8])
        piota = sbt("piota", [128, 1])
        cw = sbt("cw", [128, 4, CK])
        cvec = sbt("cvec", [128, 3, 4])
        scT = sbt("scT", [128, 8, 2], BF16)
        condT = sbt("condT", [128, 8, 2])
        tiny = sbt("tiny", [128, 8])
        negpi = sbt("negpi", [128, 1])
        epst = sbt("epst", [128, 1])
        dmask = sbt("dmask", [128, 32])

        slot = [R(f"slot{i}") for i in range(33)]
        ov = R("ov")
        Rx = [[R(f"x{b}{s}") for s in range(2)] for b in range(2)]
        RhT = [R("hT0"), R("hT1")]
        Rzb = [R("zb0"), R("zb1")]
        Rgate, Rlng, Rlnb, Rsc = R("gate"), R("lng"), R("lnb"), R("sc")
        Rident = R("ident")
        Rst = [R(f"st{i}") for i in range(4)]
        Rgst = [R("gst0"), R("gst1")]
        Rdec = R("dec")
        Rcw = R("cw")
        Rrbt = [R("rbt0"), R("rbt1")]
        Rmods = R("mods")
        Rrope = R("rope")
        Rxs = [R(f"xs{t}") for t in range(NT)]
        Rrbd = [R(f"rbd{c}") for c in range(NCH)]
        Rmisc = R("misc")

        banks = [st.enter_context(nc.psum_tensor(f"bank{i}", [128, 512], F32)) for i in range(8)]
        Rb = [R(f"bank{i}") for i in range(8)]

        def aview(off, n, dt, pat=None, **kw):
            v = arena[:, off:off + n]
            if dt is not F32:
                v = v.bitcast(dt)
            if pat:
                v = v.rearrange(pat, **kw)
            return v

        def xview(off, n, dt, pat=None, **kw):
            v = aux[:, off:off + n]
            if dt is not F32:
                v = v.bitcast(dt)
            if pat:
                v = v.rearrange(pat, **kw)
            return v

        uT = xview(0, 2816, BF16, "p (j t) -> p j t", t=256)
        silt = xview(2816, 1024, F32, "p (a t) -> p a t", t=256)
        Qf = xview(0, 1024, F32, "p (h t) -> p h t", t=256)
        Qb = xview(1024, 1024, F32, "p (h t) -> p h t", t=256)
        upb = xview(2048, 752, BF16)
        upS = xview(2048, 752, BF16, "p (c r w) -> p c r w", r=4, w=94)
        upP = xview(2048, 572, BF16, "p (c w) -> p c w", w=286)
        Dblk = xview(2800, 1984, BF16, "p (c k j) -> p c k j", c=4, k=CK)
        o = [20 * 1024]

        def oa(n, dt, pat=None, **kw):
            v = aview(o[0], n, dt, pat, **kw)
            o[0] += n
            assert o[0] <= 33 * 1024
            return v

        Mcomb = oa(512, F32)
        qT = oa(512, BF16, "p (h t) -> p h t", t=256)
        qfT = oa(512, BF16, "p (h t) -> p h t", t=256)
        qbT = oa(512, BF16, "p (h t) -> p h t", t=256)
        kT = oa(512, BF16, "p (h t) -> p h t", t=256)
        ropeCS = oa(1024, F32, "p (b c t) -> p b c t", b=2, c=2)
        rt1 = oa(512, F32, "p (b t) -> p b t", b=2)
        rt2 = oa(512, F32, "p (b t) -> p b t", b=2)
        vbf = oa(512, BF16, "p (s c) -> p s c", s=2)
        sg = oa(1024, F32, "p (s c) -> p s c", s=2)
        sgm = oa(512, F32, "p (b t) -> p b t", b=2)
        acc = oa(1024, F32, "p (c t) -> p c t", c=4)
        xn = oa(1024, F32, "p (s c) -> p s c", s=2)
        mixT = oa(1024, BF16, "p (k t) -> p k t", t=256)
        Pm = oa(512, BF16, "p (b c) -> p b c", b=2)
        kdT = oa(512, BF16, "p (b c) -> p b c", b=2)
        Rf = oa(512, F32)
        Rbs = oa(512, F32)
        Rfb = oa(512, BF16, "p (b c) -> p b c", b=2)
        on = oa(512, F32)
        og = oa(256, BF16)
        og2 = oa(256, BF16)
        on2 = sbt("on2", [128, 512])
        onb = [on, on2[:]]
        ogb = [og, og2]
        cwraw = xn

        RqT, RkT, Rcs, Rrt, Rvbf, Rsg, Rup, Rsgm, Racc, Rxn, Rmix = (R("qT"), R("kT"), [R("cs0"), R("cs1")],
            [R("rt0"), R("rt1")], R("vbf"), R("sg"), R("up"), [R("sgm0"), R("sgm1")], [R(f"acc{i}") for i in range(4)],
            [R("xn0"), R("xn1")], R("mix"))
        RPm, RkdT, RRf, RRbs, RRfb, Ron, Rog = [R("P0"), R("P1")], [R("kd0"), R("kd1")], R("Rf"), R("Rbs"), [R("Rfb0"), R("Rfb1")], [R("on0"), R("on1")], [R("og0"), R("og1")]
        RuT, Rsil = R("uT"), [R("sil0"), R("sil1"), R("sil2"), R("sil3")]
        Rdb = R("dblk")

        V, A, G = nc.vector, nc.scalar, nc.gpsimd
        T = nc.tensor

        fw.op(pool, lambda: G.memset(ident[:], 0.0), writes=[Rident])
        fw.op(pool, lambda: G.affine_select(out=ident[:], in_=ident[:], compare_op=ALU.not_equal, fill=1.0, base=0,
                                            pattern=[[-1, 128]], channel_multiplier=1), reads=[Rident], writes=[Rident])
        fw.op(dve, lambda: V.tensor_copy(out=identb[:], in_=ident[:]), reads=[Rident], writes=[Rmisc])
        fw.op(pool, lambda: G.iota(piota[:], pattern=[[0, 1]], base=0, channel_multiplier=1,
                                   allow_small_or_imprecise_dtypes=True), writes=[Rmisc])
        fw.op(pool, lambda: G.memset(negpi[:], -math.pi), writes=[Rmisc])
        fw.op(pool, lambda: G.memset(epst[:], EPS), writes=[Rmisc])
        Rdm = R("dmask")

        fw.op(dve, lambda: V.tensor_tensor(out=dmask[:], in0=ident[:, 0:32], in1=ident[:, 32:64], op=ALU.add), reads=[Rident], writes=[Rdm])
        fw.op(dve, lambda: V.tensor_tensor(out=dmask[:], in0=dmask[:], in1=ident[:, 64:96], op=ALU.add), reads=[Rident, Rdm], writes=[Rdm])
        fw.op(dve, lambda: V.tensor_tensor(out=dmask[:], in0=dmask[:], in1=ident[:, 96:128], op=ALU.add), reads=[Rident, Rdm], writes=[Rdm])

        for c_ in range(2):
            fw.dma(sp, condT[:, :, c_], cond[c_, :].rearrange("(k p) -> p k", p=128), writes=[Rmisc], key=("c0", c_))
        fw.op(act, lambda: A.activation(out=scT[:], in_=condT[:], func=AF.Silu), reads=[Rmisc], writes=[Rsc])
        wab = arena[:, 0:6144].bitcast(BF16).rearrange("p (b k c) -> p b k c", b=3, k=8)
        badat = arena[0:2, 6144:6656]
        modrow = arena[0:2, 7168:9216].rearrange("p (m c) -> p m c", m=2)[:, :, 0:512]
        Rwab = [slot[0], slot[2], slot[4]]
        Rwab2 = [slot[1], slot[3], slot[5]]
        Rmr = [slot[7], slot[8]]
        Rbad = slot[6]
        k = 0
        for l in range(NL):
            for j in range(18):
                b3 = k % 3
                fw.dma(pool, wab[:, b3], w_ada[l, :, j * 512:(j + 1) * 512].rearrange("(k p) c -> p k c", p=128),
                       writes=[Rwab[b3], Rwab2[b3]], key=("wab", b3))
                fw.dma(sp, badat[:], b_ada[l:l + 1, j * 512:(j + 1) * 512].partition_broadcast(2), writes=[Rbad], key="bad")
                bk = Rb[k % 2]

                def mm(b3=b3, bank=banks[k % 2]):
                    for kc in range(8):
                        i = T.matmul(bank[0:2, :], lhsT=scT[:, kc, :], rhs=wab[:, b3, kc, :], start=(kc == 0), stop=(kc == 7))
                    return i
                fw.op(pe, mm, reads=[Rwab[b3], Rwab2[b3], Rsc], writes=[bk])
                m2 = k % 2
                fw.op(dve, lambda m2=m2, bank=banks[k % 2]: V.tensor_tensor(out=modrow[:, m2, :], in0=bank[0:2, :], in1=badat[:], op=ALU.add),
                      reads=[bk, Rbad], writes=[Rmr[m2]])
                fw.dma(sp, mods[l, :, j * 512:(j + 1) * 512], modrow[:, m2, :], reads=[Rmr[m2]], writes=[Rmods], key=("mr", m2))
                k += 1

        if ST > 0:
            invf = sbt("invf", [128, 1])
            MAGIC = 12582912.0

            def mkf():
                G.iota(tiny[0:64, 0:1], pattern=[[0, 1]], base=0, channel_multiplier=1, allow_small_or_imprecise_dtypes=True)
                return G.iota(tiny[64:128, 0:1], pattern=[[0, 1]], base=0, channel_multiplier=1, allow_small_or_imprecise_dtypes=True)
            fw.op(pool, mkf, reads=[ov, Rmisc], writes=[Rmisc])
            fw.op(dve, lambda: V.tensor_scalar(out=tiny[:, 2:3], in0=tiny[:, 0:1], scalar1=32.0, scalar2=32.0, op0=ALU.is_ge, op1=ALU.mult), reads=[ov, Rmisc], writes=[Rmisc])
            fw.op(dve, lambda: V.tensor_tensor(out=tiny[:, 0:1], in0=tiny[:, 0:1], in1=tiny[:, 2:3], op=ALU.subtract), reads=[ov, Rmisc], writes=[Rmisc])
            fw.op(act, lambda: A.activation(out=invf[:], in_=tiny[:, 0:1], func=AF.Exp, scale=-math.log(10000.0) / 32.0), reads=[ov, Rmisc], writes=[Rmisc])
            fw.op(dve, lambda: V.tensor_single_scalar(out=invf[:], in_=invf[:], scalar=1.0 / TWO_PI, op=ALU.mult), reads=[ov, Rmisc], writes=[Rmisc])
            posv = sg
            pa = posv[:, 0, 0:256]
            pr_ = posv[:, 0, 256:512]
            ps_ = posv[:, 1, 0:256]
            pc_ = posv[:, 1, 256:512]
            SC = 6.283184
            for t in range(ST // 256):
                def mkpos(t=t):
                    G.iota(pa[0:64, :], pattern=[[1, 4], [0, 64]], base=4 * t, channel_multiplier=0, allow_small_or_imprecise_dtypes=True)
                    return G.iota(pa[64:128, :], pattern=[[0, 4], [1, 64]], base=0, channel_multiplier=0, allow_small_or_imprecise_dtypes=True)
                fw.op(pool, mkpos, reads=[ov], writes=[Rsg])
                fw.op(dve, lambda: V.tensor_scalar(out=pa, in0=pa, scalar1=invf[:, 0:1], scalar2=None, op0=ALU.mult), reads=[ov, Rsg, Rmisc], writes=[Rsg])
                for (dstv, off, rr) in ((ps_, 0.0, Rxn[0]), (pc_, 0.25, Rxn[1])):
                    fw.op(dve, lambda dstv=dstv, off=off: V.tensor_single_scalar(out=dstv, in_=pa, scalar=off, op=ALU.add), reads=[ov, Rsg], writes=[rr])
                    fw.op(dve, lambda dstv=dstv: V.tensor_single_scalar(out=pr_, in_=dstv, scalar=MAGIC, op=ALU.add), reads=[ov, rr], writes=[Rsgm[0]])
                    fw.op(dve, lambda: V.tensor_single_scalar(out=pr_, in_=pr_, scalar=MAGIC, op=ALU.subtract), reads=[ov, Rsgm[0]], writes=[Rsgm[0]])
                    fw.op(dve, lambda dstv=dstv: V.tensor_tensor(out=dstv, in0=dstv, in1=pr_, op=ALU.subtract), reads=[ov, rr, Rsgm[0]], writes=[rr])
                    fw.op(act, lambda dstv=dstv: A.activation(out=dstv, in_=dstv, func=AF.Sin, scale=SC), reads=[ov, rr], writes=[rr])
                fw.dma(sp, rope[1, :, t * 256:(t + 1) * 256], ps_, reads=[ov, Rxn[0]], writes=[Rrope], key="rp0")
                fw.dma(sp, rope[0, :, t * 256:(t + 1) * 256], pc_, reads=[ov, Rxn[1]], writes=[Rrope], key="rp1")

        def stage_consts(l, sl, half):
            for c in range(2):
                fw.dma(sp, gate[:, c, :], mods[l, c:c + 1, (3 * sl + 2) * D:(3 * sl + 3) * D].partition_broadcast(128),
                       reads=[Rmods], writes=[Rgate], key=("gate", c))
                fw.dma(sp, sc1[:, c, :], mods[l, c, (3 * sl + 1) * D:(3 * sl + 2) * D].rearrange("(k p) -> p k", p=128),
                       reads=[Rmods], writes=[Rsc], key=("sc1", c))
                fw.dma(sp, shf[:, c, :], mods[l, c, (3 * sl) * D:(3 * sl + 1) * D].rearrange("(k p) -> p k", p=128),
                       reads=[Rmods], writes=[Rsc], key=("shf", c))
            fw.dma(sp, lng[:], ln_g[l, sl:sl + 1, :].partition_broadcast(128), writes=[Rlng], key="lng")
            fw.dma(sp, lnb[:], ln_b[l, sl:sl + 1, :].partition_broadcast(128), writes=[Rlnb], key="lnb")
            fw.op(dve, lambda: V.tensor_single_scalar(out=sc1[:], in_=sc1[:], scalar=1.0, op=ALU.add), reads=[Rsc], writes=[Rsc])
            if half:
                fw.op(pool, lambda: G.tensor_single_scalar(out=gate[:], in_=gate[:], scalar=0.5, op=ALU.mult), reads=[Rgate], writes=[Rgate])

        def load_x(src, tl, b):
            for s in range(2):
                r0 = tl["tok0"] + s * 128
                fw.dma(sp, xbuf[:, b, s, :], src[r0:r0 + 128, :], reads=[Rxs[tl["idx"]]], writes=[Rx[b][s]], key=("x", b, s))

        def prologue(tl, b):
            c = tl["cond"]
            for pr in range(4):
                bk = pr % 2

                def tr(pr=pr, bk=bk):
                    for cc in range(2):
                        for s in range(2):
                            fc = pr * 2 + cc
                            i = T.transpose(out=banks[bk][:, cc * 256 + s * 128: cc * 256 + (s + 1) * 128],
                                            in_=xbuf[:, b, s, fc * 128:(fc + 1) * 128], identity=ident[:])
                    return i
                fw.op(pe, tr, reads=[Rx[b][0], Rx[b][1], Rident], writes=[Rb[bk]])

                def ev(pr=pr, bk=bk):
                    for cc in range(2):
                        fc = pr * 2 + cc
                        i = A.activation(out=hT[:, b, fc, :], in_=banks[bk][:, cc * 256:(cc + 1) * 256], func=AF.Identity,
                                         scale=sc1[:, c, fc:fc + 1], bias=shf[:, c, fc:fc + 1])
                    return i
                fw.op(act, ev, reads=[Rb[bk], Rsc], writes=[RhT[b]])

        def epilogue(tl, b, s, ybanks, dst, zi):
            c = tl["cond"]
            z = zb[:, zi, :]
            rz = Rzb[zi]
            for hf in range(2):
                fw.op(dve, lambda hf=hf: V.tensor_tensor(out=z[:, hf * 512:(hf + 1) * 512], in0=banks[ybanks[hf]][:, :],
                                                         in1=gate[:, c, hf * 512:(hf + 1) * 512], op=ALU.mult),
                      reads=[Rb[ybanks[hf]], Rgate], writes=[rz])
            fw.op(dve, lambda: V.scalar_tensor_tensor(out=z, in0=xbuf[:, b, s, :], scalar=ALPHA, in1=z, op0=ALU.mult, op1=ALU.add),
                  reads=[Rx[b][s], rz], writes=[rz])
            ln_rows(z, rz, zi, D)
            fw.op(act, lambda: A.activation(out=z, in_=z, func=AF.Identity, scale=rstd[:, zi, 0:1], bias=rstd[:, zi, 1:2]),
                  reads=[rz, Rst[zi]], writes=[rz])
            fw.op(pool, lambda: G.tensor_tensor(out=z, in0=z, in1=lng[:], op=ALU.mult), reads=[rz, Rlng], writes=[rz])
            fw.op(pool, lambda: G.tensor_tensor(out=z, in0=z, in1=lnb[:], op=ALU.add), reads=[rz, Rlnb], writes=[rz])
            r0 = tl["tok0"] + s * 128
            fw.dma(sp, dst[r0:r0 + 128, :], z, reads=[rz], writes=[Rxs[tl["idx"]]], key=("xo", zi))

        def ln_rows(src, rsrc, si, n):
            nchunk = n // 512

            def bs():
                for q in range(nchunk):
                    i = V.bn_stats(out=stats[:, si, q * 6:(q + 1) * 6], in_=src[:, q * 512:(q + 1) * 512])
                return i
            fw.op(dve, bs, reads=[rsrc], writes=[Rst[si]])
            fw.op(dve, lambda: V.bn_aggr(out=mv[:, si, :], in_=stats[:, si, 0:6 * nchunk]), reads=[Rst[si]], writes=[Rst[si]])
            fw.op(act, lambda: A.activation(out=rstd[:, si, 0:1], in_=mv[:, si, 1:2], func=AF.Sqrt, bias=epst[:, 0:1]), reads=[Rst[si], Rmisc], writes=[Rst[si]])
            fw.op(dve, lambda: V.reciprocal(out=rstd[:, si, 0:1], in_=rstd[:, si, 0:1]), reads=[Rst[si]], writes=[Rst[si]])
            fw.op(dve, lambda: V.scalar_tensor_tensor(out=rstd[:, si, 1:2], in0=mv[:, si, 0:1], scalar=-1.0, in1=rstd[:, si, 0:1], op0=ALU.mult, op1=ALU.mult),
                  reads=[Rst[si]], writes=[Rst[si]])

        claim_t = tiny

        def claim(extra=()):
            fw.op(pool, lambda: G.memset(claim_t[:, 4:5], 0.0), writes=[ov] + list(extra))

        def w1_dma(l, i, j):
            for part in range(2):
                dstv = aview(j * 2048, 2048, BF16, "p (k c) -> p k c", c=512)[:, :, part * 256:(part + 1) * 256]
                c0 = part * DFF + j * 256
                fw.dma(pool, dstv, ffn_w1[l, i, :, c0:c0 + 256].rearrange("(k p) c -> p k c", p=128),
                       reads=([ov] if 2 * j + 1 >= 20 else []), writes=[slot[2 * j], slot[2 * j + 1]], key=("w", 2 * j, part))

        def w2_dma(l, i, j):
            dstv = aview(22 * 1024 + j * 1024, 1024, BF16, "p (e c) -> p e c", e=2)
            fw.dma(pool, dstv, ffn_w2[l, i, j * 256:(j + 1) * 256, :].rearrange("(e p) c -> p e c", p=128),
                   reads=[ov], writes=[slot[22 + j]], key=("w", 22 + j, 0))

        def mixer_dma(l, m):
            if m < 6:
                fw.dma(pool, WIN(m), w_in[l, :, m * 512:(m + 1) * 512].rearrange("(k p) c -> p k c", p=128),
                       writes=[slot[2 * m], slot[2 * m + 1]], key=("w", 2 * m, 0))
            elif m in (8, 9):
                hf = m - 8
                fw.dma(pool, WIN(8 + hf), w_out[l, :, hf * 512:(hf + 1) * 512].rearrange("(k p) c -> p k c", p=128),
                       writes=[slot[16 + 2 * hf], slot[17 + 2 * hf]], key=("w", 16 + 2 * hf, 0))

        def stage_F(l, i, src, dst, have_w=False, nxt=None):
            sl = 0 if i == 0 else 2
            if not have_w:
                claim()
                for j in range(11):
                    w1_dma(l, i, j)
                for j in range(11):
                    w2_dma(l, i, j)
            stage_consts(l, sl, True)
            load_x(src, tiles[0], 0)
            zi = 0
            prologue(tiles[0], 0)
            for ti, tl in enumerate(tiles):
                b = ti % 2
                if ti + 1 < NT:
                    load_x(src, tiles[ti + 1], 1 - b)
                for hc in range(NHC):
                    j, e = hc // 2, hc % 2
                    w1v = aview(j * 2048, 2048, BF16, "p (k c) -> p k c", c=512)
                    bk = hc % 4

                    def mm(w1v=w1v, e=e, bk=bk, b=b):
                        for part in range(2):
                            for kc in range(8):
                                i_ = T.matmul(banks[bk][:, part * 256:(part + 1) * 256],
                                              lhsT=w1v[:, kc, part * 256 + e * 128: part * 256 + (e + 1) * 128],
                                              rhs=hT[:, b, kc, :], start=(kc == 0), stop=(kc == 7))
                        return i_
                    fw.op(pe, mm, reads=[slot[2 * j], slot[2 * j + 1], RhT[b]], writes=[Rb[bk]])
                    sb_ = hc % 4
                    fw.op(act, lambda bk=bk, sb_=sb_: A.activation(out=silt[:, sb_, :], in_=banks[bk][:, 0:256], func=AF.Silu),
                          reads=[Rb[bk], ov], writes=[Rsil[sb_]])
                    fw.op(dve, lambda bk=bk, sb_=sb_, hc=hc: V.tensor_tensor(out=uT[:, hc, :], in0=banks[bk][:, 256:512], in1=silt[:, sb_, :], op=ALU.mult),
                          reads=[Rb[bk], Rsil[sb_], ov], writes=[RuT])
                    if ti == NT - 1 and nxt is not None and e == 1:
                        if nxt[0] == "F":
                            w1_dma(nxt[1], nxt[2], j)
                        else:
                            mixer_dma(nxt[1], j)
                if ti + 1 < NT:
                    prologue(tiles[ti + 1], 1 - b)
                for s in range(2):
                    for hf in range(2):
                        bk = 4 + s * 2 + hf

                        def mm2(s=s, hf=hf, bk=bk):
                            for hc in range(NHC):
                                w2v = aview(22 * 1024 + (hc // 2) * 1024, 1024, BF16, "p (e c) -> p e c", e=2)
                                i_ = T.matmul(banks[bk][:, :], lhsT=uT[:, hc, s * 128:(s + 1) * 128],
                                              rhs=w2v[:, hc % 2, hf * 512:(hf + 1) * 512], start=(hc == 0), stop=(hc == NHC - 1))
                            return i_
                        fw.op(pe, mm2, reads=[RuT, ov] + [slot[22 + q] for q in range(11)], writes=[Rb[bk]])
                if ti == NT - 1 and nxt is not None and nxt[0] == "F":
                    for j in range(11):
                        w2_dma(nxt[1], nxt[2], j)
                for s in range(2):
                    epilogue(tl, b, s, (4 + s * 2, 5 + s * 2), dst, zi)
                    zi = 1 - zi

        def WIN(m):
            return aview(m * 2048, 2048, BF16, "p (k c) -> p k c", c=512)

        def load_mixer(l, have_w=False):
            if not have_w:
                for m in (0, 1, 2, 3, 4, 5, 8, 9):
                    mixer_dma(l, m)
            for m in (2, 3):
                srcv = aview(m * 2048, 2048, BF16, "p (a two f) -> p a two f", two=2, f=32)
                dstv = aview((m + 4) * 2048, 2048, BF16, "p (a two f) -> p a two f", two=2, f=32)
                fw.op(act, lambda srcv=srcv, dstv=dstv: A.mul(out=dstv[:, :, 0, :], in_=srcv[:, :, 1, :], mul=-1.0),
                      reads=[slot[2 * m], slot[2 * m + 1]], writes=[slot[2 * m + 8], slot[2 * m + 9]])
                fw.op(act, lambda srcv=srcv, dstv=dstv: A.copy(out=dstv[:, :, 1, :], in_=srcv[:, :, 0, :]),
                      reads=[slot[2 * m], slot[2 * m + 1]], writes=[slot[2 * m + 8], slot[2 * m + 9]])

        def layer_consts(l):
            fw.dma(sp, lgt[:], dlogit[l:l + 1, :].partition_broadcast(128), writes=[Rdec], key="lgt")
            fw.op(act, lambda: A.activation(out=lgt[:], in_=lgt[:], func=AF.Exp, scale=-1.0), reads=[Rdec], writes=[Rdec])
            fw.op(act, lambda: A.activation(out=lgt[:], in_=lgt[:], func=AF.Ln, bias=1.0), reads=[Rdec], writes=[Rdec])
            fw.op(act, lambda: A.mul(out=lgt[:], in_=lgt[:], mul=-1.0), reads=[Rdec], writes=[Rdec])
            fw.op(act, lambda: A.activation(out=gC[:], in_=lgt[:], func=AF.Exp, scale=128.0), reads=[Rdec], writes=[Rdec])
            fw.op(dve, lambda: V.tensor_scalar(out=tiny[:, 1:2], in0=piota[:], scalar1=-1.0, scalar2=127.0, op0=ALU.mult, op1=ALU.add),
                  reads=[Rmisc], writes=[Rmisc])
            fw.op(dve, lambda: V.tensor_scalar(out=kdf[:], in0=lgt[:, 0:4], scalar1=tiny[:, 1:2], scalar2=None, op0=ALU.mult), reads=[Rdec, Rmisc], writes=[Rdec])
            fw.op(dve, lambda: V.tensor_scalar(out=kdb[:], in0=lgt[:, 4:8], scalar1=piota[:, 0:1], scalar2=None, op0=ALU.mult), reads=[Rdec, Rmisc], writes=[Rdec])
            fw.op(act, lambda: A.activation(out=kdf[:], in_=kdf[:], func=AF.Exp), reads=[Rdec], writes=[Rdec])
            fw.op(act, lambda: A.activation(out=kdb[:], in_=kdb[:], func=AF.Exp), reads=[Rdec], writes=[Rdec])
            dist = xn[:, 0, 0:128]
            ci = xn[:, 0, 128:256]
            t1 = xn[:, 0, 256:384]
            t2 = xn[:, 0, 384:512]
            mk1 = xn[:, 1, 0:128]
            fw.op(pool, lambda: G.iota(dist, pattern=[[1, 128]], base=0, channel_multiplier=-1, allow_small_or_imprecise_dtypes=True),
                  reads=[ov], writes=[Rxn[0]])
            fw.op(pool, lambda: G.iota(ci, pattern=[[1, 128]], base=0, channel_multiplier=0, allow_small_or_imprecise_dtypes=True),
                  reads=[ov], writes=[Rxn[0]])
            for h in range(4):
                fw.op(dve, lambda: V.tensor_single_scalar(out=t1, in_=dist, scalar=0.0, op=ALU.max), reads=[Rxn[0], ov], writes=[Rxn[0]])
                fw.op(act, lambda h=h: A.activation(out=t1, in_=t1, func=AF.Exp, scale=lgt[:, h:h + 1]), reads=[Rxn[0], Rdec, ov], writes=[Rxn[0]])
                fw.op(dve, lambda: V.tensor_single_scalar(out=mk1, in_=dist, scalar=0.0, op=ALU.is_ge), reads=[Rxn[0], ov], writes=[Rxn[1]])
                fw.op(dve, lambda: V.tensor_tensor(out=t1, in0=t1, in1=mk1, op=ALU.mult), reads=[Rxn[0], Rxn[1], ov], writes=[Rxn[0]])
                fw.op(dve, lambda: V.tensor_scalar(out=t2, in0=dist, scalar1=-1.0, scalar2=0.0, op0=ALU.mult, op1=ALU.max), reads=[Rxn[0], ov], writes=[Rxn[0]])
                fw.op(act, lambda h=h: A.activation(out=t2, in_=t2, func=AF.Exp, scale=lgt[:, 4 + h:5 + h]), reads=[Rxn[0], Rdec, ov], writes=[Rxn[0]])
                fw.op(dve, lambda: V.tensor_single_scalar(out=mk1, in_=dist, scalar=0.0, op=ALU.is_le), reads=[Rxn[0], ov], writes=[Rxn[1]])
                fw.op(dve, lambda: V.tensor_tensor(out=t2, in0=t2, in1=mk1, op=ALU.mult), reads=[Rxn[0], Rxn[1], ov], writes=[Rxn[0]])
                fw.op(dve, lambda h=h: V.scalar_tensor_tensor(out=Mcomb[:, h * 128:(h + 1) * 128], in0=t1, scalar=1.0, in1=t2, op0=ALU.mult, op1=ALU.add),
                      reads=[Rxn[0], ov], writes=[Rdec])
                fw.op(dve, lambda: V.tensor_single_scalar(out=t1, in_=ci, scalar=1.0, op=ALU.add), reads=[Rxn[0], ov], writes=[Rxn[0]])
                fw.op(act, lambda h=h: A.activation(out=Qf[:, h, 0:128], in_=t1, func=AF.Exp, scale=lgt[:, h:h + 1]), reads=[Rxn[0], Rdec, ov], writes=[Rdec])
                fw.op(dve, lambda: V.tensor_scalar(out=t2, in0=ci, scalar1=-1.0, scalar2=128.0, op0=ALU.mult, op1=ALU.add), reads=[Rxn[0], ov], writes=[Rxn[0]])
                fw.op(act, lambda h=h: A.activation(out=Qb[:, h, 0:128], in_=t2, func=AF.Exp, scale=lgt[:, 4 + h:5 + h]), reads=[Rxn[0], Rdec, ov], writes=[Rdec])
            fw.op(dve, lambda: V.tensor_single_scalar(out=Mcomb, in_=Mcomb, scalar=QSCALE, op=ALU.mult), reads=[Rdec, ov], writes=[Rdec])
            fw.op(dve, lambda: V.tensor_single_scalar(out=Qf[:, :, 0:128], in_=Qf[:, :, 0:128], scalar=QSCALE, op=ALU.mult), reads=[Rdec, ov], writes=[Rdec])
            fw.op(dve, lambda: V.tensor_single_scalar(out=Qb[:, :, 0:128], in_=Qb[:, :, 0:128], scalar=QSCALE, op=ALU.mult), reads=[Rdec, ov], writes=[Rdec])
            fw.op(dve, lambda: V.tensor_copy(out=Qf[:, :, 128:256], in_=Qf[:, :, 0:128]), reads=[Rdec, ov], writes=[Rdec])
            fw.op(dve, lambda: V.tensor_copy(out=Qb[:, :, 128:256], in_=Qb[:, :, 0:128]), reads=[Rdec, ov], writes=[Rdec])
            fw.dma(sp, cwraw[0:CK, 1, :], conv_w[l, :, :], reads=[ov], writes=[Rxn[1]], key="cwr")

            def trw():
                for cc in range(4):
                    i_ = T.transpose(out=banks[0][:, cc * 32:cc * 32 + CK], in_=cwraw[0:CK, 1, cc * 128:(cc + 1) * 128], identity=ident[0:CK, 0:CK])
                return i_
            fw.op(pe, trw, reads=[Rxn[1], Rident, ov], writes=[Rb[0]])
            fw.op(dve, lambda: V.tensor_copy(out=cw[:], in_=banks[0][:, 0:128].rearrange("p (c k) -> p c k", k=32)[:, :, 0:CK]), reads=[Rb[0]], writes=[Rcw])
            for vi, vec in enumerate((conv_b, conv_ln_g, conv_ln_b)):
                fw.dma(sp, cvec[:, vi, :], vec[l, :].rearrange("(c p) -> p c", p=128), writes=[Rcw], key=("cv", vi))
            fw.op(pool, lambda: G.memset(upb, 0.0), reads=[ov], writes=[Rup])

            def mkd():
                for cc in range(4):
                    for kk in range(CK):
                        i_ = V.tensor_scalar(out=Dblk[:, cc, kk, :], in0=dmask[:], scalar1=cw[:, cc, kk:kk + 1], scalar2=None, op0=ALU.mult)
                return i_
            fw.op(dve, mkd, reads=[Rcw, Rdm, ov], writes=[Rdb])

        def proj_fm(tl, m, outT, Rout, versions=None, bk0=2):
            isS = tl["kind"] == "S"
            b = tl["idx"] % 2
            for h in range(4):
                bk = bk0 + (h % 2)
                nparts = 2 if isS else 1

                def mm(h=h, bk=bk, nparts=nparts):
                    for part in range(nparts):
                        wv = WIN(m if part == 0 else m + 4)
                        for kc in range(8):
                            i_ = T.matmul(banks[bk][:, part * 256:(part + 1) * 256], lhsT=wv[:, kc, h * 128:(h + 1) * 128], rhs=hT[:, b, kc, :],
                                          start=(kc == 0), stop=(kc == 7))
                    return i_
                rd = [slot[2 * m], slot[2 * m + 1], RhT[b]] + ([slot[2 * m + 8], slot[2 * m + 9]] if isS else [])
                fw.op(pe, mm, reads=rd, writes=[Rb[bk]])
                r = h % 2
                if isS:
                    fw.op(dve, lambda bk=bk, r=r: V.tensor_tensor(out=rt1[:, r, :], in0=banks[bk][:, 0:256], in1=ropeCS[:, b, 0, :], op=ALU.mult),
                          reads=[Rb[bk], Rcs[b], ov], writes=[Rrt[r]])
                    fw.op(dve, lambda bk=bk, r=r: V.tensor_tensor(out=rt2[:, r, :], in0=banks[bk][:, 256:512], in1=ropeCS[:, b, 1, :], op=ALU.mult),
                          reads=[Rb[bk], Rcs[b], ov], writes=[Rrt[r]])
                    fw.op(pool, lambda r=r: G.tensor_tensor(out=rt1[:, r, :], in0=rt1[:, r, :], in1=rt2[:, r, :], op=ALU.add),
                          reads=[Rrt[r], ov], writes=[Rrt[r]])
                    srcv, rsrc = rt1[:, r, :], Rrt[r]
                else:
                    srcv, rsrc = banks[bk][:, 0:256], Rb[bk]
                fw.op(act, lambda h=h, srcv=srcv: A.copy(out=outT[:, h, :], in_=srcv), reads=[rsrc, ov], writes=[Rout])
                if versions:
                    for (Qt, dstT) in versions:
                        fw.op(dve, lambda h=h, srcv=srcv, Qt=Qt, dstT=dstT: V.tensor_tensor(out=dstT[:, h, :], in0=srcv, in1=Qt[:, h, :], op=ALU.mult),
                              reads=[rsrc, Rdec, ov], writes=[Rout])

        def proj_tm(s, m, bk, b):
            def mm():
                for kc in range(8):
                    i_ = T.matmul(banks[bk][:, :], lhsT=hT[:, b, kc, s * 128:(s + 1) * 128], rhs=WIN(m)[:, kc, :], start=(kc == 0), stop=(kc == 7))
                return i_
            fw.op(pe, mm, reads=[slot[2 * m], slot[2 * m + 1], RhT[b]], writes=[Rb[bk]])

        def k_tm(s, kd, bk, ki):
            pb = banks[bk][:, 0:256].bitcast(BF16)

            def tr():
                for h in range(4):
                    i_ = T.transpose(out=pb[:, h * 128:(h + 1) * 128], in_=kT[:, h, s * 128:(s + 1) * 128], identity=identb[:])
                return i_
            fw.op(pe, tr, reads=[RkT, Rmisc, ov], writes=[Rb[bk]])

            def ev():
                for h in range(4):
                    i_ = A.activation(out=kdT[:, ki, h * 128:(h + 1) * 128], in_=pb[:, h * 128:(h + 1) * 128], func=AF.Copy, scale=kd[:, h:h + 1])
                return i_
            fw.op(act, ev, reads=[Rb[bk], Rdec, ov], writes=[RkdT[ki]])

        def kv_mm(s, bk, ki):
            def mm():
                for h in range(4):
                    i_ = T.matmul(banks[bk][:, h * 128:(h + 1) * 128], lhsT=kdT[:, ki, h * 128:(h + 1) * 128], rhs=vbf[:, s, h * 128:(h + 1) * 128],
                                  start=True, stop=True)
                return i_
            fw.op(pe, mm, reads=[RkdT[ki], Rvbf, ov], writes=[Rb[bk]])

        def state_update(Rt, RRt, bk, goff):
            def up():
                for h in range(4):
                    i_ = V.scalar_tensor_tensor(out=Rt[:, h * 128:(h + 1) * 128], in0=Rt[:, h * 128:(h + 1) * 128], scalar=gC[:, goff + h:goff + h + 1],
                                                in1=banks[bk][:, h * 128:(h + 1) * 128], op0=ALU.mult, op1=ALU.add)
                return i_
            fw.op(dve, up, reads=[Rb[bk], Rdec, ov, RRt], writes=[RRt])

        def load_cs(tl):
            if tl["kind"] == "S":
                b = tl["idx"] % 2
                for c in range(2):
                    fw.dma(sp, ropeCS[:, b, c, :], rope[c, :, tl["tok0"]:tl["tok0"] + 256], reads=[Rrope, ov], writes=[Rcs[b]], key=("cs", b, c))

        def stage_KB(l, src, have_w=False):
            claim(extra=[slot[i] for i in range(20, 33)])
            load_mixer(l, have_w)
            layer_consts(l)
            stage_consts(l, 1, False)
            order = list(reversed(tiles))

            def kb_pk(tl):
                proj_fm(tl, 3, kT, RkT)

            def kb_pv(tl):
                b = tl["idx"] % 2
                for s in range(2):
                    proj_tm(s, 4, 4 + s, b)
                    fw.op(act, lambda s=s: A.copy(out=vbf[:, s, :], in_=banks[4 + s][:, :]), reads=[Rb[4 + s], ov], writes=[Rvbf])

            def kb_ld(tl):
                load_x(src, tl, tl["idx"] % 2)
                load_cs(tl)

            kb_ld(order[0])
            if NT > 1:
                kb_ld(order[1])
            prologue(order[0], order[0]["idx"] % 2)
            kb_pk(order[0])
            kb_pv(order[0])
            if NT > 1:
                prologue(order[1], order[1]["idx"] % 2)
            for oi, tl in enumerate(order):
                b = tl["idx"] % 2
                isS = tl["kind"] == "S"
                if oi + 2 < NT:
                    kb_ld(order[oi + 2])
                k_tm(1, kdb, 7, 1)
                k_tm(0, kdb, 6, 0)
                if oi + 1 < NT:
                    kb_pk(order[oi + 1])
                last_tile_of_seq = (not isS) or (tl["idx"] == ST // 256 - 1)
                if last_tile_of_seq:
                    if isS:
                        fw.dma(sp, Rbs.rearrange("p (h v) -> p h v", h=4), stb[l].rearrange("h d v -> d h v"), reads=[ov], writes=[RRbs], key="rbs")
                    else:
                        fw.op(pool, lambda: G.memset(Rbs, 0.0), reads=[ov], writes=[RRbs])
                for s in (1, 0):
                    ch = tl["tok0"] // 128 + s
                    kv_mm(s, 6 + s, s)
                    ri = ch % 2
                    fw.op(act, lambda ri=ri: A.copy(out=rbt[:, ri, 0, :], in_=Rbs), reads=[RRbs, ov], writes=[Rrbt[ri]])
                    fw.dma(sp, rbd[ch], rbt[:, ri, 0, :], reads=[Rrbt[ri]], writes=[Rrbd[ch]], key=("rbo", ri))
                    state_update(Rbs, RRbs, 6 + s, 4)
                if not isS:
                    fw.dma(sp, nsb[tl["seq"], l].rearrange("h d v -> d h v"), Rbs.rearrange("p (h v) -> p h v", h=4), reads=[RRbs, ov], key="nsb")
                if oi + 1 < NT:
                    kb_pv(order[oi + 1])
                if oi + 2 < NT:
                    prologue(order[oi + 2], order[oi + 2]["idx"] % 2)

        def stage_M(l, src, dst):
            stage_consts(l, 1, False)
            load_x(src, tiles[0], 0)
            load_cs(tiles[0])
            prologue(tiles[0], 0)
            zi = [0]

            def conv_in(tl, b):
                isS = tl["kind"] == "S"
                for cc in range(4):
                    bk = 6 + cc % 2

                    def mmc(cc=cc, bk=bk, b=b):
                        for part in range(2):
                            for kc in range(8):
                                i_ = T.matmul(banks[bk][:, part * 256:(part + 1) * 256], lhsT=WIN(part)[:, kc, cc * 128:(cc + 1) * 128], rhs=hT[:, b, kc, :],
                                              start=(kc == 0), stop=(kc == 7))
                        return i_
                    fw.op(pe, mmc, reads=[slot[0], slot[1], slot[2], slot[3], RhT[b]], writes=[Rb[bk]])
                    r = cc % 2
                    fw.op(act, lambda bk=bk, r=r: A.activation(out=sgm[:, r, :], in_=banks[bk][:, 256:512], func=AF.Sigmoid), reads=[Rb[bk], ov], writes=[Rsgm[r]])
                    if isS:
                        outv = upS[:, cc, :, 15:79]
                        in0 = banks[bk][:, 0:256].rearrange("p (r w) -> p r w", w=64)
                        in1 = sgm[:, r, :].rearrange("p (r w) -> p r w", w=64)
                    else:
                        outv = upP[:, cc, 15:271]
                        in0 = banks[bk][:, 0:256]
                        in1 = sgm[:, r, :]
                    fw.op(dve, lambda outv=outv, in0=in0, in1=in1: V.tensor_tensor(out=outv, in0=in0, in1=in1, op=ALU.mult),
                          reads=[Rb[bk], Rsgm[r], ov], writes=[Rup])

            CBK = [0, 1, 6, 7]

            def conv_mm(tl, ccs):
                isS = tl["kind"] == "S"

                def mmcv():
                    for cc in ccs:
                        for kk in range(CK):
                            for g in range(4):
                                ps = slice(32 * g, 32 * g + 32)
                                bkc = CBK[(g + cc) % 4]
                                if isS:
                                    ov_ = banks[bkc][ps, 0:256].rearrange("p (r w) -> p r w", w=64)
                                    rh = upS[ps, cc, :, kk:kk + 64]
                                else:
                                    ov_ = banks[bkc][ps, 0:256]
                                    rh = upP[ps, cc, kk:kk + 256]
                                i_ = T.matmul(ov_, lhsT=Dblk[ps, cc, kk, :], rhs=rh, start=(kk == 0), stop=(kk == CK - 1),
                                              tile_position=(32 * g, 32 * g))
                    return i_
                fw.op(pe, mmcv, reads=[Rup, Rdb, ov], writes=[Rb[q_] for q_ in CBK])

            def conv_evac():
                for cc in range(4):
                    def evc(cc=cc):
                        for g in range(4):
                            ps = slice(32 * g, 32 * g + 32)
                            i_ = V.tensor_scalar(out=acc[ps, cc, :], in0=banks[CBK[(g + cc) % 4]][ps, 0:256], scalar1=cvec[ps, 0, cc:cc + 1], scalar2=None, op0=ALU.add)
                        return i_
                    fw.op(dve, evc, reads=[Rb[q_] for q_ in CBK] + [Rcw, ov], writes=[Racc[cc]])

            def conv_ln_a():
                for s in range(2):
                    bk = 4 + s

                    def trc(s=s, bk=bk):
                        for cc in range(4):
                            i_ = T.transpose(out=banks[bk][:, cc * 128:(cc + 1) * 128], in_=acc[:, cc, s * 128:(s + 1) * 128], identity=ident[:])
                        return i_
                    fw.op(pe, trc, reads=Racc + [Rident, ov], writes=[Rb[bk]])
                for s in range(2):
                    bk = 4 + s
                    ln_rows(banks[bk], Rb[bk], 2 + s, 512)
                    fw.op(act, lambda s=s, bk=bk: A.activation(out=xn[:, s, :], in_=banks[bk][:, :], func=AF.Identity, scale=rstd[:, 2 + s, 0:1], bias=rstd[:, 2 + s, 1:2]),
                          reads=[Rb[bk], Rst[2 + s], ov], writes=[Rxn[s]])

            def conv_ln_b():
                for pr in range(2):
                    bk = 4 + pr

                    def trb(pr=pr, bk=bk):
                        for c2 in range(2):
                            for s in range(2):
                                cc = pr * 2 + c2
                                i_ = T.transpose(out=banks[bk][:, c2 * 256 + s * 128:c2 * 256 + (s + 1) * 128], in_=xn[:, s, cc * 128:(cc + 1) * 128], identity=ident[:])
                        return i_
                    fw.op(pe, trb, reads=[Rxn[0], Rxn[1], Rident, ov], writes=[Rb[bk]])

                    def evb(pr=pr, bk=bk):
                        for c2 in range(2):
                            cc = pr * 2 + c2
                            i_ = A.activation(out=mixT[:, cc, :], in_=banks[bk][:, c2 * 256:(c2 + 1) * 256], func=AF.Silu, scale=cvec[:, 1, cc:cc + 1], bias=cvec[:, 2, cc:cc + 1])
                        return i_
                    fw.op(act, evb, reads=[Rb[bk], Rcw, ov], writes=[Rmix])

            def ret_rfb(tl, s):
                fi = (tl["tok0"] // 128 + s) % 2
                fw.op(act, lambda fi=fi: A.copy(out=Rfb[:, fi, :], in_=Rf), reads=[RRf, ov], writes=[RRfb[fi]])

            def ret_scores(s, bk):
                def mms(s=s, bk=bk):
                    for h in range(4):
                        i_ = T.matmul(banks[bk][:, h * 128:(h + 1) * 128], lhsT=kT[:, h, s * 128:(s + 1) * 128], rhs=qT[:, h, s * 128:(s + 1) * 128], start=True, stop=True)
                    return i_
                fw.op(pe, mms, reads=[RkT, RqT, ov], writes=[Rb[bk]])
                fw.op(dve, lambda s=s, bk=bk: V.tensor_tensor(out=Pm[:, s, :], in0=banks[bk][:, :], in1=Mcomb, op=ALU.mult), reads=[Rb[bk], Rdec, ov], writes=[RPm[s]])

            def ret_o(tl, b, s, bk):
                fi = (tl["tok0"] // 128 + s) % 2

                def mmo(s=s, fi=fi, b=b, bk=bk):
                    for h in range(4):
                        hs = slice(h * 128, (h + 1) * 128)
                        T.matmul(banks[bk][:, hs], lhsT=Pm[:, s, hs], rhs=vbf[:, s, hs], start=True, stop=False)
                        T.matmul(banks[bk][:, hs], lhsT=qfT[:, h, s * 128:(s + 1) * 128], rhs=Rfb[:, fi, hs], start=False, stop=False)
                        i_ = T.matmul(banks[bk][:, hs], lhsT=qbT[:, h, s * 128:(s + 1) * 128], rhs=rbt[:, b, s, hs], start=False, stop=True)
                    return i_
                fw.op(pe, mmo, reads=[RPm[s], Rvbf, RqT, RRfb[fi], Rrbt[b], ov], writes=[Rb[bk]])

            def gn_pre(s, bko):
                gi = s

                def gbs(gi=gi):
                    for h in range(4):
                        i_ = V.bn_stats(out=gst[:, gi, h * 6:(h + 1) * 6], in_=banks[bko][:, h * 128:(h + 1) * 128])
                    return i_
                fw.op(dve, gbs, reads=[Rb[bko]], writes=[Rgst[gi]])

                def gag(gi=gi):
                    for h in range(4):
                        i_ = V.bn_aggr(out=gmv[:, gi, h * 2:(h + 1) * 2], in_=gst[:, gi, h * 6:(h + 1) * 6])
                    return i_
                fw.op(dve, gag, reads=[Rgst[gi]], writes=[Rgst[gi]])
                gm = gmv[:, gi, :].rearrange("p (h two) -> p h two", two=2)
                fw.op(act, lambda gi=gi, gm=gm: A.activation(out=grs[:, gi, 0:4], in_=gm[:, :, 1], func=AF.Sqrt, bias=epst[:, 0:1]), reads=[Rgst[gi], Rmisc], writes=[Rgst[gi]])
                fw.op(dve, lambda gi=gi: V.reciprocal(out=grs[:, gi, 0:4], in_=grs[:, gi, 0:4]), reads=[Rgst[gi]], writes=[Rgst[gi]])
                fw.op(dve, lambda gi=gi, gm=gm: V.scalar_tensor_tensor(out=grs[:, gi, 4:8], in0=gm[:, :, 0], scalar=-1.0, in1=grs[:, gi, 0:4], op0=ALU.mult, op1=ALU.mult),
                      reads=[Rgst[gi]], writes=[Rgst[gi]])
                onv, ogv = onb[s], ogb[s]

                def gev(gi=gi, onv=onv):
                    for h in range(4):
                        i_ = A.activation(out=onv[:, h * 128:(h + 1) * 128], in_=banks[bko][:, h * 128:(h + 1) * 128], func=AF.Identity,
                                          scale=grs[:, gi, h:h + 1], bias=grs[:, gi, 4 + h:5 + h])
                    return i_
                fw.op(act, gev, reads=[Rb[bko], Rgst[gi], ov], writes=[Ron[s]])
                fw.op(pool, lambda s=s, onv=onv, ogv=ogv: G.tensor_tensor(out=ogv, in0=onv, in1=sg[:, s, :], op=ALU.mult), reads=[Ron[s], Rsg, ov], writes=[Rog[s]])

            def gn_post(s, bkt):
                ogv = ogb[s]
                pb = banks[bkt][:, 0:256].bitcast(BF16)

                def tro(pb=pb, ogv=ogv):
                    for h in range(4):
                        i_ = T.transpose(out=pb[:, h * 128:(h + 1) * 128], in_=ogv[:, h * 128:(h + 1) * 128], identity=identb[:])
                    return i_
                fw.op(pe, tro, reads=[Rog[s], Rmisc, ov], writes=[Rb[bkt]])
                fw.op(act, lambda s=s, pb=pb: A.copy(out=mixT[:, 4:8, s * 128:(s + 1) * 128], in_=pb.rearrange("p (h t) -> p h t", t=128)),
                      reads=[Rb[bkt], ov], writes=[Rmix])

            def head1(tl):
                b = tl["idx"] % 2
                isS = tl["kind"] == "S"
                if (not isS) and tiles[tl["idx"] - 1]["kind"] == "S":
                    fw.op(pool, lambda: G.memset(upb, 0.0), reads=[ov, Rup], writes=[Rup])
                conv_in(tl, b)
                proj_fm(tl, 2, qT, RqT, versions=[(Qf, qfT), (Qb, qbT)], bk0=0)
                proj_fm(tl, 3, kT, RkT, bk0=0)

            def head2(tl):
                b = tl["idx"] % 2
                for s in range(2):
                    proj_tm(s, 4, 0, b)
                    fw.op(act, lambda s=s: A.copy(out=vbf[:, s, :], in_=banks[0][:, :]), reads=[Rb[0], ov], writes=[Rvbf])
                    proj_tm(s, 5, 1, b)
                    fw.op(act, lambda s=s: A.activation(out=sg[:, s, :], in_=banks[1][:, :], func=AF.Silu), reads=[Rb[1], ov], writes=[Rsg])
                conv_mm(tl, (0, 1, 2, 3))
                conv_evac()

            def rbt_load(tl):
                b = tl["idx"] % 2
                for s in range(2):
                    ch = tl["tok0"] // 128 + s
                    fw.dma(sp, rbt[:, b, s, :], rbd[ch], reads=[Rrbd[ch]], writes=[Rrbt[b]], key=("rbi", b, s))

            def tailA1(tl):
                isS = tl["kind"] == "S"
                if isS and tl["first"]:
                    fw.dma(sp, Rf.rearrange("p (h v) -> p h v", h=4), stf[l].rearrange("h d v -> d h v"), reads=[ov], writes=[RRf], key="rfs")
                elif not isS:
                    fw.op(pool, lambda: G.memset(Rf, 0.0), reads=[ov], writes=[RRf])
                ret_rfb(tl, 0)
                ret_scores(0, 2)
                ret_scores(1, 3)
                k_tm(0, kdf, 4, 0)
                k_tm(1, kdf, 5, 1)

            rbt_load(tiles[0])
            if NT > 1:
                load_x(src, tiles[1], 1)
                load_cs(tiles[1])
            head1(tiles[0])
            head2(tiles[0])
            tailA1(tiles[0])
            for ti, tl in enumerate(tiles):
                b = ti % 2
                isS = tl["kind"] == "S"
                if ti + 1 < NT:
                    rbt_load(tiles[ti + 1])
                kv_mm(0, 4, 0)
                kv_mm(1, 5, 1)
                ret_o(tl, b, 0, 2)
                state_update(Rf, RRf, 4, 0)
                ret_rfb(tl, 1)
                ret_o(tl, b, 1, 3)
                state_update(Rf, RRf, 5, 0)
                if not isS:
                    fw.dma(sp, nsf[tl["seq"], l].rearrange("h d v -> d h v"), Rf.rearrange("p (h v) -> p h v", h=4), reads=[RRf, ov], key="nsf")
                conv_ln_a()
                gn_pre(0, 2)
                gn_pre(1, 3)
                if ti + 1 < NT:
                    prologue(tiles[ti + 1], 1 - b)
                    head1(tiles[ti + 1])
                conv_ln_b()
                gn_post(0, 6)
                gn_post(1, 7)
                if ti + 1 < NT:
                    head2(tiles[ti + 1])
                    tailA1(tiles[ti + 1])
                for s in range(2):
                    yb = (6, 7) if s == 0 else (0, 1)
                    for hf in range(2):
                        bk = yb[hf]

                        def mmw(s=s, hf=hf, bk=bk):
                            for kc in range(8):
                                i_ = T.matmul(banks[bk][:, :], lhsT=mixT[:, kc, s * 128:(s + 1) * 128], rhs=WIN(8 + hf)[:, kc, :], start=(kc == 0), stop=(kc == 7))
                            return i_
                        fw.op(pe, mmw, reads=[Rmix, ov, slot[16 + 2 * hf], slot[17 + 2 * hf]], writes=[Rb[bk]])
                for s in range(2):
                    yb = (6, 7) if s == 0 else (0, 1)
                    epilogue(tl, b, s, yb, dst, zi[0])
                    zi[0] = 1 - zi[0]
                if ti + 2 < NT:
                    load_x(src, tiles[ti + 2], b)
                    load_cs(tiles[ti + 2])

        seq = []
        for l in range(NL):
            seq += [(l, "F0"), (l, "KB"), (l, "M"), (l, "F1")]
        if stages is not None:
            seq = [s_ for s_ in seq if s_ in stages]
        n_xstage = sum(1 for s_ in seq if s_[1] != "KB")
        xi = 0
        cur = xin
        have = False
        for si, (l, nm) in enumerate(seq):
            nx = seq[si + 1] if si + 1 < len(seq) else None
            if nm == "KB":
                stage_KB(l, cur, have_w=have)
                have = False
                continue
            xi += 1
            dst = yout if xi == n_xstage else xs
            if nm in ("F0", "F1"):
                nxt = None
                if PREFETCH and nx is not None:
                    if nx[1] == "KB":
                        nxt = ("M", nx[0])
                    elif nx[1] in ("F0", "F1"):
                        nxt = ("F", nx[0], 0 if nx[1] == "F0" else 1)
                stage_F(l, 0 if nm == "F0" else 1, cur, dst, have_w=have, nxt=nxt)
                have = nxt is not None
            else:
                stage_M(l, cur, dst)
                have = False
            cur = dst
        fw.finish(sp)
        fw.run()
    return nc


_W_KEYS = ["w_ada", "b_ada", "ln_g", "ln_b", "ffn_w1", "ffn_w2", "w_in", "w_out", "conv_w", "conv_b", "conv_ln_g", "conv_ln_b"]


def kernel(x_prompt, x_sample, state_ret_fwd, state_ret_bwd, c, c_ctx, w_ada, b_ada, ln_g, ln_b,
           ffn_w1, ffn_w2, w_in, w_out, conv_w, conv_b, conv_ln_g, conv_ln_b, ret_decay_logit):
    f = lambda a: np.ascontiguousarray(np.asarray(a, dtype=np.float32))
    NCORE = 8
    NL = w_in.shape[0]
    B, S = x_prompt.shape[0], x_prompt.shape[1]
    DB, ST = x_sample.shape[0], x_sample.shape[1]
    NP = B // NCORE
    assert DB == NCORE and S == 256
    nc = build(NL=NL, ST=ST, NP=NP)
    wts = dict(w_ada=f(w_ada), b_ada=f(b_ada), ln_g=f(ln_g), ln_b=f(ln_b), ffn_w1=f(ffn_w1), ffn_w2=f(ffn_w2), w_in=f(w_in),
               w_out=f(w_out), conv_w=f(conv_w), conv_b=f(conv_b), conv_ln_g=f(conv_ln_g), conv_ln_b=f(conv_ln_b),
               dlogit=f(ret_decay_logit).reshape(NL, 8))
    in_maps = []
    for i in range(NCORE):
        m = dict(wts)
        m["xin"] = np.concatenate([f(x_sample[i]), f(x_prompt[i * NP:(i + 1) * NP]).reshape(NP * S, D)], axis=0)
        m["stf"] = f(state_ret_fwd[i])
        m["stb"] = f(state_ret_bwd[i])
        m["cond"] = np.stack([f(c[i]), f(c_ctx)], axis=0)
        in_maps.append(m)
    res = run_bass_kernel_spmd(nc, in_maps, core_ids=list(range(NCORE)))
    outs = res.results
    y_sample = np.stack([outs[i]["yout"][:ST] for i in range(NCORE)], axis=0)
    y_prompt = np.concatenate([outs[i]["yout"][ST:].reshape(NP, S, D) for i in range(NCORE)], axis=0)
    nf = np.concatenate([outs[i]["nsf"] for i in range(NCORE)], axis=0)
    nb = np.concatenate([outs[i]["nsb"] for i in range(NCORE)], axis=0)
    return (y_prompt.astype(np.float32), y_sample.astype(np.float32), nf.astype(np.float32), nb.astype(np.float32))
```

```python
import contextlib
import math
import numpy as np
import concourse.bass as bass
import concourse.mybir as mybir
from concourse.bass_utils import run_bass_kernel_spmd

F32 = mybir.dt.float32
BF16 = mybir.dt.bfloat16
AF = mybir.ActivationFunctionType
ALU = mybir.AluOpType

D = 1024
DFF = 2816
NHC = 22
DEPTH = 4
ALPHA = (2.0 * DEPTH) ** 0.25
EPS = 1e-5
CK = 31
QSCALE = 128.0 ** -0.5
TWO_PI = 2.0 * math.pi
PREFETCH = True


class Res:
    __slots__ = ("name", "w", "rs")

    def __init__(self, name):
        self.name = name
        self.w = None
        self.rs = {}


class Eng:
    def __init__(self, name, h, sem, is_pe=False):
        self.name = name
        self.h = h
        self.sem = sem
        self.n = 0
        self.seen = {}
        self.is_pe = is_pe
        self.prog = []


class FW:
    def __init__(self, nc, stack):
        self.nc = nc
        self.stack = stack
        mk = lambda nm, h, pe=False: Eng(nm, h, stack.enter_context(nc.semaphore("s_" + nm)), pe)
        self.pe = mk("pe", nc.tensor, True)
        self.act = mk("act", nc.scalar)
        self.dve = mk("dve", nc.vector)
        self.pool = mk("pool", nc.gpsimd)
        self.sp = mk("sp", nc.sync)
        self.dma_sems = {}
        self.nres = 0

    def res(self, name=None):
        self.nres += 1
        return Res(name or f"r{self.nres}")

    def _wait(self, eng, dep):
        key, (sem, idx, is_eng) = dep
        if is_eng and key == eng.name and eng.is_pe:
            return
        if eng.seen.get(key, 0) >= idx:
            return
        eng.prog.append(lambda h=eng.h, s=sem, v=idx: h.wait_ge(s, v))
        eng.seen[key] = idx

    def _deps(self, reads, writes):
        deps = []
        for r in reads:
            if r.w is not None:
                deps.append(r.w)
        for w in writes:
            if w.w is not None:
                deps.append(w.w)
            deps.extend(w.rs.items())
        return deps

    def _mark(self, me, reads, writes):
        key, val = me
        for r in reads:
            r.rs[key] = val
        for w in writes:
            w.w = me
            w.rs = {}

    def op(self, eng, fn, reads=(), writes=()):
        for d in self._deps(reads, writes):
            self._wait(eng, d)
        eng.n += 1
        eng.prog.append(lambda fn=fn, s=eng.sem: fn().then_inc(s, 1))
        me = (eng.name, (eng.sem, eng.n, True))
        self._mark(me, reads, writes)
        return me

    def dma(self, qeng, out, in_, reads=(), writes=(), key=None):
        for d in self._deps(reads, writes):
            self._wait(qeng, d)
        if key not in self.dma_sems:
            self.dma_sems[key] = [self.stack.enter_context(self.nc.semaphore("d_%d" % len(self.dma_sems))), 0]
        ent = self.dma_sems[key]
        ent[1] += 16
        qeng.prog.append(lambda h=qeng.h, o=out, i=in_, s=ent[0]: h.dma_start(out=o, in_=i).then_inc(s, 16))
        me = (("dma", key), (ent[0], ent[1], False))
        self._mark(me, reads, writes)
        return me

    def finish(self, eng):
        for key, (sem, val) in self.dma_sems.items():
            if val:
                eng.prog.append(lambda h=eng.h, s=sem, v=val: h.wait_ge(s, v))
        for e2 in (self.pe, self.act, self.dve, self.pool, self.sp):
            if e2 is not eng and e2.n:
                eng.prog.append(lambda h=eng.h, s=e2.sem, v=e2.n: h.wait_ge(s, v))

    def run(self):
        with self.nc.Block() as block:
            @block.tensor
            def _(e):
                for t in self.pe.prog:
                    t()

            @block.scalar
            def _(e):
                for t in self.act.prog:
                    t()

            @block.vector
            def _(e):
                for t in self.dve.prog:
                    t()

            @block.gpsimd
            def _(e):
                for t in self.pool.prog:
                    t()

            @block.sync
            def _(e):
                for t in self.sp.prog:
                    t()


def build(NL=4, ST=4096, NP=4, stages=None, dbg=False):
    NTOK = ST + NP * 256
    NCH = NTOK // 128
    nc = bass.Bass("TRN2", target_bir_lowering=False)
    din = lambda n, s, dt=F32: nc.dram_tensor(n, s, dt, kind="ExternalInput").ap()
    dout = lambda n, s: nc.dram_tensor(n, s, F32, kind="ExternalOutput").ap()
    dint = lambda n, s, dt=F32: nc.dram_tensor(n, s, dt, kind="Internal").ap()
    xin = din("xin", [NTOK, D])
    stf = din("stf", [NL, 4, 128, 128])
    stb = din("stb", [NL, 4, 128, 128])
    cond = din("cond", [2, D])
    w_ada = din("w_ada", [NL, D, 9 * D])
    b_ada = din("b_ada", [NL, 9 * D])
    ln_g = din("ln_g", [NL, 3, D])
    ln_b = din("ln_b", [NL, 3, D])
    ffn_w1 = din("ffn_w1", [NL, 2, D, 2 * DFF])
    ffn_w2 = din("ffn_w2", [NL, 2, DFF, D])
    w_in = din("w_in", [NL, D, 3072])
    w_out = din("w_out", [NL, D, D])
    conv_w = din("conv_w", [NL, CK, 512])
    conv_b = din("conv_b", [NL, 512])
    conv_ln_g = din("conv_ln_g", [NL, 512])
    conv_ln_b = din("conv_ln_b", [NL, 512])
    dlogit = din("dlogit", [NL, 8])
    yout = dout("yout", [NTOK, D])
    nsf = dout("nsf", [max(NP, 1), NL, 4, 128, 128])
    nsb = dout("nsb", [max(NP, 1), NL, 4, 128, 128])
    xs = dint("xs", [NTOK, D])
    mods = dint("mods", [NL, 2, 9 * D])
    rope = (dout if dbg else dint)("rope", [2, 128, max(ST, 256)])
    rbd = dint("rbd", [NCH, 128, 512], BF16)

    tiles = []
    for t in range(ST // 256):
        tiles.append(dict(kind="S", tok0=t * 256, cond=0, first=(t == 0), idx=len(tiles)))
    for n in range(NP):
        tiles.append(dict(kind="P", tok0=ST + n * 256, cond=1, seq=n, idx=len(tiles)))
    NT = len(tiles)

    with contextlib.ExitStack() as st, nc.allow_non_contiguous_dma(reason="small vector gathers"):
        fw = FW(nc, st)
        pe, act, dve, pool, sp = fw.pe, fw.act, fw.dve, fw.pool, fw.sp
        sbt = lambda n, s, dt=F32: st.enter_context(nc.sbuf_tensor(n, s, dt))
        R = fw.res

        arena = sbt("arena", [128, 33 * 1024])
        aux = sbt("aux", [128, 4864])
        xbuf = sbt("xbuf", [128, 2, 2, D])
        hT = sbt("hT", [128, 2, 8, 256], BF16)
        zb = sbt("zb", [128, 2, D])
        gate = sbt("gate", [128, 2, D])
        lng = sbt("lng", [128, D])
        lnb = sbt("lnb", [128, D])
        rbt = sbt("rbt", [128, 2, 2, 512], BF16)
        ident = sbt("ident", [128, 128])
        identb = sbt("identb", [128, 128], BF16)
        sc1 = sbt("sc1", [128, 2, 8])
        shf = sbt("shf", [128, 2, 8])
        stats = sbt("stats", [128, 4, 12])
        mv = sbt("mv", [128, 4, 2])
        rstd = sbt("rstd", [128, 4, 2])
        gst = sbt("gst", [128, 2, 24])
        gmv = sbt("gmv", [128, 2, 8])
        grs = sbt("grs", [128, 2, 8])
        lgt = sbt("lgt", [128, 8])
        kdf = sbt("kdf", [128, 4])
        kdb = sbt("kdb", [128, 4])
        gC = sbt("gC", [128, 8])
        piota = sbt("piota", [128, 1])
        cw = sbt("cw", [128, 4, CK])
        cvec = sbt("cvec", [128, 3, 4])
        scT = sbt("scT", [128, 8, 2], BF16)
        condT = sbt("condT", [128, 8, 2])
        tiny = sbt("tiny", [128, 8])
        negpi = sbt("negpi", [128, 1])
        epst = sbt("epst", [128, 1])
        dmask = sbt("dmask", [128, 32])

        slot = [R(f"slot{i}") for i in range(33)]
        ov = R("ov")
        Rx = [[R(f"x{b}{s}") for s in range(2)] for b in range(2)]
        RhT = [R("hT0"), R("hT1")]
        Rzb = [R("zb0"), R("zb1")]
        Rgate, Rlng, Rlnb, Rsc = R("gate"), R("lng"), R("lnb"), R("sc")
        Rident = R("ident")
        Rst = [R(f"st{i}") for i in range(4)]
        Rgst = [R("gst0"), R("gst1")]
        Rdec = R("dec")
        Rcw = R("cw")
        Rrbt = [R("rbt0"), R("rbt1")]
        Rmods = R("mods")
        Rrope = R("rope")
        Rxs = [R(f"xs{t}") for t in range(NT)]
        Rrbd = [R(f"rbd{c}") for c in range(NCH)]
        Rmisc = R("misc")

        banks = [st.enter_context(nc.psum_tensor(f"bank{i}", [128, 512], F32)) for i in range(8)]
        Rb = [R(f"bank{i}") for i in range(8)]

        def aview(off, n, dt, pat=None, **kw):
            v = arena[:, off:off + n]
            if dt is not F32:
                v = v.bitcast(dt)
            if pat:
                v = v.rearrange(pat, **kw)
            return v

        def xview(off, n, dt, pat=None, **kw):
            v = aux[:, off:off + n]
            if dt is not F32:
                v = v.bitcast(dt)
            if pat:
                v = v.rearrange(pat, **kw)
            return v

        uT = xview(0, 2816, BF16, "p (j t) -> p j t", t=256)
        silt = xview(2816, 1024, F32, "p (a t) -> p a t", t=256)
        Qf = xview(0, 1024, F32, "p (h t) -> p h t", t=256)
        Qb = xview(1024, 1024, F32, "p (h t) -> p h t", t=256)
        upb = xview(2048, 752, BF16)
        upS = xview(2048, 752, BF16, "p (c r w) -> p c r w", r=4, w=94)
        upP = xview(2048, 572, BF16, "p (c w) -> p c w", w=286)
        Dblk = xview(2800, 1984, BF16, "p (c k j) -> p c k j", c=4, k=CK)
        o = [20 * 1024]

        def oa(n, dt, pat=None, **kw):
            v = aview(o[0], n, dt, pat, **kw)
            o[0] += n
            assert o[0] <= 33 * 1024
            return v

        Mcomb = oa(512, F32)
        qT = oa(512, BF16, "p (h t) -> p h t", t=256)
        qfT = oa(512, BF16, "p (h t) -> p h t", t=256)
        qbT = oa(512, BF16, "p (h t) -> p h t", t=256)
        kT = oa(512, BF16, "p (h t) -> p h t", t=256)
        ropeCS = oa(1024, F32, "p (b c t) -> p b c t", b=2, c=2)
        rt1 = oa(512, F32, "p (b t) -> p b t", b=2)
        rt2 = oa(512, F32, "p (b t) -> p b t", b=2)
        vbf = oa(512, BF16, "p (s c) -> p s c", s=2)
        sg = oa(1024, F32, "p (s c) -> p s c", s=2)
        sgm = oa(512, F32, "p (b t) -> p b t", b=2)
        acc = oa(1024, F32, "p (c t) -> p c t", c=4)
        xn = oa(1024, F32, "p (s c) -> p s c", s=2)
        mixT = oa(1024, BF16, "p (k t) -> p k t", t=256)
        Pm = oa(512, BF16, "p (b c) -> p b c", b=2)
        kdT = oa(512, BF16, "p (b c) -> p b c", b=2)
        Rf = oa(512, F32)
        Rbs = oa(512, F32)
        Rfb = oa(512, BF16, "p (b c) -> p b c", b=2)
        on = oa(512, F32)
        og = oa(256, BF16)
        og2 = oa(256, BF16)
        on2 = sbt("on2", [128, 512])
        onb = [on, on2[:]]
        ogb = [og, og2]
        cwraw = xn

        RqT, RkT, Rcs, Rrt, Rvbf, Rsg, Rup, Rsgm, Racc, Rxn, Rmix = (R("qT"), R("kT"), [R("cs0"), R("cs1")],
            [R("rt0"), R("rt1")], R("vbf"), R("sg"), R("up"), [R("sgm0"), R("sgm1")], [R(f"acc{i}") for i in range(4)],
            [R("xn0"), R("xn1")], R("mix"))
        RPm, RkdT, RRf, RRbs, RRfb, Ron, Rog = [R("P0"), R("P1")], [R("kd0"), R("kd1")], R("Rf"), R("Rbs"), [R("Rfb0"), R("Rfb1")], [R("on0"), R("on1")], [R("og0"), R("og1")]
        RuT, Rsil = R("uT"), [R("sil0"), R("sil1"), R("sil2"), R("sil3")]
        Rdb = R("dblk")

        V, A, G = nc.vector, nc.scalar, nc.gpsimd
        T = nc.tensor

        fw.op(pool, lambda: G.memset(ident[:], 0.0), writes=[Rident])
        fw.op(pool, lambda: G.affine_select(out=ident[:], in_=ident[:], compare_op=ALU.not_equal, fill=1.0, base=0,
                                            pattern=[[-1, 128]], channel_multiplier=1), reads=[Rident], writes=[Rident])
        fw.op(dve, lambda: V.tensor_copy(out=identb[:], in_=ident[:]), reads=[Rident], writes=[Rmisc])
        fw.op(pool, lambda: G.iota(piota[:], pattern=[[0, 1]], base=0, channel_multiplier=1,
                                   allow_small_or_imprecise_dtypes=True), writes=[Rmisc])
        fw.op(pool, lambda: G.memset(negpi[:], -math.pi), writes=[Rmisc])
        fw.op(pool, lambda: G.memset(epst[:], EPS), writes=[Rmisc])
        Rdm = R("dmask")

        fw.op(dve, lambda: V.tensor_tensor(out=dmask[:], in0=ident[:, 0:32], in1=ident[:, 32:64], op=ALU.add), reads=[Rident], writes=[Rdm])
        fw.op(dve, lambda: V.tensor_tensor(out=dmask[:], in0=dmask[:], in1=ident[:, 64:96], op=ALU.add), reads=[Rident, Rdm], writes=[Rdm])
        fw.op(dve, lambda: V.tensor_tensor(out=dmask[:], in0=dmask[:], in1=ident[:, 96:128], op=ALU.add), reads=[Rident, Rdm], writes=[Rdm])

        for c_ in range(2):
            fw.dma(sp, condT[:, :, c_], cond[c_, :].rearrange("(k p) -> p k", p=128), writes=[Rmisc], key=("c0", c_))
        fw.op(act, lambda: A.activation(out=scT[:], in_=condT[:], func=AF.Silu), reads=[Rmisc], writes=[Rsc])
        wab = arena[:, 0:6144].bitcast(BF16).rearrange("p (b k c) -> p b k c", b=3, k=8)
        badat = arena[0:2, 6144:6656]
        modrow = arena[0:2, 7168:9216].rearrange("p (m c) -> p m c", m=2)[:, :, 0:512]
        Rwab = [slot[0], slot[2], slot[4]]
        Rwab2 = [slot[1], slot[3], slot[5]]
        Rmr = [slot[7], slot[8]]
        Rbad = slot[6]
        k = 0
        for l in range(NL):
            for j in range(18):
                b3 = k % 3
                fw.dma(pool, wab[:, b3], w_ada[l, :, j * 512:(j + 1) * 512].rearrange("(k p) c -> p k c", p=128),
                       writes=[Rwab[b3], Rwab2[b3]], key=("wab", b3))
                fw.dma(sp, badat[:], b_ada[l:l + 1, j * 512:(j + 1) * 512].partition_broadcast(2), writes=[Rbad], key="bad")
                bk = Rb[k % 2]

                def mm(b3=b3, bank=banks[k % 2]):
                    for kc in range(8):
                        i = T.matmul(bank[0:2, :], lhsT=scT[:, kc, :], rhs=wab[:, b3, kc, :], start=(kc == 0), stop=(kc == 7))
                    return i
                fw.op(pe, mm, reads=[Rwab[b3], Rwab2[b3], Rsc], writes=[bk])
                m2 = k % 2
                fw.op(dve, lambda m2=m2, bank=banks[k % 2]: V.tensor_tensor(out=modrow[:, m2, :], in0=bank[0:2, :], in1=badat[:], op=ALU.add),
                      reads=[bk, Rbad], writes=[Rmr[m2]])
                fw.dma(sp, mods[l, :, j * 512:(j + 1) * 512], modrow[:, m2, :], reads=[Rmr[m2]], writes=[Rmods], key=("mr", m2))
                k += 1

        if ST > 0:
            invf = sbt("invf", [128, 1])
            MAGIC = 12582912.0

            def mkf():
                G.iota(tiny[0:64, 0:1], pattern=[[0, 1]], base=0, channel_multiplier=1, allow_small_or_imprecise_dtypes=True)
                return G.iota(tiny[64:128, 0:1], pattern=[[0, 1]], base=0, channel_multiplier=1, allow_small_or_imprecise_dtypes=True)
            fw.op(pool, mkf, reads=[ov, Rmisc], writes=[Rmisc])
            fw.op(dve, lambda: V.tensor_scalar(out=tiny[:, 2:3], in0=tiny[:, 0:1], scalar1=32.0, scalar2=32.0, op0=ALU.is_ge, op1=ALU.mult), reads=[ov, Rmisc], writes=[Rmisc])
            fw.op(dve, lambda: V.tensor_tensor(out=tiny[:, 0:1], in0=tiny[:, 0:1], in1=tiny[:, 2:3], op=ALU.subtract), reads=[ov, Rmisc], writes=[Rmisc])
            fw.op(act, lambda: A.activation(out=invf[:], in_=tiny[:, 0:1], func=AF.Exp, scale=-math.log(10000.0) / 32.0), reads=[ov, Rmisc], writes=[Rmisc])
            fw.op(dve, lambda: V.tensor_single_scalar(out=invf[:], in_=invf[:], scalar=1.0 / TWO_PI, op=ALU.mult), reads=[ov, Rmisc], writes=[Rmisc])
            posv = sg
            pa = posv[:, 0, 0:256]
            pr_ = posv[:, 0, 256:512]
            ps_ = posv[:, 1, 0:256]
            pc_ = posv[:, 1, 256:512]
            SC = 6.283184
            for t in range(ST // 256):
                def mkpos(t=t):
                    G.iota(pa[0:64, :], pattern=[[1, 4], [0, 64]], base=4 * t, channel_multiplier=0, allow_small_or_imprecise_dtypes=True)
                    return G.iota(pa[64:128, :], pattern=[[0, 4], [1, 64]], base=0, channel_multiplier=0, allow_small_or_imprecise_dtypes=True)
                fw.op(pool, mkpos, reads=[ov], writes=[Rsg])
                fw.op(dve, lambda: V.tensor_scalar(out=pa, in0=pa, scalar1=invf[:, 0:1], scalar2=None, op0=ALU.mult), reads=[ov, Rsg, Rmisc], writes=[Rsg])
                for (dstv, off, rr) in ((ps_, 0.0, Rxn[0]), (pc_, 0.25, Rxn[1])):
                    fw.op(dve, lambda dstv=dstv, off=off: V.tensor_single_scalar(out=dstv, in_=pa, scalar=off, op=ALU.add), reads=[ov, Rsg], writes=[rr])
                    fw.op(dve, lambda dstv=dstv: V.tensor_single_scalar(out=pr_, in_=dstv, scalar=MAGIC, op=ALU.add), reads=[ov, rr], writes=[Rsgm[0]])
                    fw.op(dve, lambda: V.tensor_single_scalar(out=pr_, in_=pr_, scalar=MAGIC, op=ALU.subtract), reads=[ov, Rsgm[0]], writes=[Rsgm[0]])
                    fw.op(dve, lambda dstv=dstv: V.tensor_tensor(out=dstv, in0=dstv, in1=pr_, op=ALU.subtract), reads=[ov, rr, Rsgm[0]], writes=[rr])
                    fw.op(act, lambda dstv=dstv: A.activation(out=dstv, in_=dstv, func=AF.Sin, scale=SC), reads=[ov, rr], writes=[rr])
                fw.dma(sp, rope[1, :, t * 256:(t + 1) * 256], ps_, reads=[ov, Rxn[0]], writes=[Rrope], key="rp0")
                fw.dma(sp, rope[0, :, t * 256:(t + 1) * 256], pc_, reads=[ov, Rxn[1]], writes=[Rrope], key="rp1")

        def stage_consts(l, sl, half):
            for c in range(2):
                fw.dma(sp, gate[:, c, :], mods[l, c:c + 1, (3 * sl + 2) * D:(3 * sl + 3) * D].partition_broadcast(128),
                       reads=[Rmods], writes=[Rgate], key=("gate", c))
                fw.dma(sp, sc1[:, c, :], mods[l, c, (3 * sl + 1) * D:(3 * sl + 2) * D].rearrange("(k p) -> p k", p=128),
                       reads=[Rmods], writes=[Rsc], key=("sc1", c))
                fw.dma(sp, shf[:, c, :], mods[l, c, (3 * sl) * D:(3 * sl + 1) * D].rearrange("(k p) -> p k", p=128),
                       reads=[Rmods], writes=[Rsc], key=("shf", c))
            fw.dma(sp, lng[:], ln_g[l, sl:sl + 1, :].partition_broadcast(128), writes=[Rlng], key="lng")
            fw.dma(sp, lnb[:], ln_b[l, sl:sl + 1, :].partition_broadcast(128), writes=[Rlnb], key="lnb")
            fw.op(dve, lambda: V.tensor_single_scalar(out=sc1[:], in_=sc1[:], scalar=1.0, op=ALU.add), reads=[Rsc], writes=[Rsc])
            if half:
                fw.op(pool, lambda: G.tensor_single_scalar(out=gate[:], in_=gate[:], scalar=0.5, op=ALU.mult), reads=[Rgate], writes=[Rgate])

        def load_x(src, tl, b):
            for s in range(2):
                r0 = tl["tok0"] + s * 128
                fw.dma(sp, xbuf[:, b, s, :], src[r0:r0 + 128, :], reads=[Rxs[tl["idx"]]], writes=[Rx[b][s]], key=("x", b, s))

        def prologue(tl, b):
            c = tl["cond"]
            for pr in range(4):
                bk = pr % 2

                def tr(pr=pr, bk=bk):
                    for cc in range(2):
                        for s in range(2):
                            fc = pr * 2 + cc
                            i = T.transpose(out=banks[bk][:, cc * 256 + s * 128: cc * 256 + (s + 1) * 128],
                                            in_=xbuf[:, b, s, fc * 128:(fc + 1) * 128], identity=ident[:])
                    return i
                fw.op(pe, tr, reads=[Rx[b][0], Rx[b][1], Rident], writes=[Rb[bk]])

                def ev(pr=pr, bk=bk):
                    for cc in range(2):
                        fc = pr * 2 + cc
                        i = A.activation(out=hT[:, b, fc, :], in_=banks[bk][:, cc * 256:(cc + 1) * 256], func=AF.Identity,
                                         scale=sc1[:, c, fc:fc + 1], bias=shf[:, c, fc:fc + 1])
                    return i
                fw.op(act, ev, reads=[Rb[bk], Rsc], writes=[RhT[b]])

        def epilogue(tl, b, s, ybanks, dst, zi):
            c = tl["cond"]
            z = zb[:, zi, :]
            rz = Rzb[zi]
            for hf in range(2):
                fw.op(dve, lambda hf=hf: V.tensor_tensor(out=z[:, hf * 512:(hf + 1) * 512], in0=banks[ybanks[hf]][:, :],
                                                         in1=gate[:, c, hf * 512:(hf + 1) * 512], op=ALU.mult),
                      reads=[Rb[ybanks[hf]], Rgate], writes=[rz])
            fw.op(dve, lambda: V.scalar_tensor_tensor(out=z, in0=xbuf[:, b, s, :], scalar=ALPHA, in1=z, op0=ALU.mult, op1=ALU.add),
                  reads=[Rx[b][s], rz], writes=[rz])
            ln_rows(z, rz, zi, D)
            fw.op(act, lambda: A.activation(out=z, in_=z, func=AF.Identity, scale=rstd[:, zi, 0:1], bias=rstd[:, zi, 1:2]),
                  reads=[rz, Rst[zi]], writes=[rz])
            fw.op(pool, lambda: G.tensor_tensor(out=z, in0=z, in1=lng[:], op=ALU.mult), reads=[rz, Rlng], writes=[rz])
            fw.op(pool, lambda: G.tensor_tensor(out=z, in0=z, in1=lnb[:], op=ALU.add), reads=[rz, Rlnb], writes=[rz])
            r0 = tl["tok0"] + s * 128
            fw.dma(sp, dst[r0:r0 + 128, :], z, reads=[rz], writes=[Rxs[tl["idx"]]], key=("xo", zi))

        def ln_rows(src, rsrc, si, n):
            nchunk = n // 512

            def bs():
                for q in range(nchunk):
                    i = V.bn_stats(out=stats[:, si, q * 6:(q + 1) * 6], in_=src[:, q * 512:(q + 1) * 512])
                return i
            fw.op(dve, bs, reads=[rsrc], writes=[Rst[si]])
            fw.op(dve, lambda: V.bn_aggr(out=mv[:, si, :], in_=stats[:, si, 0:6 * nchunk]), reads=[Rst[si]], writes=[Rst[si]])
            fw.op(act, lambda: A.activation(out=rstd[:, si, 0:1], in_=mv[:, si, 1:2], func=AF.Sqrt, bias=epst[:, 0:1]), reads=[Rst[si], Rmisc], writes=[Rst[si]])
            fw.op(dve, lambda: V.reciprocal(out=rstd[:, si, 0:1], in_=rstd[:, si, 0:1]), reads=[Rst[si]], writes=[Rst[si]])
            fw.op(dve, lambda: V.scalar_tensor_tensor(out=rstd[:, si, 1:2], in0=mv[:, si, 0:1], scalar=-1.0, in1=rstd[:, si, 0:1], op0=ALU.mult, op1=ALU.mult),
                  reads=[Rst[si]], writes=[Rst[si]])

        claim_t = tiny

        def claim(extra=()):
            fw.op(pool, lambda: G.memset(claim_t[:, 4:5], 0.0), writes=[ov] + list(extra))

        def w1_dma(l, i, j):
            for part in range(2):
                dstv = aview(j * 2048, 2048, BF16, "p (k c) -> p k c", c=512)[:, :, part * 256:(part + 1) * 256]
                c0 = part * DFF + j * 256
                fw.dma(pool, dstv, ffn_w1[l, i, :, c0:c0 + 256].rearrange("(k p) c -> p k c", p=128),
                       reads=([ov] if 2 * j + 1 >= 20 else []), writes=[slot[2 * j], slot[2 * j + 1]], key=("w", 2 * j, part))

        def w2_dma(l, i, j):
            dstv = aview(22 * 1024 + j * 1024, 1024, BF16, "p (e c) -> p e c", e=2)
            fw.dma(pool, dstv, ffn_w2[l, i, j * 256:(j + 1) * 256, :].rearrange("(e p) c -> p e c", p=128),
                   reads=[ov], writes=[slot[22 + j]], key=("w", 22 + j, 0))

        def mixer_dma(l, m):
            if m < 6:
                fw.dma(pool, WIN(m), w_in[l, :, m * 512:(m + 1) * 512].rearrange("(k p) c -> p k c", p=128),
                       writes=[slot[2 * m], slot[2 * m + 1]], key=("w", 2 * m, 0))
            elif m in (8, 9):
                hf = m - 8
                fw.dma(pool, WIN(8 + hf), w_out[l, :, hf * 512:(hf + 1) * 512].rearrange("(k p) c -> p k c", p=128),
                       writes=[slot[16 + 2 * hf], slot[17 + 2 * hf]], key=("w", 16 + 2 * hf, 0))

        def stage_F(l, i, src, dst, have_w=False, nxt=None):
            sl = 0 if i == 0 else 2
            if not have_w:
                claim()
                for j in range(11):
                    w1_dma(l, i, j)
                for j in range(11):
                    w2_dma(l, i, j)
            stage_consts(l, sl, True)
            load_x(src, tiles[0], 0)
            zi = 0
            prologue(tiles[0], 0)
            for ti, tl in enumerate(tiles):
                b = ti % 2
                if ti + 1 < NT:
                    load_x(src, tiles[ti + 1], 1 - b)
                for hc in range(NHC):
                    j, e = hc // 2, hc % 2
                    w1v = aview(j * 2048, 2048, BF16, "p (k c) -> p k c", c=512)
                    bk = hc % 4

                    def mm(w1v=w1v, e=e, bk=bk, b=b):
                        for part in range(2):
                            for kc in range(8):
                                i_ = T.matmul(banks[bk][:, part * 256:(part + 1) * 256],
                                              lhsT=w1v[:, kc, part * 256 + e * 128: part * 256 + (e + 1) * 128],
                                              rhs=hT[:, b, kc, :], start=(kc == 0), stop=(kc == 7))
                        return i_
                    fw.op(pe, mm, reads=[slot[2 * j], slot[2 * j + 1], RhT[b]], writes=[Rb[bk]])
                    sb_ = hc % 4
                    fw.op(act, lambda bk=bk, sb_=sb_: A.activation(out=silt[:, sb_, :], in_=banks[bk][:, 0:256], func=AF.Silu),
                          reads=[Rb[bk], ov], writes=[Rsil[sb_]])
                    fw.op(dve, lambda bk=bk, sb_=sb_, hc=hc: V.tensor_tensor(out=uT[:, hc, :], in0=banks[bk][:, 256:512], in1=silt[:, sb_, :], op=ALU.mult),
                          reads=[Rb[bk], Rsil[sb_], ov], writes=[RuT])
                    if ti == NT - 1 and nxt is not None and e == 1:
                        if nxt[0] == "F":
                            w1_dma(nxt[1], nxt[2], j)
                        else:
                            mixer_dma(nxt[1], j)
                if ti + 1 < NT:
                    prologue(tiles[ti + 1], 1 - b)
                for s in range(2):
                    for hf in range(2):
                        bk = 4 + s * 2 + hf

                        def mm2(s=s, hf=hf, bk=bk):
                            for hc in range(NHC):
                                w2v = aview(22 * 1024 + (hc // 2) * 1024, 1024, BF16, "p (e c) -> p e c", e=2)
                                i_ = T.matmul(banks[bk][:, :], lhsT=uT[:, hc, s * 128:(s + 1) * 128],
                                              rhs=w2v[:, hc % 2, hf * 512:(hf + 1) * 512], start=(hc == 0), stop=(hc == NHC - 1))
                            return i_
                        fw.op(pe, mm2, reads=[RuT, ov] + [slot[22 + q] for q in range(11)], writes=[Rb[bk]])
                if ti == NT - 1 and nxt is not None and nxt[0] == "F":
                    for j in range(11):
                        w2_dma(nxt[1], nxt[2], j)
                for s in range(2):
                    epilogue(tl, b, s, (4 + s * 2, 5 + s * 2), dst, zi)
                    zi = 1 - zi

        def WIN(m):
            return aview(m * 2048, 2048, BF16, "p (k c) -> p k c", c=512)

        def load_mixer(l, have_w=False):
            if not have_w:
                for m in (0, 1, 2, 3, 4, 5, 8, 9):
                    mixer_dma(l, m)
            for m in (2, 3):
                srcv = aview(m * 2048, 2048, BF16, "p (a two f) -> p a two f", two=2, f=32)
                dstv = aview((m + 4) * 2048, 2048, BF16, "p (a two f) -> p a two f", two=2, f=32)
                fw.op(act, lambda srcv=srcv, dstv=dstv: A.mul(out=dstv[:, :, 0, :], in_=srcv[:, :, 1, :], mul=-1.0),
                      reads=[slot[2 * m], slot[2 * m + 1]], writes=[slot[2 * m + 8], slot[2 * m + 9]])
                fw.op(act, lambda srcv=srcv, dstv=dstv: A.copy(out=dstv[:, :, 1, :], in_=srcv[:, :, 0, :]),
                      reads=[slot[2 * m], slot[2 * m + 1]], writes=[slot[2 * m + 8], slot[2 * m + 9]])

        def layer_consts(l):
            fw.dma(sp, lgt[:], dlogit[l:l + 1, :].partition_broadcast(128), writes=[Rdec], key="lgt")
            fw.op(act, lambda: A.activation(out=lgt[:], in_=lgt[:], func=AF.Exp, scale=-1.0), reads=[Rdec], writes=[Rdec])
            fw.op(act, lambda: A.activation(out=lgt[:], in_=lgt[:], func=AF.Ln, bias=1.0), reads=[Rdec], writes=[Rdec])
            fw.op(act, lambda: A.mul(out=lgt[:], in_=lgt[:], mul=-1.0), reads=[Rdec], writes=[Rdec])
            fw.op(act, lambda: A.activation(out=gC[:], in_=lgt[:], func=AF.Exp, scale=128.0), reads=[Rdec], writes=[Rdec])
            fw.op(dve, lambda: V.tensor_scalar(out=tiny[:, 1:2], in0=piota[:], scalar1=-1.0, scalar2=127.0, op0=ALU.mult, op1=ALU.add),
                  reads=[Rmisc], writes=[Rmisc])
            fw.op(dve, lambda: V.tensor_scalar(out=kdf[:], in0=lgt[:, 0:4], scalar1=tiny[:, 1:2], scalar2=None, op0=ALU.mult), reads=[Rdec, Rmisc], writes=[Rdec])
            fw.op(dve, lambda: V.tensor_scalar(out=kdb[:], in0=lgt[:, 4:8], scalar1=piota[:, 0:1], scalar2=None, op0=ALU.mult), reads=[Rdec, Rmisc], writes=[Rdec])
            fw.op(act, lambda: A.activation(out=kdf[:], in_=kdf[:], func=AF.Exp), reads=[Rdec], writes=[Rdec])
            fw.op(act, lambda: A.activation(out=kdb[:], in_=kdb[:], func=AF.Exp), reads=[Rdec], writes=[Rdec])
            dist = xn[:, 0, 0:128]
            ci = xn[:, 0, 128:256]
            t1 = xn[:, 0, 256:384]
            t2 = xn[:, 0, 384:512]
            mk1 = xn[:, 1, 0:128]
            fw.op(pool, lambda: G.iota(dist, pattern=[[1, 128]], base=0, channel_multiplier=-1, allow_small_or_imprecise_dtypes=True),
                  reads=[ov], writes=[Rxn[0]])
            fw.op(pool, lambda: G.iota(ci, pattern=[[1, 128]], base=0, channel_multiplier=0, allow_small_or_imprecise_dtypes=True),
                  reads=[ov], writes=[Rxn[0]])
            for h in range(4):
                fw.op(dve, lambda: V.tensor_single_scalar(out=t1, in_=dist, scalar=0.0, op=ALU.max), reads=[Rxn[0], ov], writes=[Rxn[0]])
                fw.op(act, lambda h=h: A.activation(out=t1, in_=t1, func=AF.Exp, scale=lgt[:, h:h + 1]), reads=[Rxn[0], Rdec, ov], writes=[Rxn[0]])
                fw.op(dve, lambda: V.tensor_single_scalar(out=mk1, in_=dist, scalar=0.0, op=ALU.is_ge), reads=[Rxn[0], ov], writes=[Rxn[1]])
                fw.op(dve, lambda: V.tensor_tensor(out=t1, in0=t1, in1=mk1, op=ALU.mult), reads=[Rxn[0], Rxn[1], ov], writes=[Rxn[0]])
                fw.op(dve, lambda: V.tensor_scalar(out=t2, in0=dist, scalar1=-1.0, scalar2=0.0, op0=ALU.mult, op1=ALU.max), reads=[Rxn[0], ov], writes=[Rxn[0]])
                fw.op(act, lambda h=h: A.activation(out=t2, in_=t2, func=AF.Exp, scale=lgt[:, 4 + h:5 + h]), reads=[Rxn[0], Rdec, ov], writes=[Rxn[0]])
                fw.op(dve, lambda: V.tensor_single_scalar(out=mk1, in_=dist, scalar=0.0, op=ALU.is_le), reads=[Rxn[0], ov], writes=[Rxn[1]])
                fw.op(dve, lambda: V.tensor_tensor(out=t2, in0=t2, in1=mk1, op=ALU.mult), reads=[Rxn[0], Rxn[1], ov], writes=[Rxn[0]])
                fw.op(dve, lambda h=h: V.scalar_tensor_tensor(out=Mcomb[:, h * 128:(h + 1) * 128], in0=t1, scalar=1.0, in1=t2, op0=ALU.mult, op1=ALU.add),
                      reads=[Rxn[0], ov], writes=[Rdec])
                fw.op(dve, lambda: V.tensor_single_scalar(out=t1, in_=ci, scalar=1.0, op=ALU.add), reads=[Rxn[0], ov], writes=[Rxn[0]])
                fw.op(act, lambda h=h: A.activation(out=Qf[:, h, 0:128], in_=t1, func=AF.Exp, scale=lgt[:, h:h + 1]), reads=[Rxn[0], Rdec, ov], writes=[Rdec])
                fw.op(dve, lambda: V.tensor_scalar(out=t2, in0=ci, scalar1=-1.0, scalar2=128.0, op0=ALU.mult, op1=ALU.add), reads=[Rxn[0], ov], writes=[Rxn[0]])
                fw.op(act, lambda h=h: A.activation(out=Qb[:, h, 0:128], in_=t2, func=AF.Exp, scale=lgt[:, 4 + h:5 + h]), reads=[Rxn[0], Rdec, ov], writes=[Rdec])
            fw.op(dve, lambda: V.tensor_single_scalar(out=Mcomb, in_=Mcomb, scalar=QSCALE, op=ALU.mult), reads=[Rdec, ov], writes=[Rdec])
            fw.op(dve, lambda: V.tensor_single_scalar(out=Qf[:, :, 0:128], in_=Qf[:, :, 0:128], scalar=QSCALE, op=ALU.mult), reads=[Rdec, ov], writes=[Rdec])
            fw.op(dve, lambda: V.tensor_single_scalar(out=Qb[:, :, 0:128], in_=Qb[:, :, 0:128], scalar=QSCALE, op=ALU.mult), reads=[Rdec, ov], writes=[Rdec])
            fw.op(dve, lambda: V.tensor_copy(out=Qf[:, :, 128:256], in_=Qf[:, :, 0:128]), reads=[Rdec, ov], writes=[Rdec])
            fw.op(dve, lambda: V.tensor_copy(out=Qb[:, :, 128:256], in_=Qb[:, :, 0:128]), reads=[Rdec, ov], writes=[Rdec])
            fw.dma(sp, cwraw[0:CK, 1, :], conv_w[l, :, :], reads=[ov], writes=[Rxn[1]], key="cwr")

            def trw():
                for cc in range(4):
                    i_ = T.transpose(out=banks[0][:, cc * 32:cc * 32 + CK], in_=cwraw[0:CK, 1, cc * 128:(cc + 1) * 128], identity=ident[0:CK, 0:CK])
                return i_
            fw.op(pe, trw, reads=[Rxn[1], Rident, ov], writes=[Rb[0]])
            fw.op(dve, lambda: V.tensor_copy(out=cw[:], in_=banks[0][:, 0:128].rearrange("p (c k) -> p c k", k=32)[:, :, 0:CK]), reads=[Rb[0]], writes=[Rcw])
            for vi, vec in enumerate((conv_b, conv_ln_g, conv_ln_b)):
                fw.dma(sp, cvec[:, vi, :], vec[l, :].rearrange("(c p) -> p c", p=128), writes=[Rcw], key=("cv", vi))
            fw.op(pool, lambda: G.memset(upb, 0.0), reads=[ov], writes=[Rup])

            def mkd():
                for cc in range(4):
                    for kk in range(CK):
                        i_ = V.tensor_scalar(out=Dblk[:, cc, kk, :], in0=dmask[:], scalar1=cw[:, cc, kk:kk + 1], scalar2=None, op0=ALU.mult)
                return i_
            fw.op(dve, mkd, reads=[Rcw, Rdm, ov], writes=[Rdb])

        def proj_fm(tl, m, outT, Rout, versions=None, bk0=2):
            isS = tl["kind"] == "S"
            b = tl["idx"] % 2
            for h in range(4):
                bk = bk0 + (h % 2)
                nparts = 2 if isS else 1

                def mm(h=h, bk=bk, nparts=nparts):
                    for part in range(nparts):
                        wv = WIN(m if part == 0 else m + 4)
                        for kc in range(8):
                            i_ = T.matmul(banks[bk][:, part * 256:(part + 1) * 256], lhsT=wv[:, kc, h * 128:(h + 1) * 128], rhs=hT[:, b, kc, :],
                                          start=(kc == 0), stop=(kc == 7))
                    return i_
                rd = [slot[2 * m], slot[2 * m + 1], RhT[b]] + ([slot[2 * m + 8], slot[2 * m + 9]] if isS else [])
                fw.op(pe, mm, reads=rd, writes=[Rb[bk]])
                r = h % 2
                if isS:
                    fw.op(dve, lambda bk=bk, r=r: V.tensor_tensor(out=rt1[:, r, :], in0=banks[bk][:, 0:256], in1=ropeCS[:, b, 0, :], op=ALU.mult),
                          reads=[Rb[bk], Rcs[b], ov], writes=[Rrt[r]])
                    fw.op(dve, lambda bk=bk, r=r: V.tensor_tensor(out=rt2[:, r, :], in0=banks[bk][:, 256:512], in1=ropeCS[:, b, 1, :], op=ALU.mult),
                          reads=[Rb[bk], Rcs[b], ov], writes=[Rrt[r]])
                    fw.op(pool, lambda r=r: G.tensor_tensor(out=rt1[:, r, :], in0=rt1[:, r, :], in1=rt2[:, r, :], op=ALU.add),
                          reads=[Rrt[r], ov], writes=[Rrt[r]])
                    srcv, rsrc = rt1[:, r, :], Rrt[r]
                else:
                    srcv, rsrc = banks[bk][:, 0:256], Rb[bk]
                fw.op(act, lambda h=h, srcv=srcv: A.copy(out=outT[:, h, :], in_=srcv), reads=[rsrc, ov], writes=[Rout])
                if versions:
                    for (Qt, dstT) in versions:
                        if isS:
                            fw.op(pool, lambda h=h, srcv=srcv, Qt=Qt, dstT=dstT: G.tensor_tensor(out=dstT[:, h, :], in0=srcv, in1=Qt[:, h, :], op=ALU.mult),
                                  reads=[rsrc, Rdec, ov], writes=[Rout])
                        else:
                            fw.op(dve, lambda h=h, srcv=srcv, Qt=Qt, dstT=dstT: V.tensor_tensor(out=dstT[:, h, :], in0=srcv, in1=Qt[:, h, :], op=ALU.mult),
                                  reads=[rsrc, Rdec, ov], writes=[Rout])

        def proj_tm(s, m, bk, b):
            def mm():
                for kc in range(8):
                    i_ = T.matmul(banks[bk][:, :], lhsT=hT[:, b, kc, s * 128:(s + 1) * 128], rhs=WIN(m)[:, kc, :], start=(kc == 0), stop=(kc == 7))
                return i_
            fw.op(pe, mm, reads=[slot[2 * m], slot[2 * m + 1], RhT[b]], writes=[Rb[bk]])

        def k_tm(s, kd, bk, ki):
            pb = banks[bk][:, 0:256].bitcast(BF16)

            def tr():
                for h in range(4):
                    i_ = T.transpose(out=pb[:, h * 128:(h + 1) * 128], in_=kT[:, h, s * 128:(s + 1) * 128], identity=identb[:])
                return i_
            fw.op(pe, tr, reads=[RkT, Rmisc, ov], writes=[Rb[bk]])

            def ev():
                for h in range(4):
                    i_ = A.activation(out=kdT[:, ki, h * 128:(h + 1) * 128], in_=pb[:, h * 128:(h + 1) * 128], func=AF.Copy, scale=kd[:, h:h + 1])
                return i_
            fw.op(act, ev, reads=[Rb[bk], Rdec, ov], writes=[RkdT[ki]])

        def kv_mm(s, bk, ki):
            def mm():
                for h in range(4):
                    i_ = T.matmul(banks[bk][:, h * 128:(h + 1) * 128], lhsT=kdT[:, ki, h * 128:(h + 1) * 128], rhs=vbf[:, s, h * 128:(h + 1) * 128],
                                  start=True, stop=True)
                return i_
            fw.op(pe, mm, reads=[RkdT[ki], Rvbf, ov], writes=[Rb[bk]])

        def state_update(Rt, RRt, bk, goff):
            def up():
                for h in range(4):
                    i_ = V.scalar_tensor_tensor(out=Rt[:, h * 128:(h + 1) * 128], in0=Rt[:, h * 128:(h + 1) * 128], scalar=gC[:, goff + h:goff + h + 1],
                                                in1=banks[bk][:, h * 128:(h + 1) * 128], op0=ALU.mult, op1=ALU.add)
                return i_
            fw.op(dve, up, reads=[Rb[bk], Rdec, ov, RRt], writes=[RRt])

        def load_cs(tl):
            if tl["kind"] == "S":
                b = tl["idx"] % 2
                for c in range(2):
                    fw.dma(sp, ropeCS[:, b, c, :], rope[c, :, tl["tok0"]:tl["tok0"] + 256], reads=[Rrope, ov], writes=[Rcs[b]], key=("cs", b, c))

        def stage_KB(l, src, have_w=False):
            claim(extra=[slot[i] for i in range(20, 33)])
            load_mixer(l, have_w)
            layer_consts(l)
            stage_consts(l, 1, False)
            order = list(reversed(tiles))

            def kb_pk(tl):
                proj_fm(tl, 3, kT, RkT)

            def kb_pv(tl):
                b = tl["idx"] % 2
                for s in range(2):
                    proj_tm(s, 4, 4 + s, b)
                    fw.op(act, lambda s=s: A.copy(out=vbf[:, s, :], in_=banks[4 + s][:, :]), reads=[Rb[4 + s], ov], writes=[Rvbf])

            def kb_ld(tl):
                load_x(src, tl, tl["idx"] % 2)
                load_cs(tl)

            kb_ld(order[0])
            if NT > 1:
                kb_ld(order[1])
            prologue(order[0], order[0]["idx"] % 2)
            kb_pk(order[0])
            kb_pv(order[0])
            if NT > 1:
                prologue(order[1], order[1]["idx"] % 2)
            for oi, tl in enumerate(order):
                b = tl["idx"] % 2
                isS = tl["kind"] == "S"
                if oi + 2 < NT:
                    kb_ld(order[oi + 2])
                k_tm(1, kdb, 7, 1)
                k_tm(0, kdb, 6, 0)
                if oi + 1 < NT:
                    kb_pk(order[oi + 1])
                last_tile_of_seq = (not isS) or (tl["idx"] == ST // 256 - 1)
                if last_tile_of_seq:
                    if isS:
                        fw.dma(sp, Rbs.rearrange("p (h v) -> p h v", h=4), stb[l].rearrange("h d v -> d h v"), reads=[ov], writes=[RRbs], key="rbs")
                    else:
                        fw.op(pool, lambda: G.memset(Rbs, 0.0), reads=[ov], writes=[RRbs])
                for s in (1, 0):
                    ch = tl["tok0"] // 128 + s
                    kv_mm(s, 6 + s, s)
                    ri = ch % 2
                    fw.op(act, lambda ri=ri: A.copy(out=rbt[:, ri, 0, :], in_=Rbs), reads=[RRbs, ov], writes=[Rrbt[ri]])
                    fw.dma(sp, rbd[ch], rbt[:, ri, 0, :], reads=[Rrbt[ri]], writes=[Rrbd[ch]], key=("rbo", ri))
                    state_update(Rbs, RRbs, 6 + s, 4)
                if not isS:
                    fw.dma(sp, nsb[tl["seq"], l].rearrange("h d v -> d h v"), Rbs.rearrange("p (h v) -> p h v", h=4), reads=[RRbs, ov], key="nsb")
                if oi + 1 < NT:
                    kb_pv(order[oi + 1])
                if oi + 2 < NT:
                    prologue(order[oi + 2], order[oi + 2]["idx"] % 2)

        def stage_M(l, src, dst):
            stage_consts(l, 1, False)
            load_x(src, tiles[0], 0)
            load_cs(tiles[0])
            prologue(tiles[0], 0)
            zi = [0]

            def conv_in(tl, b):
                isS = tl["kind"] == "S"
                for cc in range(4):
                    bk = 6 + cc % 2

                    def mmc(cc=cc, bk=bk, b=b):
                        for part in range(2):
                            for kc in range(8):
                                i_ = T.matmul(banks[bk][:, part * 256:(part + 1) * 256], lhsT=WIN(part)[:, kc, cc * 128:(cc + 1) * 128], rhs=hT[:, b, kc, :],
                                              start=(kc == 0), stop=(kc == 7))
                        return i_
                    fw.op(pe, mmc, reads=[slot[0], slot[1], slot[2], slot[3], RhT[b]], writes=[Rb[bk]])
                    r = cc % 2
                    fw.op(act, lambda bk=bk, r=r: A.activation(out=sgm[:, r, :], in_=banks[bk][:, 256:512], func=AF.Sigmoid), reads=[Rb[bk], ov], writes=[Rsgm[r]])
                    if isS:
                        outv = upS[:, cc, :, 15:79]
                        in0 = banks[bk][:, 0:256].rearrange("p (r w) -> p r w", w=64)
                        in1 = sgm[:, r, :].rearrange("p (r w) -> p r w", w=64)
                    else:
                        outv = upP[:, cc, 15:271]
                        in0 = banks[bk][:, 0:256]
                        in1 = sgm[:, r, :]
                    fw.op(dve, lambda outv=outv, in0=in0, in1=in1: V.tensor_tensor(out=outv, in0=in0, in1=in1, op=ALU.mult),
                          reads=[Rb[bk], Rsgm[r], ov], writes=[Rup])

            CBK = [0, 1, 6, 7]

            def conv_mm(tl, ccs):
                isS = tl["kind"] == "S"

                def mmcv():
                    for cc in ccs:
                        for kk in range(CK):
                            for g in range(4):
                                ps = slice(32 * g, 32 * g + 32)
                                bkc = CBK[(g + cc) % 4]
                                if isS:
                                    ov_ = banks[bkc][ps, 0:256].rearrange("p (r w) -> p r w", w=64)
                                    rh = upS[ps, cc, :, kk:kk + 64]
                                else:
                                    ov_ = banks[bkc][ps, 0:256]
                                    rh = upP[ps, cc, kk:kk + 256]
                                i_ = T.matmul(ov_, lhsT=Dblk[ps, cc, kk, :], rhs=rh, start=(kk == 0), stop=(kk == CK - 1),
                                              tile_position=(32 * g, 32 * g))
                    return i_
                fw.op(pe, mmcv, reads=[Rup, Rdb, ov], writes=[Rb[q_] for q_ in CBK])

            def conv_evac():
                for cc in range(4):
                    def evc(cc=cc):
                        for g in range(4):
                            ps = slice(32 * g, 32 * g + 32)
                            i_ = V.tensor_scalar(out=acc[ps, cc, :], in0=banks[CBK[(g + cc) % 4]][ps, 0:256], scalar1=cvec[ps, 0, cc:cc + 1], scalar2=None, op0=ALU.add)
                        return i_
                    fw.op(dve, evc, reads=[Rb[q_] for q_ in CBK] + [Rcw, ov], writes=[Racc[cc]])

            def conv_ln_a():
                for s in range(2):
                    bk = 4 + s

                    def trc(s=s, bk=bk):
                        for cc in range(4):
                            i_ = T.transpose(out=banks[bk][:, cc * 128:(cc + 1) * 128], in_=acc[:, cc, s * 128:(s + 1) * 128], identity=ident[:])
                        return i_
                    fw.op(pe, trc, reads=Racc + [Rident, ov], writes=[Rb[bk]])
                for s in range(2):
                    bk = 4 + s
                    ln_rows(banks[bk], Rb[bk], 2 + s, 512)
                    fw.op(act, lambda s=s, bk=bk: A.activation(out=xn[:, s, :], in_=banks[bk][:, :], func=AF.Identity, scale=rstd[:, 2 + s, 0:1], bias=rstd[:, 2 + s, 1:2]),
                          reads=[Rb[bk], Rst[2 + s], ov], writes=[Rxn[s]])

            def conv_ln_b():
                for pr in range(2):
                    bk = 4 + pr

                    def trb(pr=pr, bk=bk):
                        for c2 in range(2):
                            for s in range(2):
                                cc = pr * 2 + c2
                                i_ = T.transpose(out=banks[bk][:, c2 * 256 + s * 128:c2 * 256 + (s + 1) * 128], in_=xn[:, s, cc * 128:(cc + 1) * 128], identity=ident[:])
                        return i_
                    fw.op(pe, trb, reads=[Rxn[0], Rxn[1], Rident, ov], writes=[Rb[bk]])

                    def evb(pr=pr, bk=bk):
                        for c2 in range(2):
                            cc = pr * 2 + c2
                            i_ = A.activation(out=mixT[:, cc, :], in_=banks[bk][:, c2 * 256:(c2 + 1) * 256], func=AF.Silu, scale=cvec[:, 1, cc:cc + 1], bias=cvec[:, 2, cc:cc + 1])
                        return i_
                    fw.op(act, evb, reads=[Rb[bk], Rcw, ov], writes=[Rmix])

            def ret_rfb(tl, s):
                fi = (tl["tok0"] // 128 + s) % 2
                fw.op(act, lambda fi=fi: A.copy(out=Rfb[:, fi, :], in_=Rf), reads=[RRf, ov], writes=[RRfb[fi]])

            def ret_scores(s, bk):
                def mms(s=s, bk=bk):
                    for h in range(4):
                        i_ = T.matmul(banks[bk][:, h * 128:(h + 1) * 128], lhsT=kT[:, h, s * 128:(s + 1) * 128], rhs=qT[:, h, s * 128:(s + 1) * 128], start=True, stop=True)
                    return i_
                fw.op(pe, mms, reads=[RkT, RqT, ov], writes=[Rb[bk]])
                fw.op(dve, lambda s=s, bk=bk: V.tensor_tensor(out=Pm[:, s, :], in0=banks[bk][:, :], in1=Mcomb, op=ALU.mult), reads=[Rb[bk], Rdec, ov], writes=[RPm[s]])

            def ret_o(tl, b, s, bk):
                fi = (tl["tok0"] // 128 + s) % 2

                def mmo(s=s, fi=fi, b=b, bk=bk):
                    for h in range(4):
                        hs = slice(h * 128, (h + 1) * 128)
                        T.matmul(banks[bk][:, hs], lhsT=Pm[:, s, hs], rhs=vbf[:, s, hs], start=True, stop=False)
                        T.matmul(banks[bk][:, hs], lhsT=qfT[:, h, s * 128:(s + 1) * 128], rhs=Rfb[:, fi, hs], start=False, stop=False)
                        i_ = T.matmul(banks[bk][:, hs], lhsT=qbT[:, h, s * 128:(s + 1) * 128], rhs=rbt[:, b, s, hs], start=False, stop=True)
                    return i_
                fw.op(pe, mmo, reads=[RPm[s], Rvbf, RqT, RRfb[fi], Rrbt[b], ov], writes=[Rb[bk]])

            def gn_pre(s, bko):
                gi = s

                def gbs(gi=gi):
                    for h in range(4):
                        i_ = V.bn_stats(out=gst[:, gi, h * 6:(h + 1) * 6], in_=banks[bko][:, h * 128:(h + 1) * 128])
                    return i_
                fw.op(dve, gbs, reads=[Rb[bko]], writes=[Rgst[gi]])

                def gag(gi=gi):
                    for h in range(4):
                        i_ = V.bn_aggr(out=gmv[:, gi, h * 2:(h + 1) * 2], in_=gst[:, gi, h * 6:(h + 1) * 6])
                    return i_
                fw.op(dve, gag, reads=[Rgst[gi]], writes=[Rgst[gi]])
                gm = gmv[:, gi, :].rearrange("p (h two) -> p h two", two=2)
                fw.op(act, lambda gi=gi, gm=gm: A.activation(out=grs[:, gi, 0:4], in_=gm[:, :, 1], func=AF.Sqrt, bias=epst[:, 0:1]), reads=[Rgst[gi], Rmisc], writes=[Rgst[gi]])
                fw.op(dve, lambda gi=gi: V.reciprocal(out=grs[:, gi, 0:4], in_=grs[:, gi, 0:4]), reads=[Rgst[gi]], writes=[Rgst[gi]])
                fw.op(dve, lambda gi=gi, gm=gm: V.scalar_tensor_tensor(out=grs[:, gi, 4:8], in0=gm[:, :, 0], scalar=-1.0, in1=grs[:, gi, 0:4], op0=ALU.mult, op1=ALU.mult),
                      reads=[Rgst[gi]], writes=[Rgst[gi]])
                onv, ogv = onb[s], ogb[s]

                def gev(gi=gi, onv=onv):
                    for h in range(4):
                        i_ = A.activation(out=onv[:, h * 128:(h + 1) * 128], in_=banks[bko][:, h * 128:(h + 1) * 128], func=AF.Identity,
                                          scale=grs[:, gi, h:h + 1], bias=grs[:, gi, 4 + h:5 + h])
                    return i_
                fw.op(act, gev, reads=[Rb[bko], Rgst[gi], ov], writes=[Ron[s]])
                fw.op(pool, lambda s=s, onv=onv, ogv=ogv: G.tensor_tensor(out=ogv, in0=onv, in1=sg[:, s, :], op=ALU.mult), reads=[Ron[s], Rsg, ov], writes=[Rog[s]])

            def gn_post(s, bkt):
                ogv = ogb[s]
                pb = banks[bkt][:, 0:256].bitcast(BF16)

                def tro(pb=pb, ogv=ogv):
                    for h in range(4):
                        i_ = T.transpose(out=pb[:, h * 128:(h + 1) * 128], in_=ogv[:, h * 128:(h + 1) * 128], identity=identb[:])
                    return i_
                fw.op(pe, tro, reads=[Rog[s], Rmisc, ov], writes=[Rb[bkt]])
                fw.op(act, lambda s=s, pb=pb: A.copy(out=mixT[:, 4:8, s * 128:(s + 1) * 128], in_=pb.rearrange("p (h t) -> p h t", t=128)),
                      reads=[Rb[bkt], ov], writes=[Rmix])

            def head1(tl):
                b = tl["idx"] % 2
                isS = tl["kind"] == "S"
                if (not isS) and tiles[tl["idx"] - 1]["kind"] == "S":
                    fw.op(pool, lambda: G.memset(upb, 0.0), reads=[ov, Rup], writes=[Rup])
                conv_in(tl, b)
                proj_fm(tl, 2, qT, RqT, versions=[(Qf, qfT), (Qb, qbT)], bk0=0)
                proj_fm(tl, 3, kT, RkT, bk0=0)

            def head2(tl):
                b = tl["idx"] % 2
                for s in range(2):
                    proj_tm(s, 4, 0, b)
                    fw.op(act, lambda s=s: A.copy(out=vbf[:, s, :], in_=banks[0][:, :]), reads=[Rb[0], ov], writes=[Rvbf])
                    proj_tm(s, 5, 1, b)
                    fw.op(act, lambda s=s: A.activation(out=sg[:, s, :], in_=banks[1][:, :], func=AF.Silu), reads=[Rb[1], ov], writes=[Rsg])
                conv_mm(tl, (0, 1, 2, 3))
                conv_evac()

            def rbt_load(tl):
                b = tl["idx"] % 2
                for s in range(2):
                    ch = tl["tok0"] // 128 + s
                    fw.dma(sp, rbt[:, b, s, :], rbd[ch], reads=[Rrbd[ch]], writes=[Rrbt[b]], key=("rbi", b, s))

            def tailA1(tl):
                isS = tl["kind"] == "S"
                if isS and tl["first"]:
                    fw.dma(sp, Rf.rearrange("p (h v) -> p h v", h=4), stf[l].rearrange("h d v -> d h v"), reads=[ov], writes=[RRf], key="rfs")
                elif not isS:
                    fw.op(pool, lambda: G.memset(Rf, 0.0), reads=[ov], writes=[RRf])
                ret_rfb(tl, 0)
                ret_scores(0, 2)
                ret_scores(1, 3)
                k_tm(0, kdf, 4, 0)
                k_tm(1, kdf, 5, 1)

            rbt_load(tiles[0])
            if NT > 1:
                load_x(src, tiles[1], 1)
                load_cs(tiles[1])
            head1(tiles[0])
            head2(tiles[0])
            tailA1(tiles[0])
            for ti, tl in enumerate(tiles):
                b = ti % 2
                isS = tl["kind"] == "S"
                if ti + 1 < NT:
                    rbt_load(tiles[ti + 1])
                kv_mm(0, 4, 0)
                kv_mm(1, 5, 1)
                ret_o(tl, b, 0, 2)
                state_update(Rf, RRf, 4, 0)
                ret_rfb(tl, 1)
                ret_o(tl, b, 1, 3)
                state_update(Rf, RRf, 5, 0)
                if not isS:
                    fw.dma(sp, nsf[tl["seq"], l].rearrange("h d v -> d h v"), Rf.rearrange("p (h v) -> p h v", h=4), reads=[RRf, ov], key="nsf")
                conv_ln_a()
                gn_pre(0, 2)
                gn_pre(1, 3)
                if ti + 1 < NT:
                    prologue(tiles[ti + 1], 1 - b)
                    head1(tiles[ti + 1])
                    head2(tiles[ti + 1])
                conv_ln_b()
                gn_post(0, 2)
                gn_post(1, 3)
                if ti + 1 < NT:
                    tailA1(tiles[ti + 1])
                for s in range(2):
                    yb = (6, 7) if s == 0 else (0, 1)
                    for hf in range(2):
                        bk = yb[hf]

                        def mmw(s=s, hf=hf, bk=bk):
                            for kc in range(8):
                                i_ = T.matmul(banks[bk][:, :], lhsT=mixT[:, kc, s * 128:(s + 1) * 128], rhs=WIN(8 + hf)[:, kc, :], start=(kc == 0), stop=(kc == 7))
                            return i_
                        fw.op(pe, mmw, reads=[Rmix, ov, slot[16 + 2 * hf], slot[17 + 2 * hf]], writes=[Rb[bk]])
                for s in range(2):
                    yb = (6, 7) if s == 0 else (0, 1)
                    epilogue(tl, b, s, yb, dst, zi[0])
                    zi[0] = 1 - zi[0]
                if ti + 2 < NT:
                    load_x(src, tiles[ti + 2], b)
                    load_cs(tiles[ti + 2])

        seq = []
        for l in range(NL):
            seq += [(l, "F0"), (l, "KB"), (l, "M"), (l, "F1")]
        if stages is not None:
            seq = [s_ for s_ in seq if s_ in stages]
        n_xstage = sum(1 for s_ in seq if s_[1] != "KB")
        xi = 0
        cur = xin
        have = False
        for si, (l, nm) in enumerate(seq):
            nx = seq[si + 1] if si + 1 < len(seq) else None
            if nm == "KB":
                stage_KB(l, cur, have_w=have)
                have = False
                continue
            xi += 1
            dst = yout if xi == n_xstage else xs
            if nm in ("F0", "F1"):
                nxt = None
                if PREFETCH and nx is not None:
                    if nx[1] == "KB":
                        nxt = ("M", nx[0])
                    elif nx[1] in ("F0", "F1"):
                        nxt = ("F", nx[0], 0 if nx[1] == "F0" else 1)
                stage_F(l, 0 if nm == "F0" else 1, cur, dst, have_w=have, nxt=nxt)
                have = nxt is not None
            else:
                stage_M(l, cur, dst)
                have = False
            cur = dst
        fw.finish(sp)
        fw.run()
    return nc


_W_KEYS = ["w_ada", "b_ada", "ln_g", "ln_b", "ffn_w1", "ffn_w2", "w_in", "w_out", "conv_w", "conv_b", "conv_ln_g", "conv_ln_b"]


def kernel(x_prompt, x_sample, state_ret_fwd, state_ret_bwd, c, c_ctx, w_ada, b_ada, ln_g, ln_b,
           ffn_w1, ffn_w2, w_in, w_out, conv_w, conv_b, conv_ln_g, conv_ln_b, ret_decay_logit):
    f = lambda a: np.ascontiguousarray(np.asarray(a, dtype=np.float32))
    NCORE = 8
    NL = w_in.shape[0]
    B, S = x_prompt.shape[0], x_prompt.shape[1]
    DB, ST = x_sample.shape[0], x_sample.shape[1]
    NP = B // NCORE
    assert DB == NCORE and S == 256
    nc = build(NL=NL, ST=ST, NP=NP)
    wts = dict(w_ada=f(w_ada), b_ada=f(b_ada), ln_g=f(ln_g), ln_b=f(ln_b), ffn_w1=f(ffn_w1), ffn_w2=f(ffn_w2), w_in=f(w_in),
               w_out=f(w_out), conv_w=f(conv_w), conv_b=f(conv_b), conv_ln_g=f(conv_ln_g), conv_ln_b=f(conv_ln_b),
               dlogit=f(ret_decay_logit).reshape(NL, 8))
    in_maps = []
    for i in range(NCORE):
        m = dict(wts)
        m["xin"] = np.concatenate([f(x_sample[i]), f(x_prompt[i * NP:(i + 1) * NP]).reshape(NP * S, D)], axis=0)
        m["stf"] = f(state_ret_fwd[i])
        m["stb"] = f(state_ret_bwd[i])
        m["cond"] = np.stack([f(c[i]), f(c_ctx)], axis=0)
        in_maps.append(m)
    res = run_bass_kernel_spmd(nc, in_maps, core_ids=list(range(NCORE)))
    outs = res.results
    y_sample = np.stack([outs[i]["yout"][:ST] for i in range(NCORE)], axis=0)
    y_prompt = np.concatenate([outs[i]["yout"][ST:].reshape(NP, S, D) for i in range(NCORE)], axis=0)
    nf = np.concatenate([outs[i]["nsf"] for i in range(NCORE)], axis=0)
    nb = np.concatenate([outs[i]["nsb"] for i in range(NCORE)], axis=0)
    return (y_prompt.astype(np.float32), y_sample.astype(np.float32), nf.astype(np.float32), nb.astype(np.float32))
```

```python
import contextlib
import math
import numpy as np
import concourse.bass as bass
import concourse.mybir as mybir
from concourse.bass_utils import run_bass_kernel_spmd

F32 = mybir.dt.float32
BF16 = mybir.dt.bfloat16
AF = mybir.ActivationFunctionType
ALU = mybir.AluOpType

D = 1024
DFF = 2816
NHC = 22
DEPTH = 4
ALPHA = (2.0 * DEPTH) ** 0.25
EPS = 1e-5
CK = 31
QSCALE = 128.0 ** -0.5
TWO_PI = 2.0 * math.pi
PREFETCH = True


class Res:
    __slots__ = ("name", "w", "rs")

    def __init__(self, name):
        self.name = name
        self.w = None
        self.rs = {}


class Eng:
    def __init__(self, name, h, sem, is_pe=False):
        self.name = name
        self.h = h
        self.sem = sem
        self.n = 0
        self.seen = {}
        self.is_pe = is_pe
        self.prog = []


class FW:
    def __init__(self, nc, stack):
        self.nc = nc
        self.stack = stack
        mk = lambda nm, h, pe=False: Eng(nm, h, stack.enter_context(nc.semaphore("s_" + nm)), pe)
        self.pe = mk("pe", nc.tensor, True)
        self.act = mk("act", nc.scalar)
        self.dve = mk("dve", nc.vector)
        self.pool = mk("pool", nc.gpsimd)
        self.sp = mk("sp", nc.sync)
        self.dma_sems = {}
        self.nres = 0

    def res(self, name=None):
        self.nres += 1
        return Res(name or f"r{self.nres}")

    def _wait(self, eng, dep):
        key, (sem, idx, is_eng) = dep
        if is_eng and key == eng.name and eng.is_pe:
            return
        if eng.seen.get(key, 0) >= idx:
            return
        eng.prog.append(lambda h=eng.h, s=sem, v=idx: h.wait_ge(s, v))
        eng.seen[key] = idx

    def _deps(self, reads, writes):
        deps = []
        for r in reads:
            if r.w is not None:
                deps.append(r.w)
        for w in writes:
            if w.w is not None:
                deps.append(w.w)
            deps.extend(w.rs.items())
        return deps

    def _mark(self, me, reads, writes):
        key, val = me
        for r in reads:
            r.rs[key] = val
        for w in writes:
            w.w = me
            w.rs = {}

    def op(self, eng, fn, reads=(), writes=()):
        for d in self._deps(reads, writes):
            self._wait(eng, d)
        eng.n += 1
        eng.prog.append(lambda fn=fn, s=eng.sem: fn().then_inc(s, 1))
        me = (eng.name, (eng.sem, eng.n, True))
        self._mark(me, reads, writes)
        return me

    def dma(self, qeng, out, in_, reads=(), writes=(), key=None):
        for d in self._deps(reads, writes):
            self._wait(qeng, d)
        if key not in self.dma_sems:
            self.dma_sems[key] = [self.stack.enter_context(self.nc.semaphore("d_%d" % len(self.dma_sems))), 0]
        ent = self.dma_sems[key]
        ent[1] += 16
        qeng.prog.append(lambda h=qeng.h, o=out, i=in_, s=ent[0]: h.dma_start(out=o, in_=i).then_inc(s, 16))
        me = (("dma", key), (ent[0], ent[1], False))
        self._mark(me, reads, writes)
        return me

    def finish(self, eng):
        for key, (sem, val) in self.dma_sems.items():
            if val:
                eng.prog.append(lambda h=eng.h, s=sem, v=val: h.wait_ge(s, v))
        for e2 in (self.pe, self.act, self.dve, self.pool, self.sp):
            if e2 is not eng and e2.n:
                eng.prog.append(lambda h=eng.h, s=e2.sem, v=e2.n: h.wait_ge(s, v))

    def run(self):
        with self.nc.Block() as block:
            @block.tensor
            def _(e):
                for t in self.pe.prog:
                    t()

            @block.scalar
            def _(e):
                for t in self.act.prog:
                    t()

            @block.vector
            def _(e):
                for t in self.dve.prog:
                    t()

            @block.gpsimd
            def _(e):
                for t in self.pool.prog:
                    t()

            @block.sync
            def _(e):
                for t in self.sp.prog:
                    t()


def build(NL=4, ST=4096, NP=4, stages=None, dbg=False):
    NTOK = ST + NP * 256
    NCH = NTOK // 128
    nc = bass.Bass("TRN2", target_bir_lowering=False)
    din = lambda n, s, dt=F32: nc.dram_tensor(n, s, dt, kind="ExternalInput").ap()
    dout = lambda n, s: nc.dram_tensor(n, s, F32, kind="ExternalOutput").ap()
    dint = lambda n, s, dt=F32: nc.dram_tensor(n, s, dt, kind="Internal").ap()
    xin = din("xin", [NTOK, D])
    stf = din("stf", [NL, 4, 128, 128])
    stb = din("stb", [NL, 4, 128, 128])
    cond = din("cond", [2, D])
    w_ada = din("w_ada", [NL, D, 9 * D])
    b_ada = din("b_ada", [NL, 9 * D])
    ln_g = din("ln_g", [NL, 3, D])
    ln_b = din("ln_b", [NL, 3, D])
    ffn_w1 = din("ffn_w1", [NL, 2, D, 2 * DFF])
    ffn_w2 = din("ffn_w2", [NL, 2, DFF, D])
    w_in = din("w_in", [NL, D, 3072])
    w_out = din("w_out", [NL, D, D])
    conv_w = din("conv_w", [NL, CK, 512])
    conv_b = din("conv_b", [NL, 512])
    conv_ln_g = din("conv_ln_g", [NL, 512])
    conv_ln_b = din("conv_ln_b", [NL, 512])
    dlogit = din("dlogit", [NL, 8])
    yout = dout("yout", [NTOK, D])
    nsf = dout("nsf", [max(NP, 1), NL, 4, 128, 128])
    nsb = dout("nsb", [max(NP, 1), NL, 4, 128, 128])
    xs = dint("xs", [NTOK, D])
    mods = dint("mods", [NL, 2, 9 * D])
    rope = (dout if dbg else dint)("rope", [2, 128, max(ST, 256)])
    rbd = dint("rbd", [NCH, 128, 512], BF16)

    tiles = []
    for t in range(ST // 256):
        tiles.append(dict(kind="S", tok0=t * 256, cond=0, first=(t == 0), idx=len(tiles)))
    for n in range(NP):
        tiles.append(dict(kind="P", tok0=ST + n * 256, cond=1, seq=n, idx=len(tiles)))
    NT = len(tiles)

    with contextlib.ExitStack() as st, nc.allow_non_contiguous_dma(reason="small vector gathers"):
        fw = FW(nc, st)
        pe, act, dve, pool, sp = fw.pe, fw.act, fw.dve, fw.pool, fw.sp
        sbt = lambda n, s, dt=F32: st.enter_context(nc.sbuf_tensor(n, s, dt))
        R = fw.res

        arena = sbt("arena", [128, 33 * 1024])
        aux = sbt("aux", [128, 4864])
        xbuf = sbt("xbuf", [128, 2, 2, D])
        hT = sbt("hT", [128, 2, 8, 256], BF16)
        zb = sbt("zb", [128, 2, D])
        gate = sbt("gate", [128, 2, D])
        lng = sbt("lng", [128, D])
        lnb = sbt("lnb", [128, D])
        rbt = sbt("rbt", [128, 2, 2, 512], BF16)
        ident = sbt("ident", [128, 128])
        identb = sbt("identb", [128, 128], BF16)
        sc1 = sbt("sc1", [128, 2, 8])
        shf = sbt("shf", [128, 2, 8])
        stats = sbt("stats", [128, 4, 12])
        mv = sbt("mv", [128, 4, 2])
        rstd = sbt("rstd", [128, 4, 2])
        gst = sbt("gst", [128, 2, 24])
        gmv = sbt("gmv", [128, 2, 8])
        grs = sbt("grs", [128, 2, 8])
        lgt = sbt("lgt", [128, 8])
        kdf = sbt("kdf", [128, 4])
        kdb = sbt("kdb", [128, 4])
        gC = sbt("gC", [128, 8])
        piota = sbt("piota", [128, 1])
        cw = sbt("cw", [128, 4, CK])
        cvec = sbt("cvec", [128, 3, 4])
        scT = sbt("scT", [128, 8, 2], BF16)
        condT = sbt("condT", [128, 8, 2])
        tiny = sbt("tiny", [128, 8])
        negpi = sbt("negpi", [128, 1])
        epst = sbt("epst", [128, 1])
        dmask = sbt("dmask", [128, 32])

        slot = [R(f"slot{i}") for i in range(33)]
        ov = R("ov")
        Rx = [[R(f"x{b}{s}") for s in range(2)] for b in range(2)]
        RhT = [R("hT0"), R("hT1")]
        Rzb = [R("zb0"), R("zb1")]
        Rgate, Rlng, Rlnb, Rsc = R("gate"), R("lng"), R("lnb"), R("sc")
        Rident = R("ident")
        Rst = [R(f"st{i}") for i in range(4)]
        Rgst = [R("gst0"), R("gst1")]
        Rdec = R("dec")
        Rcw = R("cw")
        Rrbt = [R("rbt0"), R("rbt1")]
        Rmods = R("mods")
        Rrope = R("rope")
        Rxs = [R(f"xs{t}") for t in range(NT)]
        Rrbd = [R(f"rbd{c}") for c in range(NCH)]
        Rmisc = R("misc")

        banks = [st.enter_context(nc.psum_tensor(f"bank{i}", [128, 512], F32)) for i in range(8)]
        Rb = [R(f"bank{i}") for i in range(8)]

        def aview(off, n, dt, pat=None, **kw):
            v = arena[:, off:off + n]
            if dt is not F32:
                v = v.bitcast(dt)
            if pat:
                v = v.rearrange(pat, **kw)
            return v

        def xview(off, n, dt, pat=None, **kw):
            v = aux[:, off:off + n]
            if dt is not F32:
                v = v.bitcast(dt)
            if pat:
                v = v.rearrange(pat, **kw)
            return v

        uT = xview(0, 2816, BF16, "p (j t) -> p j t", t=256)
        silt = xview(2816, 1024, F32, "p (a t) -> p a t", t=256)
        Qf = xview(0, 1024, F32, "p (h t) -> p h t", t=256)
        Qb = xview(1024, 1024, F32, "p (h t) -> p h t", t=256)
        upb = xview(2048, 752, BF16)
        upS = xview(2048, 752, BF16, "p (c r w) -> p c r w", r=4, w=94)
        upP = xview(2048, 572, BF16, "p (c w) -> p c w", w=286)
        Dblk = xview(2800, 1984, BF16, "p (c k j) -> p c k j", c=4, k=CK)
        o = [20 * 1024]

        def oa(n, dt, pat=None, **kw):
            v = aview(o[0], n, dt, pat, **kw)
            o[0] += n
            assert o[0] <= 33 * 1024
            return v

        Mcomb = oa(512, F32)
        qT = oa(512, BF16, "p (h t) -> p h t", t=256)
        qfT = oa(512, BF16, "p (h t) -> p h t", t=256)
        qbT = oa(512, BF16, "p (h t) -> p h t", t=256)
        kT = oa(512, BF16, "p (h t) -> p h t", t=256)
        ropeCS = oa(1024, F32, "p (b c t) -> p b c t", b=2, c=2)
        rt1 = oa(512, F32, "p (b t) -> p b t", b=2)
        rt2 = oa(512, F32, "p (b t) -> p b t", b=2)
        vbf = oa(512, BF16, "p (s c) -> p s c", s=2)
        sg = oa(1024, F32, "p (s c) -> p s c", s=2)
        sgm = oa(512, F32, "p (b t) -> p b t", b=2)
        acc = oa(1024, F32, "p (c t) -> p c t", c=4)
        xn = oa(1024, F32, "p (s c) -> p s c", s=2)
        mixT = oa(1024, BF16, "p (k t) -> p k t", t=256)
        Pm = oa(512, BF16, "p (b c) -> p b c", b=2)
        kdT = oa(512, BF16, "p (b c) -> p b c", b=2)
        Rf = oa(512, F32)
        Rbs = oa(512, F32)
        Rfb = oa(512, BF16, "p (b c) -> p b c", b=2)
        on = oa(512, F32)
        og = oa(256, BF16)
        og2 = oa(256, BF16)
        on2 = sbt("on2", [128, 512])
        onb = [on, on2[:]]
        ogb = [og, og2]
        cwraw = xn

        RqT, RkT, Rcs, Rrt, Rvbf, Rsg, Rup, Rsgm, Racc, Rxn, Rmix = (R("qT"), R("kT"), [R("cs0"), R("cs1")],
            [R("rt0"), R("rt1")], R("vbf"), R("sg"), R("up"), [R("sgm0"), R("sgm1")], [R(f"acc{i}") for i in range(4)],
            [R("xn0"), R("xn1")], R("mix"))
        RPm, RkdT, RRf, RRbs, RRfb, Ron, Rog = [R("P0"), R("P1")], [R("kd0"), R("kd1")], R("Rf"), R("Rbs"), [R("Rfb0"), R("Rfb1")], [R("on0"), R("on1")], [R("og0"), R("og1")]
        RuT, Rsil = R("uT"), [R("sil0"), R("sil1"), R("sil2"), R("sil3")]
        Rdb = R("dblk")

        V, A, G = nc.vector, nc.scalar, nc.gpsimd
        T = nc.tensor

        fw.op(pool, lambda: G.memset(ident[:], 0.0), writes=[Rident])
        fw.op(pool, lambda: G.affine_select(out=ident[:], in_=ident[:], compare_op=ALU.not_equal, fill=1.0, base=0,
                                            pattern=[[-1, 128]], channel_multiplier=1), reads=[Rident], writes=[Rident])
        fw.op(dve, lambda: V.tensor_copy(out=identb[:], in_=ident[:]), reads=[Rident], writes=[Rmisc])
        fw.op(pool, lambda: G.iota(piota[:], pattern=[[0, 1]], base=0, channel_multiplier=1,
                                   allow_small_or_imprecise_dtypes=True), writes=[Rmisc])
        fw.op(pool, lambda: G.memset(negpi[:], -math.pi), writes=[Rmisc])
        fw.op(pool, lambda: G.memset(epst[:], EPS), writes=[Rmisc])
        Rdm = R("dmask")

        fw.op(dve, lambda: V.tensor_tensor(out=dmask[:], in0=ident[:, 0:32], in1=ident[:, 32:64], op=ALU.add), reads=[Rident], writes=[Rdm])
        fw.op(dve, lambda: V.tensor_tensor(out=dmask[:], in0=dmask[:], in1=ident[:, 64:96], op=ALU.add), reads=[Rident, Rdm], writes=[Rdm])
        fw.op(dve, lambda: V.tensor_tensor(out=dmask[:], in0=dmask[:], in1=ident[:, 96:128], op=ALU.add), reads=[Rident, Rdm], writes=[Rdm])

        for c_ in range(2):
            fw.dma(sp, condT[:, :, c_], cond[c_, :].rearrange("(k p) -> p k", p=128), writes=[Rmisc], key=("c0", c_))
        fw.op(act, lambda: A.activation(out=scT[:], in_=condT[:], func=AF.Silu), reads=[Rmisc], writes=[Rsc])
        wab = arena[:, 0:6144].bitcast(BF16).rearrange("p (b k c) -> p b k c", b=3, k=8)
        badat = arena[0:2, 6144:6656]
        modrow = arena[0:2, 7168:9216].rearrange("p (m c) -> p m c", m=2)[:, :, 0:512]
        Rwab = [slot[0], slot[2], slot[4]]
        Rwab2 = [slot[1], slot[3], slot[5]]
        Rmr = [slot[7], slot[8]]
        Rbad = slot[6]
        k = 0
        for l in range(NL):
            for j in range(18):
                b3 = k % 3
                fw.dma(pool, wab[:, b3], w_ada[l, :, j * 512:(j + 1) * 512].rearrange("(k p) c -> p k c", p=128),
                       writes=[Rwab[b3], Rwab2[b3]], key=("wab", b3))
                fw.dma(sp, badat[:], b_ada[l:l + 1, j * 512:(j + 1) * 512].partition_broadcast(2), writes=[Rbad], key="bad")
                bk = Rb[k % 2]

                def mm(b3=b3, bank=banks[k % 2]):
                    for kc in range(8):
                        i = T.matmul(bank[0:2, :], lhsT=scT[:, kc, :], rhs=wab[:, b3, kc, :], start=(kc == 0), stop=(kc == 7))
                    return i
                fw.op(pe, mm, reads=[Rwab[b3], Rwab2[b3], Rsc], writes=[bk])
                m2 = k % 2
                fw.op(dve, lambda m2=m2, bank=banks[k % 2]: V.tensor_tensor(out=modrow[:, m2, :], in0=bank[0:2, :], in1=badat[:], op=ALU.add),
                      reads=[bk, Rbad], writes=[Rmr[m2]])
                fw.dma(sp, mods[l, :, j * 512:(j + 1) * 512], modrow[:, m2, :], reads=[Rmr[m2]], writes=[Rmods], key=("mr", m2))
                k += 1

        if ST > 0:
            invf = sbt("invf", [128, 1])
            MAGIC = 12582912.0

            def mkf():
                G.iota(tiny[0:64, 0:1], pattern=[[0, 1]], base=0, channel_multiplier=1, allow_small_or_imprecise_dtypes=True)
                return G.iota(tiny[64:128, 0:1], pattern=[[0, 1]], base=0, channel_multiplier=1, allow_small_or_imprecise_dtypes=True)
            fw.op(pool, mkf, reads=[ov, Rmisc], writes=[Rmisc])
            fw.op(dve, lambda: V.tensor_scalar(out=tiny[:, 2:3], in0=tiny[:, 0:1], scalar1=32.0, scalar2=32.0, op0=ALU.is_ge, op1=ALU.mult), reads=[ov, Rmisc], writes=[Rmisc])
            fw.op(dve, lambda: V.tensor_tensor(out=tiny[:, 0:1], in0=tiny[:, 0:1], in1=tiny[:, 2:3], op=ALU.subtract), reads=[ov, Rmisc], writes=[Rmisc])
            fw.op(act, lambda: A.activation(out=invf[:], in_=tiny[:, 0:1], func=AF.Exp, scale=-math.log(10000.0) / 32.0), reads=[ov, Rmisc], writes=[Rmisc])
            fw.op(dve, lambda: V.tensor_single_scalar(out=invf[:], in_=invf[:], scalar=1.0 / TWO_PI, op=ALU.mult), reads=[ov, Rmisc], writes=[Rmisc])
            posv = sg
            pa = posv[:, 0, 0:256]
            pr_ = posv[:, 0, 256:512]
            ps_ = posv[:, 1, 0:256]
            pc_ = posv[:, 1, 256:512]
            SC = 6.283184
            for t in range(ST // 256):
                def mkpos(t=t):
                    G.iota(pa[0:64, :], pattern=[[1, 4], [0, 64]], base=4 * t, channel_multiplier=0, allow_small_or_imprecise_dtypes=True)
                    return G.iota(pa[64:128, :], pattern=[[0, 4], [1, 64]], base=0, channel_multiplier=0, allow_small_or_imprecise_dtypes=True)
                fw.op(pool, mkpos, reads=[ov], writes=[Rsg])
                fw.op(dve, lambda: V.tensor_scalar(out=pa, in0=pa, scalar1=invf[:, 0:1], scalar2=None, op0=ALU.mult), reads=[ov, Rsg, Rmisc], writes=[Rsg])
                for (dstv, off, rr) in ((ps_, 0.0, Rxn[0]), (pc_, 0.25, Rxn[1])):
                    fw.op(dve, lambda dstv=dstv, off=off: V.tensor_single_scalar(out=dstv, in_=pa, scalar=off, op=ALU.add), reads=[ov, Rsg], writes=[rr])
                    fw.op(dve, lambda dstv=dstv: V.tensor_single_scalar(out=pr_, in_=dstv, scalar=MAGIC, op=ALU.add), reads=[ov, rr], writes=[Rsgm[0]])
                    fw.op(dve, lambda: V.tensor_single_scalar(out=pr_, in_=pr_, scalar=MAGIC, op=ALU.subtract), reads=[ov, Rsgm[0]], writes=[Rsgm[0]])
                    fw.op(dve, lambda dstv=dstv: V.tensor_tensor(out=dstv, in0=dstv, in1=pr_, op=ALU.subtract), reads=[ov, rr, Rsgm[0]], writes=[rr])
                    fw.op(act, lambda dstv=dstv: A.activation(out=dstv, in_=dstv, func=AF.Sin, scale=SC), reads=[ov, rr], writes=[rr])
                fw.dma(sp, rope[1, :, t * 256:(t + 1) * 256], ps_, reads=[ov, Rxn[0]], writes=[Rrope], key="rp0")
                fw.dma(sp, rope[0, :, t * 256:(t + 1) * 256], pc_, reads=[ov, Rxn[1]], writes=[Rrope], key="rp1")

        def stage_consts(l, sl, half):
            for c in range(2):
                fw.dma(sp, gate[:, c, :], mods[l, c:c + 1, (3 * sl + 2) * D:(3 * sl + 3) * D].partition_broadcast(128),
                       reads=[Rmods], writes=[Rgate], key=("gate", c))
                fw.dma(sp, sc1[:, c, :], mods[l, c, (3 * sl + 1) * D:(3 * sl + 2) * D].rearrange("(k p) -> p k", p=128),
                       reads=[Rmods], writes=[Rsc], key=("sc1", c))
                fw.dma(sp, shf[:, c, :], mods[l, c, (3 * sl) * D:(3 * sl + 1) * D].rearrange("(k p) -> p k", p=128),
                       reads=[Rmods], writes=[Rsc], key=("shf", c))
            fw.dma(sp, lng[:], ln_g[l, sl:sl + 1, :].partition_broadcast(128), writes=[Rlng], key="lng")
            fw.dma(sp, lnb[:], ln_b[l, sl:sl + 1, :].partition_broadcast(128), writes=[Rlnb], key="lnb")
            fw.op(dve, lambda: V.tensor_single_scalar(out=sc1[:], in_=sc1[:], scalar=1.0, op=ALU.add), reads=[Rsc], writes=[Rsc])
            if half:
                fw.op(pool, lambda: G.tensor_single_scalar(out=gate[:], in_=gate[:], scalar=0.5, op=ALU.mult), reads=[Rgate], writes=[Rgate])

        def load_x(src, tl, b):
            for s in range(2):
                r0 = tl["tok0"] + s * 128
                fw.dma(sp, xbuf[:, b, s, :], src[r0:r0 + 128, :], reads=[Rxs[tl["idx"]]], writes=[Rx[b][s]], key=("x", b, s))

        def prologue(tl, b):
            c = tl["cond"]
            for pr in range(4):
                bk = pr % 2

                def tr(pr=pr, bk=bk):
                    for cc in range(2):
                        for s in range(2):
                            fc = pr * 2 + cc
                            i = T.transpose(out=banks[bk][:, cc * 256 + s * 128: cc * 256 + (s + 1) * 128],
                                            in_=xbuf[:, b, s, fc * 128:(fc + 1) * 128], identity=ident[:])
                    return i
                fw.op(pe, tr, reads=[Rx[b][0], Rx[b][1], Rident], writes=[Rb[bk]])

                def ev(pr=pr, bk=bk):
                    for cc in range(2):
                        fc = pr * 2 + cc
                        i = A.activation(out=hT[:, b, fc, :], in_=banks[bk][:, cc * 256:(cc + 1) * 256], func=AF.Identity,
                                         scale=sc1[:, c, fc:fc + 1], bias=shf[:, c, fc:fc + 1])
                    return i
                fw.op(act, ev, reads=[Rb[bk], Rsc], writes=[RhT[b]])

        def epilogue(tl, b, s, ybanks, dst, zi):
            c = tl["cond"]
            z = zb[:, zi, :]
            rz = Rzb[zi]
            for hf in range(2):
                fw.op(dve, lambda hf=hf: V.tensor_tensor(out=z[:, hf * 512:(hf + 1) * 512], in0=banks[ybanks[hf]][:, :],
                                                         in1=gate[:, c, hf * 512:(hf + 1) * 512], op=ALU.mult),
                      reads=[Rb[ybanks[hf]], Rgate], writes=[rz])
            fw.op(dve, lambda: V.scalar_tensor_tensor(out=z, in0=xbuf[:, b, s, :], scalar=ALPHA, in1=z, op0=ALU.mult, op1=ALU.add),
                  reads=[Rx[b][s], rz], writes=[rz])
            ln_rows(z, rz, zi, D)
            fw.op(act, lambda: A.activation(out=z, in_=z, func=AF.Identity, scale=rstd[:, zi, 0:1], bias=rstd[:, zi, 1:2]),
                  reads=[rz, Rst[zi]], writes=[rz])
            fw.op(pool, lambda: G.tensor_tensor(out=z, in0=z, in1=lng[:], op=ALU.mult), reads=[rz, Rlng], writes=[rz])
            fw.op(pool, lambda: G.tensor_tensor(out=z, in0=z, in1=lnb[:], op=ALU.add), reads=[rz, Rlnb], writes=[rz])
            r0 = tl["tok0"] + s * 128
            fw.dma(sp, dst[r0:r0 + 128, :], z, reads=[rz], writes=[Rxs[tl["idx"]]], key=("xo", zi))

        def ln_rows(src, rsrc, si, n):
            nchunk = n // 512

            def bs():
                for q in range(nchunk):
                    i = V.bn_stats(out=stats[:, si, q * 6:(q + 1) * 6], in_=src[:, q * 512:(q + 1) * 512])
                return i
            fw.op(dve, bs, reads=[rsrc], writes=[Rst[si]])
            fw.op(dve, lambda: V.bn_aggr(out=mv[:, si, :], in_=stats[:, si, 0:6 * nchunk]), reads=[Rst[si]], writes=[Rst[si]])
            fw.op(act, lambda: A.activation(out=rstd[:, si, 0:1], in_=mv[:, si, 1:2], func=AF.Sqrt, bias=epst[:, 0:1]), reads=[Rst[si], Rmisc], writes=[Rst[si]])
            fw.op(dve, lambda: V.reciprocal(out=rstd[:, si, 0:1], in_=rstd[:, si, 0:1]), reads=[Rst[si]], writes=[Rst[si]])
            fw.op(dve, lambda: V.scalar_tensor_tensor(out=rstd[:, si, 1:2], in0=mv[:, si, 0:1], scalar=-1.0, in1=rstd[:, si, 0:1], op0=ALU.mult, op1=ALU.mult),
                  reads=[Rst[si]], writes=[Rst[si]])

        claim_t = tiny

        def claim(extra=()):
            fw.op(pool, lambda: G.memset(claim_t[:, 4:5], 0.0), writes=[ov] + list(extra))

        def w1_dma(l, i, j):
            for part in range(2):
                dstv = aview(j * 2048, 2048, BF16, "p (k c) -> p k c", c=512)[:, :, part * 256:(part + 1) * 256]
                c0 = part * DFF + j * 256
                fw.dma(pool, dstv, ffn_w1[l, i, :, c0:c0 + 256].rearrange("(k p) c -> p k c", p=128),
                       reads=([ov] if 2 * j + 1 >= 20 else []), writes=[slot[2 * j], slot[2 * j + 1]], key=("w", 2 * j, part))

        def w2_dma(l, i, j):
            dstv = aview(22 * 1024 + j * 1024, 1024, BF16, "p (e c) -> p e c", e=2)
            fw.dma(pool, dstv, ffn_w2[l, i, j * 256:(j + 1) * 256, :].rearrange("(e p) c -> p e c", p=128),
                   reads=[ov], writes=[slot[22 + j]], key=("w", 22 + j, 0))

        def mixer_dma(l, m):
            if m < 6:
                fw.dma(pool, WIN(m), w_in[l, :, m * 512:(m + 1) * 512].rearrange("(k p) c -> p k c", p=128),
                       writes=[slot[2 * m], slot[2 * m + 1]], key=("w", 2 * m, 0))
            elif m in (8, 9):
                hf = m - 8
                fw.dma(pool, WIN(8 + hf), w_out[l, :, hf * 512:(hf + 1) * 512].rearrange("(k p) c -> p k c", p=128),
                       writes=[slot[16 + 2 * hf], slot[17 + 2 * hf]], key=("w", 16 + 2 * hf, 0))

        def stage_F(l, i, src, dst, have_w=False, nxt=None):
            sl = 0 if i == 0 else 2
            if not have_w:
                claim()
                for j in range(11):
                    w1_dma(l, i, j)
                for j in range(11):
                    w2_dma(l, i, j)
            stage_consts(l, sl, True)
            load_x(src, tiles[0], 0)
            zi = 0
            prologue(tiles[0], 0)
            for ti, tl in enumerate(tiles):
                b = ti % 2
                if ti + 1 < NT:
                    load_x(src, tiles[ti + 1], 1 - b)
                for hc in range(NHC):
                    j, e = hc // 2, hc % 2
                    w1v = aview(j * 2048, 2048, BF16, "p (k c) -> p k c", c=512)
                    bk = hc % 4

                    def mm(w1v=w1v, e=e, bk=bk, b=b):
                        for part in range(2):
                            for kc in range(8):
                                i_ = T.matmul(banks[bk][:, part * 256:(part + 1) * 256],
                                              lhsT=w1v[:, kc, part * 256 + e * 128: part * 256 + (e + 1) * 128],
                                              rhs=hT[:, b, kc, :], start=(kc == 0), stop=(kc == 7))
                        return i_
                    fw.op(pe, mm, reads=[slot[2 * j], slot[2 * j + 1], RhT[b]], writes=[Rb[bk]])
                    sb_ = hc % 4
                    fw.op(act, lambda bk=bk, sb_=sb_: A.activation(out=silt[:, sb_, :], in_=banks[bk][:, 0:256], func=AF.Silu),
                          reads=[Rb[bk], ov], writes=[Rsil[sb_]])
                    fw.op(dve, lambda bk=bk, sb_=sb_, hc=hc: V.tensor_tensor(out=uT[:, hc, :], in0=banks[bk][:, 256:512], in1=silt[:, sb_, :], op=ALU.mult),
                          reads=[Rb[bk], Rsil[sb_], ov], writes=[RuT])
                    if ti == NT - 1 and nxt is not None and e == 1:
                        if nxt[0] == "F":
                            w1_dma(nxt[1], nxt[2], j)
                        else:
                            mixer_dma(nxt[1], j)
                if ti + 1 < NT:
                    prologue(tiles[ti + 1], 1 - b)
                for s in range(2):
                    for hf in range(2):
                        bk = 4 + s * 2 + hf

                        def mm2(s=s, hf=hf, bk=bk):
                            for hc in range(NHC):
                                w2v = aview(22 * 1024 + (hc // 2) * 1024, 1024, BF16, "p (e c) -> p e c", e=2)
                                i_ = T.matmul(banks[bk][:, :], lhsT=uT[:, hc, s * 128:(s + 1) * 128],
                                              rhs=w2v[:, hc % 2, hf * 512:(hf + 1) * 512], start=(hc == 0), stop=(hc == NHC - 1))
                            return i_
                        fw.op(pe, mm2, reads=[RuT, ov] + [slot[22 + q] for q in range(11)], writes=[Rb[bk]])
                if ti == NT - 1 and nxt is not None and nxt[0] == "F":
                    for j in range(11):
                        w2_dma(nxt[1], nxt[2], j)
                for s in range(2):
                    epilogue(tl, b, s, (4 + s * 2, 5 + s * 2), dst, zi)
                    zi = 1 - zi

        def WIN(m):
            return aview(m * 2048, 2048, BF16, "p (k c) -> p k c", c=512)

        def load_mixer(l, have_w=False):
            if not have_w:
                for m in (0, 1, 2, 3, 4, 5, 8, 9):
                    mixer_dma(l, m)
            for m in (2, 3):
                srcv = aview(m * 2048, 2048, BF16, "p (a two f) -> p a two f", two=2, f=32)
                dstv = aview((m + 4) * 2048, 2048, BF16, "p (a two f) -> p a two f", two=2, f=32)
                fw.op(act, lambda srcv=srcv, dstv=dstv: A.mul(out=dstv[:, :, 0, :], in_=srcv[:, :, 1, :], mul=-1.0),
                      reads=[slot[2 * m], slot[2 * m + 1]], writes=[slot[2 * m + 8], slot[2 * m + 9]])
                fw.op(act, lambda srcv=srcv, dstv=dstv: A.copy(out=dstv[:, :, 1, :], in_=srcv[:, :, 0, :]),
                      reads=[slot[2 * m], slot[2 * m + 1]], writes=[slot[2 * m + 8], slot[2 * m + 9]])

        def layer_consts(l):
            fw.dma(sp, lgt[:], dlogit[l:l + 1, :].partition_broadcast(128), writes=[Rdec], key="lgt")
            fw.op(act, lambda: A.activation(out=lgt[:], in_=lgt[:], func=AF.Exp, scale=-1.0), reads=[Rdec], writes=[Rdec])
            fw.op(act, lambda: A.activation(out=lgt[:], in_=lgt[:], func=AF.Ln, bias=1.0), reads=[Rdec], writes=[Rdec])
            fw.op(act, lambda: A.mul(out=lgt[:], in_=lgt[:], mul=-1.0), reads=[Rdec], writes=[Rdec])
            fw.op(act, lambda: A.activation(out=gC[:], in_=lgt[:], func=AF.Exp, scale=128.0), reads=[Rdec], writes=[Rdec])
            fw.op(dve, lambda: V.tensor_scalar(out=tiny[:, 1:2], in0=piota[:], scalar1=-1.0, scalar2=127.0, op0=ALU.mult, op1=ALU.add),
                  reads=[Rmisc], writes=[Rmisc])
            fw.op(dve, lambda: V.tensor_scalar(out=kdf[:], in0=lgt[:, 0:4], scalar1=tiny[:, 1:2], scalar2=None, op0=ALU.mult), reads=[Rdec, Rmisc], writes=[Rdec])
            fw.op(dve, lambda: V.tensor_scalar(out=kdb[:], in0=lgt[:, 4:8], scalar1=piota[:, 0:1], scalar2=None, op0=ALU.mult), reads=[Rdec, Rmisc], writes=[Rdec])
            fw.op(act, lambda: A.activation(out=kdf[:], in_=kdf[:], func=AF.Exp), reads=[Rdec], writes=[Rdec])
            fw.op(act, lambda: A.activation(out=kdb[:], in_=kdb[:], func=AF.Exp), reads=[Rdec], writes=[Rdec])
            dist = xn[:, 0, 0:128]
            ci = xn[:, 0, 128:256]
            t1 = xn[:, 0, 256:384]
            t2 = xn[:, 0, 384:512]
            mk1 = xn[:, 1, 0:128]
            fw.op(pool, lambda: G.iota(dist, pattern=[[1, 128]], base=0, channel_multiplier=-1, allow_small_or_imprecise_dtypes=True),
                  reads=[ov], writes=[Rxn[0]])
            fw.op(pool, lambda: G.iota(ci, pattern=[[1, 128]], base=0, channel_multiplier=0, allow_small_or_imprecise_dtypes=True),
                  reads=[ov], writes=[Rxn[0]])
            for h in range(4):
                fw.op(dve, lambda: V.tensor_single_scalar(out=t1, in_=dist, scalar=0.0, op=ALU.max), reads=[Rxn[0], ov], writes=[Rxn[0]])
                fw.op(act, lambda h=h: A.activation(out=t1, in_=t1, func=AF.Exp, scale=lgt[:, h:h + 1]), reads=[Rxn[0], Rdec, ov], writes=[Rxn[0]])
                fw.op(dve, lambda: V.tensor_single_scalar(out=mk1, in_=dist, scalar=0.0, op=ALU.is_ge), reads=[Rxn[0], ov], writes=[Rxn[1]])
                fw.op(dve, lambda: V.tensor_tensor(out=t1, in0=t1, in1=mk1, op=ALU.mult), reads=[Rxn[0], Rxn[1], ov], writes=[Rxn[0]])
                fw.op(dve, lambda: V.tensor_scalar(out=t2, in0=dist, scalar1=-1.0, scalar2=0.0, op0=ALU.mult, op1=ALU.max), reads=[Rxn[0], ov], writes=[Rxn[0]])
                fw.op(act, lambda h=h: A.activation(out=t2, in_=t2, func=AF.Exp, scale=lgt[:, 4 + h:5 + h]), reads=[Rxn[0], Rdec, ov], writes=[Rxn[0]])
                fw.op(dve, lambda: V.tensor_single_scalar(out=mk1, in_=dist, scalar=0.0, op=ALU.is_le), reads=[Rxn[0], ov], writes=[Rxn[1]])
                fw.op(dve, lambda: V.tensor_tensor(out=t2, in0=t2, in1=mk1, op=ALU.mult), reads=[Rxn[0], Rxn[1], ov], writes=[Rxn[0]])
                fw.op(dve, lambda h=h: V.scalar_tensor_tensor(out=Mcomb[:, h * 128:(h + 1) * 128], in0=t1, scalar=1.0, in1=t2, op0=ALU.mult, op1=ALU.add),
                      reads=[Rxn[0], ov], writes=[Rdec])
                fw.op(dve, lambda: V.tensor_single_scalar(out=t1, in_=ci, scalar=1.0, op=ALU.add), reads=[Rxn[0], ov], writes=[Rxn[0]])
                fw.op(act, lambda h=h: A.activation(out=Qf[:, h, 0:128], in_=t1, func=AF.Exp, scale=lgt[:, h:h + 1]), reads=[Rxn[0], Rdec, ov], writes=[Rdec])
                fw.op(dve, lambda: V.tensor_scalar(out=t2, in0=ci, scalar1=-1.0, scalar2=128.0, op0=ALU.mult, op1=ALU.add), reads=[Rxn[0], ov], writes=[Rxn[0]])
                fw.op(act, lambda h=h: A.activation(out=Qb[:, h, 0:128], in_=t2, func=AF.Exp, scale=lgt[:, 4 + h:5 + h]), reads=[Rxn[0], Rdec, ov], writes=[Rdec])
            fw.op(dve, lambda: V.tensor_single_scalar(out=Mcomb, in_=Mcomb, scalar=QSCALE, op=ALU.mult), reads=[Rdec, ov], writes=[Rdec])
            fw.op(dve, lambda: V.tensor_single_scalar(out=Qf[:, :, 0:128], in_=Qf[:, :, 0:128], scalar=QSCALE, op=ALU.mult), reads=[Rdec, ov], writes=[Rdec])
            fw.op(dve, lambda: V.tensor_single_scalar(out=Qb[:, :, 0:128], in_=Qb[:, :, 0:128], scalar=QSCALE, op=ALU.mult), reads=[Rdec, ov], writes=[Rdec])
            fw.op(dve, lambda: V.tensor_copy(out=Qf[:, :, 128:256], in_=Qf[:, :, 0:128]), reads=[Rdec, ov], writes=[Rdec])
            fw.op(dve, lambda: V.tensor_copy(out=Qb[:, :, 128:256], in_=Qb[:, :, 0:128]), reads=[Rdec, ov], writes=[Rdec])
            fw.dma(sp, cwraw[0:CK, 1, :], conv_w[l, :, :], reads=[ov], writes=[Rxn[1]], key="cwr")

            def trw():
                for cc in range(4):
                    i_ = T.transpose(out=banks[0][:, cc * 32:cc * 32 + CK], in_=cwraw[0:CK, 1, cc * 128:(cc + 1) * 128], identity=ident[0:CK, 0:CK])
                return i_
            fw.op(pe, trw, reads=[Rxn[1], Rident, ov], writes=[Rb[0]])
            fw.op(dve, lambda: V.tensor_copy(out=cw[:], in_=banks[0][:, 0:128].rearrange("p (c k) -> p c k", k=32)[:, :, 0:CK]), reads=[Rb[0]], writes=[Rcw])
            for vi, vec in enumerate((conv_b, conv_ln_g, conv_ln_b)):
                fw.dma(sp, cvec[:, vi, :], vec[l, :].rearrange("(c p) -> p c", p=128), writes=[Rcw], key=("cv", vi))
            fw.op(pool, lambda: G.memset(upb, 0.0), reads=[ov], writes=[Rup])

            def mkd():
                for cc in range(4):
                    for kk in range(CK):
                        i_ = V.tensor_scalar(out=Dblk[:, cc, kk, :], in0=dmask[:], scalar1=cw[:, cc, kk:kk + 1], scalar2=None, op0=ALU.mult)
                return i_
            fw.op(dve, mkd, reads=[Rcw, Rdm, ov], writes=[Rdb])

        def proj_fm(tl, m, outT, Rout, versions=None, bk0=2):
            isS = tl["kind"] == "S"
            b = tl["idx"] % 2
            for h in range(4):
                bk = bk0 + (h % 2)
                nparts = 2 if isS else 1

                def mm(h=h, bk=bk, nparts=nparts):
                    for part in range(nparts):
                        wv = WIN(m if part == 0 else m + 4)
                        for kc in range(8):
                            i_ = T.matmul(banks[bk][:, part * 256:(part + 1) * 256], lhsT=wv[:, kc, h * 128:(h + 1) * 128], rhs=hT[:, b, kc, :],
                                          start=(kc == 0), stop=(kc == 7))
                    return i_
                rd = [slot[2 * m], slot[2 * m + 1], RhT[b]] + ([slot[2 * m + 8], slot[2 * m + 9]] if isS else [])
                fw.op(pe, mm, reads=rd, writes=[Rb[bk]])
                r = h % 2
                if isS:
                    fw.op(dve, lambda bk=bk, r=r: V.tensor_tensor(out=rt1[:, r, :], in0=banks[bk][:, 0:256], in1=ropeCS[:, b, 0, :], op=ALU.mult),
                          reads=[Rb[bk], Rcs[b], ov], writes=[Rrt[r]])
                    fw.op(dve, lambda bk=bk, r=r: V.tensor_tensor(out=rt2[:, r, :], in0=banks[bk][:, 256:512], in1=ropeCS[:, b, 1, :], op=ALU.mult),
                          reads=[Rb[bk], Rcs[b], ov], writes=[Rrt[r]])
                    fw.op(pool, lambda r=r: G.tensor_tensor(out=rt1[:, r, :], in0=rt1[:, r, :], in1=rt2[:, r, :], op=ALU.add),
                          reads=[Rrt[r], ov], writes=[Rrt[r]])
                    srcv, rsrc = rt1[:, r, :], Rrt[r]
                else:
                    srcv, rsrc = banks[bk][:, 0:256], Rb[bk]
                fw.op(act, lambda h=h, srcv=srcv: A.copy(out=outT[:, h, :], in_=srcv), reads=[rsrc, ov], writes=[Rout])
                if versions:
                    for (Qt, dstT) in versions:
                        if isS:
                            fw.op(pool, lambda h=h, srcv=srcv, Qt=Qt, dstT=dstT: G.tensor_tensor(out=dstT[:, h, :], in0=srcv, in1=Qt[:, h, :], op=ALU.mult),
                                  reads=[rsrc, Rdec, ov], writes=[Rout])
                        else:
                            fw.op(dve, lambda h=h, srcv=srcv, Qt=Qt, dstT=dstT: V.tensor_tensor(out=dstT[:, h, :], in0=srcv, in1=Qt[:, h, :], op=ALU.mult),
                                  reads=[rsrc, Rdec, ov], writes=[Rout])

        def proj_tm(s, m, bk, b):
            def mm():
                for kc in range(8):
                    i_ = T.matmul(banks[bk][:, :], lhsT=hT[:, b, kc, s * 128:(s + 1) * 128], rhs=WIN(m)[:, kc, :], start=(kc == 0), stop=(kc == 7))
                return i_
            fw.op(pe, mm, reads=[slot[2 * m], slot[2 * m + 1], RhT[b]], writes=[Rb[bk]])

        def k_tm(s, kd, bk, ki):
            pb = banks[bk][:, 0:256].bitcast(BF16)

            def tr():
                for h in range(4):
                    i_ = T.transpose(out=pb[:, h * 128:(h + 1) * 128], in_=kT[:, h, s * 128:(s + 1) * 128], identity=identb[:])
                return i_
            fw.op(pe, tr, reads=[RkT, Rmisc, ov], writes=[Rb[bk]])

            def ev():
                for h in range(4):
                    i_ = A.activation(out=kdT[:, ki, h * 128:(h + 1) * 128], in_=pb[:, h * 128:(h + 1) * 128], func=AF.Copy, scale=kd[:, h:h + 1])
                return i_
            fw.op(act, ev, reads=[Rb[bk], Rdec, ov], writes=[RkdT[ki]])

        def kv_mm(s, bk, ki):
            def mm():
                for h in range(4):
                    i_ = T.matmul(banks[bk][:, h * 128:(h + 1) * 128], lhsT=kdT[:, ki, h * 128:(h + 1) * 128], rhs=vbf[:, s, h * 128:(h + 1) * 128],
                                  start=True, stop=True)
                return i_
            fw.op(pe, mm, reads=[RkdT[ki], Rvbf, ov], writes=[Rb[bk]])

        def state_update(Rt, RRt, bk, goff):
            def up():
                for h in range(4):
                    i_ = V.scalar_tensor_tensor(out=Rt[:, h * 128:(h + 1) * 128], in0=Rt[:, h * 128:(h + 1) * 128], scalar=gC[:, goff + h:goff + h + 1],
                                                in1=banks[bk][:, h * 128:(h + 1) * 128], op0=ALU.mult, op1=ALU.add)
                return i_
            fw.op(dve, up, reads=[Rb[bk], Rdec, ov, RRt], writes=[RRt])

        def load_cs(tl):
            if tl["kind"] == "S":
                b = tl["idx"] % 2
                for c in range(2):
                    fw.dma(sp, ropeCS[:, b, c, :], rope[c, :, tl["tok0"]:tl["tok0"] + 256], reads=[Rrope, ov], writes=[Rcs[b]], key=("cs", b, c))

        def stage_KB(l, src, have_w=False):
            claim(extra=[slot[i] for i in range(20, 33)])
            load_mixer(l, have_w)
            layer_consts(l)
            stage_consts(l, 1, False)
            order = list(reversed(tiles))

            def kb_pk(tl):
                proj_fm(tl, 3, kT, RkT)

            def kb_pv(tl):
                b = tl["idx"] % 2
                for s in range(2):
                    proj_tm(s, 4, 4 + s, b)
                    fw.op(act, lambda s=s: A.copy(out=vbf[:, s, :], in_=banks[4 + s][:, :]), reads=[Rb[4 + s], ov], writes=[Rvbf])

            def kb_ld(tl):
                load_x(src, tl, tl["idx"] % 2)
                load_cs(tl)

            kb_ld(order[0])
            if NT > 1:
                kb_ld(order[1])
            prologue(order[0], order[0]["idx"] % 2)
            kb_pk(order[0])
            kb_pv(order[0])
            if NT > 1:
                prologue(order[1], order[1]["idx"] % 2)
            for oi, tl in enumerate(order):
                b = tl["idx"] % 2
                isS = tl["kind"] == "S"
                if oi + 2 < NT:
                    kb_ld(order[oi + 2])
                k_tm(1, kdb, 7, 1)
                k_tm(0, kdb, 6, 0)
                if oi + 1 < NT:
                    kb_pk(order[oi + 1])
                last_tile_of_seq = (not isS) or (tl["idx"] == ST // 256 - 1)
                if last_tile_of_seq:
                    if isS:
                        fw.dma(sp, Rbs.rearrange("p (h v) -> p h v", h=4), stb[l].rearrange("h d v -> d h v"), reads=[ov], writes=[RRbs], key="rbs")
                    else:
                        fw.op(pool, lambda: G.memset(Rbs, 0.0), reads=[ov], writes=[RRbs])
                for s in (1, 0):
                    ch = tl["tok0"] // 128 + s
                    kv_mm(s, 6 + s, s)
                    ri = ch % 2
                    fw.op(act, lambda ri=ri: A.copy(out=rbt[:, ri, 0, :], in_=Rbs), reads=[RRbs, ov], writes=[Rrbt[ri]])
                    fw.dma(sp, rbd[ch], rbt[:, ri, 0, :], reads=[Rrbt[ri]], writes=[Rrbd[ch]], key=("rbo", ri))
                    state_update(Rbs, RRbs, 6 + s, 4)
                if not isS:
                    fw.dma(sp, nsb[tl["seq"], l].rearrange("h d v -> d h v"), Rbs.rearrange("p (h v) -> p h v", h=4), reads=[RRbs, ov], key="nsb")
                if oi + 1 < NT:
                    kb_pv(order[oi + 1])
                if oi + 2 < NT:
                    prologue(order[oi + 2], order[oi + 2]["idx"] % 2)

        def stage_M(l, src, dst):
            stage_consts(l, 1, False)
            load_x(src, tiles[0], 0)
            load_cs(tiles[0])
            prologue(tiles[0], 0)
            zi = [0]

            def conv_in(tl, b):
                isS = tl["kind"] == "S"
                for cc in range(4):
                    bk = 6 + cc % 2

                    def mmc(cc=cc, bk=bk, b=b):
                        for part in range(2):
                            for kc in range(8):
                                i_ = T.matmul(banks[bk][:, part * 256:(part + 1) * 256], lhsT=WIN(part)[:, kc, cc * 128:(cc + 1) * 128], rhs=hT[:, b, kc, :],
                                              start=(kc == 0), stop=(kc == 7))
                        return i_
                    fw.op(pe, mmc, reads=[slot[0], slot[1], slot[2], slot[3], RhT[b]], writes=[Rb[bk]])
                    r = cc % 2
                    fw.op(act, lambda bk=bk, r=r: A.activation(out=sgm[:, r, :], in_=banks[bk][:, 256:512], func=AF.Sigmoid), reads=[Rb[bk], ov], writes=[Rsgm[r]])
                    if isS:
                        outv = upS[:, cc, :, 15:79]
                        in0 = banks[bk][:, 0:256].rearrange("p (r w) -> p r w", w=64)
                        in1 = sgm[:, r, :].rearrange("p (r w) -> p r w", w=64)
                    else:
                        outv = upP[:, cc, 15:271]
                        in0 = banks[bk][:, 0:256]
                        in1 = sgm[:, r, :]
                    fw.op(dve, lambda outv=outv, in0=in0, in1=in1: V.tensor_tensor(out=outv, in0=in0, in1=in1, op=ALU.mult),
                          reads=[Rb[bk], Rsgm[r], ov], writes=[Rup])

            CBK = [0, 1, 6, 7]

            def conv_mm(tl, ccs):
                isS = tl["kind"] == "S"

                def mmcv():
                    for cc in ccs:
                        for kk in range(CK):
                            for g in range(4):
                                ps = slice(32 * g, 32 * g + 32)
                                bkc = CBK[(g + cc) % 4]
                                if isS:
                                    ov_ = banks[bkc][ps, 0:256].rearrange("p (r w) -> p r w", w=64)
                                    rh = upS[ps, cc, :, kk:kk + 64]
                                else:
                                    ov_ = banks[bkc][ps, 0:256]
                                    rh = upP[ps, cc, kk:kk + 256]
                                i_ = T.matmul(ov_, lhsT=Dblk[ps, cc, kk, :], rhs=rh, start=(kk == 0), stop=(kk == CK - 1),
                                              tile_position=(32 * g, 32 * g))
                    return i_
                fw.op(pe, mmcv, reads=[Rup, Rdb, ov], writes=[Rb[q_] for q_ in CBK])

            def conv_evac():
                for cc in range(4):
                    def evc(cc=cc):
                        for g in range(4):
                            ps = slice(32 * g, 32 * g + 32)
                            i_ = V.tensor_scalar(out=acc[ps, cc, :], in0=banks[CBK[(g + cc) % 4]][ps, 0:256], scalar1=cvec[ps, 0, cc:cc + 1], scalar2=None, op0=ALU.add)
                        return i_
                    fw.op(dve, evc, reads=[Rb[q_] for q_ in CBK] + [Rcw, ov], writes=[Racc[cc]])

            def conv_ln_a():
                for s in range(2):
                    bk = 4 + s

                    def trc(s=s, bk=bk):
                        for cc in range(4):
                            i_ = T.transpose(out=banks[bk][:, cc * 128:(cc + 1) * 128], in_=acc[:, cc, s * 128:(s + 1) * 128], identity=ident[:])
                        return i_
                    fw.op(pe, trc, reads=Racc + [Rident, ov], writes=[Rb[bk]])
                for s in range(2):
                    bk = 4 + s
                    ln_rows(banks[bk], Rb[bk], 2 + s, 512)
                    fw.op(act, lambda s=s, bk=bk: A.activation(out=xn[:, s, :], in_=banks[bk][:, :], func=AF.Identity, scale=rstd[:, 2 + s, 0:1], bias=rstd[:, 2 + s, 1:2]),
                          reads=[Rb[bk], Rst[2 + s], ov], writes=[Rxn[s]])

            def conv_ln_b():
                for pr in range(2):
                    bk = 4 + pr

                    def trb(pr=pr, bk=bk):
                        for c2 in range(2):
                            for s in range(2):
                                cc = pr * 2 + c2
                                i_ = T.transpose(out=banks[bk][:, c2 * 256 + s * 128:c2 * 256 + (s + 1) * 128], in_=xn[:, s, cc * 128:(cc + 1) * 128], identity=ident[:])
                        return i_
                    fw.op(pe, trb, reads=[Rxn[0], Rxn[1], Rident, ov], writes=[Rb[bk]])

                    def evb(pr=pr, bk=bk):
                        for c2 in range(2):
                            cc = pr * 2 + c2
                            i_ = A.activation(out=mixT[:, cc, :], in_=banks[bk][:, c2 * 256:(c2 + 1) * 256], func=AF.Silu, scale=cvec[:, 1, cc:cc + 1], bias=cvec[:, 2, cc:cc + 1])
                        return i_
                    fw.op(act, evb, reads=[Rb[bk], Rcw, ov], writes=[Rmix])

            def ret_rfb(tl, s):
                fi = (tl["tok0"] // 128 + s) % 2
                fw.op(act, lambda fi=fi: A.copy(out=Rfb[:, fi, :], in_=Rf), reads=[RRf, ov], writes=[RRfb[fi]])

            def ret_scores(s, bk):
                def mms(s=s, bk=bk):
                    for h in range(4):
                        i_ = T.matmul(banks[bk][:, h * 128:(h + 1) * 128], lhsT=kT[:, h, s * 128:(s + 1) * 128], rhs=qT[:, h, s * 128:(s + 1) * 128], start=True, stop=True)
                    return i_
                fw.op(pe, mms, reads=[RkT, RqT, ov], writes=[Rb[bk]])
                fw.op(dve, lambda s=s, bk=bk: V.tensor_tensor(out=Pm[:, s, :], in0=banks[bk][:, :], in1=Mcomb, op=ALU.mult), reads=[Rb[bk], Rdec, ov], writes=[RPm[s]])

            def ret_o(tl, b, s, bk):
                fi = (tl["tok0"] // 128 + s) % 2

                def mmo(s=s, fi=fi, b=b, bk=bk):
                    for h in range(4):
                        hs = slice(h * 128, (h + 1) * 128)
                        T.matmul(banks[bk][:, hs], lhsT=Pm[:, s, hs], rhs=vbf[:, s, hs], start=True, stop=False)
                        T.matmul(banks[bk][:, hs], lhsT=qfT[:, h, s * 128:(s + 1) * 128], rhs=Rfb[:, fi, hs], start=False, stop=False)
                        i_ = T.matmul(banks[bk][:, hs], lhsT=qbT[:, h, s * 128:(s + 1) * 128], rhs=rbt[:, b, s, hs], start=False, stop=True)
                    return i_
                fw.op(pe, mmo, reads=[RPm[s], Rvbf, RqT, RRfb[fi], Rrbt[b], ov], writes=[Rb[bk]])

            def gn_pre(s, bko):
                gi = s

                def gbs(gi=gi):
                    for h in range(4):
                        i_ = V.bn_stats(out=gst[:, gi, h * 6:(h + 1) * 6], in_=banks[bko][:, h * 128:(h + 1) * 128])
                    return i_
                fw.op(dve, gbs, reads=[Rb[bko]], writes=[Rgst[gi]])

                def gag(gi=gi):
                    for h in range(4):
                        i_ = V.bn_aggr(out=gmv[:, gi, h * 2:(h + 1) * 2], in_=gst[:, gi, h * 6:(h + 1) * 6])
                    return i_
                fw.op(dve, gag, reads=[Rgst[gi]], writes=[Rgst[gi]])
                gm = gmv[:, gi, :].rearrange("p (h two) -> p h two", two=2)
                fw.op(act, lambda gi=gi, gm=gm: A.activation(out=grs[:, gi, 0:4], in_=gm[:, :, 1], func=AF.Sqrt, bias=epst[:, 0:1]), reads=[Rgst[gi], Rmisc], writes=[Rgst[gi]])
                fw.op(dve, lambda gi=gi: V.reciprocal(out=grs[:, gi, 0:4], in_=grs[:, gi, 0:4]), reads=[Rgst[gi]], writes=[Rgst[gi]])
                fw.op(dve, lambda gi=gi, gm=gm: V.scalar_tensor_tensor(out=grs[:, gi, 4:8], in0=gm[:, :, 0], scalar=-1.0, in1=grs[:, gi, 0:4], op0=ALU.mult, op1=ALU.mult),
                      reads=[Rgst[gi]], writes=[Rgst[gi]])
                onv, ogv = onb[s], ogb[s]

                def gev(gi=gi, onv=onv):
                    for h in range(4):
                        i_ = A.activation(out=onv[:, h * 128:(h + 1) * 128], in_=banks[bko][:, h * 128:(h + 1) * 128], func=AF.Identity,
                                          scale=grs[:, gi, h:h + 1], bias=grs[:, gi, 4 + h:5 + h])
                    return i_
                fw.op(act, gev, reads=[Rb[bko], Rgst[gi], ov], writes=[Ron[s]])
                fw.op(pool, lambda s=s, onv=onv, ogv=ogv: G.tensor_tensor(out=ogv, in0=onv, in1=sg[:, s, :], op=ALU.mult), reads=[Ron[s], Rsg, ov], writes=[Rog[s]])

            def gn_post(s, bkt):
                ogv = ogb[s]
                pb = banks[bkt][:, 0:256].bitcast(BF16)

                def tro(pb=pb, ogv=ogv):
                    for h in range(4):
                        i_ = T.transpose(out=pb[:, h * 128:(h + 1) * 128], in_=ogv[:, h * 128:(h + 1) * 128], identity=identb[:])
                    return i_
                fw.op(pe, tro, reads=[Rog[s], Rmisc, ov], writes=[Rb[bkt]])
                fw.op(act, lambda s=s, pb=pb: A.copy(out=mixT[:, 4:8, s * 128:(s + 1) * 128], in_=pb.rearrange("p (h t) -> p h t", t=128)),
                      reads=[Rb[bkt], ov], writes=[Rmix])

            def head1(tl):
                b = tl["idx"] % 2
                isS = tl["kind"] == "S"
                if (not isS) and tiles[tl["idx"] - 1]["kind"] == "S":
                    fw.op(pool, lambda: G.memset(upb, 0.0), reads=[ov, Rup], writes=[Rup])
                conv_in(tl, b)
                proj_fm(tl, 2, qT, RqT, versions=[(Qf, qfT), (Qb, qbT)], bk0=0)
                proj_fm(tl, 3, kT, RkT, bk0=0)

            def head2(tl):
                b = tl["idx"] % 2
                for s in range(2):
                    proj_tm(s, 4, 0, b)
                    fw.op(act, lambda s=s: A.copy(out=vbf[:, s, :], in_=banks[0][:, :]), reads=[Rb[0], ov], writes=[Rvbf])
                    proj_tm(s, 5, 1, b)
                    fw.op(act, lambda s=s: A.activation(out=sg[:, s, :], in_=banks[1][:, :], func=AF.Silu), reads=[Rb[1], ov], writes=[Rsg])
                conv_mm(tl, (0, 1, 2, 3))
                conv_evac()

            def rbt_load(tl):
                b = tl["idx"] % 2
                for s in range(2):
                    ch = tl["tok0"] // 128 + s
                    fw.dma(sp, rbt[:, b, s, :], rbd[ch], reads=[Rrbd[ch]], writes=[Rrbt[b]], key=("rbi", b, s))

            def tailA1(tl):
                isS = tl["kind"] == "S"
                if isS and tl["first"]:
                    fw.dma(sp, Rf.rearrange("p (h v) -> p h v", h=4), stf[l].rearrange("h d v -> d h v"), reads=[ov], writes=[RRf], key="rfs")
                elif not isS:
                    fw.op(pool, lambda: G.memset(Rf, 0.0), reads=[ov], writes=[RRf])
                ret_rfb(tl, 0)
                ret_scores(0, 2)
                ret_scores(1, 3)
                k_tm(0, kdf, 4, 0)
                k_tm(1, kdf, 5, 1)

            rbt_load(tiles[0])
            if NT > 1:
                load_x(src, tiles[1], 1)
                load_cs(tiles[1])
            head1(tiles[0])
            head2(tiles[0])
            tailA1(tiles[0])
            for ti, tl in enumerate(tiles):
                b = ti % 2
                isS = tl["kind"] == "S"
                if ti + 1 < NT:
                    rbt_load(tiles[ti + 1])
                kv_mm(0, 4, 0)
                kv_mm(1, 5, 1)
                ret_o(tl, b, 0, 2)
                state_update(Rf, RRf, 4, 0)
                ret_rfb(tl, 1)
                state_update(Rf, RRf, 5, 0)
                if not isS:
                    fw.dma(sp, nsf[tl["seq"], l].rearrange("h d v -> d h v"), Rf.rearrange("p (h v) -> p h v", h=4), reads=[RRf, ov], key="nsf")
                gn_pre(0, 2)
                conv_ln_a()
                if ti + 1 < NT:
                    prologue(tiles[ti + 1], 1 - b)
                ret_o(tl, b, 1, 3)
                gn_pre(1, 3)
                if ti + 1 < NT:
                    head1(tiles[ti + 1])
                    head2(tiles[ti + 1])
                conv_ln_b()
                gn_post(0, 2)
                gn_post(1, 3)
                if ti + 1 < NT:
                    tailA1(tiles[ti + 1])
                for s in range(2):
                    yb = (6, 7) if s == 0 else (0, 1)
                    for hf in range(2):
                        bk = yb[hf]

                        def mmw(s=s, hf=hf, bk=bk):
                            for kc in range(8):
                                i_ = T.matmul(banks[bk][:, :], lhsT=mixT[:, kc, s * 128:(s + 1) * 128], rhs=WIN(8 + hf)[:, kc, :], start=(kc == 0), stop=(kc == 7))
                            return i_
                        fw.op(pe, mmw, reads=[Rmix, ov, slot[16 + 2 * hf], slot[17 + 2 * hf]], writes=[Rb[bk]])
                for s in range(2):
                    yb = (6, 7) if s == 0 else (0, 1)
                    epilogue(tl, b, s, yb, dst, zi[0])
                    zi[0] = 1 - zi[0]
                if ti + 2 < NT:
                    load_x(src, tiles[ti + 2], b)
                    load_cs(tiles[ti + 2])

        seq = []
        for l in range(NL):
            seq += [(l, "F0"), (l, "KB"), (l, "M"), (l, "F1")]
        if stages is not None:
            seq = [s_ for s_ in seq if s_ in stages]
        n_xstage = sum(1 for s_ in seq if s_[1] != "KB")
        xi = 0
        cur = xin
        have = False
        for si, (l, nm) in enumerate(seq):
            nx = seq[si + 1] if si + 1 < len(seq) else None
            if nm == "KB":
                stage_KB(l, cur, have_w=have)
                have = False
                continue
            xi += 1
            dst = yout if xi == n_xstage else xs
            if nm in ("F0", "F1"):
                nxt = None
                if PREFETCH and nx is not None:
                    if nx[1] == "KB":
                        nxt = ("M", nx[0])
                    elif nx[1] in ("F0", "F1"):
                        nxt = ("F", nx[0], 0 if nx[1] == "F0" else 1)
                stage_F(l, 0 if nm == "F0" else 1, cur, dst, have_w=have, nxt=nxt)
                have = nxt is not None
            else:
                stage_M(l, cur, dst)
                have = False
            cur = dst
        fw.finish(sp)
        fw.run()
    return nc


_W_KEYS = ["w_ada", "b_ada", "ln_g", "ln_b", "ffn_w1", "ffn_w2", "w_in", "w_out", "conv_w", "conv_b", "conv_ln_g", "conv_ln_b"]


def kernel(x_prompt, x_sample, state_ret_fwd, state_ret_bwd, c, c_ctx, w_ada, b_ada, ln_g, ln_b,
           ffn_w1, ffn_w2, w_in, w_out, conv_w, conv_b, conv_ln_g, conv_ln_b, ret_decay_logit):
    f = lambda a: np.ascontiguousarray(np.asarray(a, dtype=np.float32))
    NCORE = 8
    NL = w_in.shape[0]
    B, S = x_prompt.shape[0], x_prompt.shape[1]
    DB, ST = x_sample.shape[0], x_sample.shape[1]
    NP = B // NCORE
    assert DB == NCORE and S == 256
    nc = build(NL=NL, ST=ST, NP=NP)
    wts = dict(w_ada=f(w_ada), b_ada=f(b_ada), ln_g=f(ln_g), ln_b=f(ln_b), ffn_w1=f(ffn_w1), ffn_w2=f(ffn_w2), w_in=f(w_in),
               w_out=f(w_out), conv_w=f(conv_w), conv_b=f(conv_b), conv_ln_g=f(conv_ln_g), conv_ln_b=f(conv_ln_b),
               dlogit=f(ret_decay_logit).reshape(NL, 8))
    in_maps = []
    for i in range(NCORE):
        m = dict(wts)
        m["xin"] = np.concatenate([f(x_sample[i]), f(x_prompt[i * NP:(i + 1) * NP]).reshape(NP * S, D)], axis=0)
        m["stf"] = f(state_ret_fwd[i])
        m["stb"] = f(state_ret_bwd[i])
        m["cond"] = np.stack([f(c[i]), f(c_ctx)], axis=0)
        in_maps.append(m)
    res = run_bass_kernel_spmd(nc, in_maps, core_ids=list(range(NCORE)))
    outs = res.results
    y_sample = np.stack([outs[i]["yout"][:ST] for i in range(NCORE)], axis=0)
    y_prompt = np.concatenate([outs[i]["yout"][ST:].reshape(NP, S, D) for i in range(NCORE)], axis=0)
    nf = np.concatenate([outs[i]["nsf"] for i in range(NCORE)], axis=0)
    nb = np.concatenate([outs[i]["nsb"] for i in range(NCORE)], axis=0)
    return (y_prompt.astype(np.float32), y_sample.astype(np.float32), nf.astype(np.float32), nb.astype(np.float32))
```

```python
import contextlib
import math
import numpy as np
import concourse.bass as bass
import concourse.mybir as mybir
from concourse.bass_utils import run_bass_kernel_spmd

F32 = mybir.dt.float32
BF16 = mybir.dt.bfloat16
AF = mybir.ActivationFunctionType
ALU = mybir.AluOpType

D = 1024
DFF = 2816
NHC = 22
DEPTH = 4
ALPHA = (2.0 * DEPTH) ** 0.25
EPS = 1e-5
CK = 31
QSCALE = 128.0 ** -0.5
TWO_PI = 2.0 * math.pi
PREFETCH = True


class Res:
    __slots__ = ("name", "w", "rs")

    def __init__(self, name):
        self.name = name
        self.w = None
        self.rs = {}


class Eng:
    def __init__(self, name, h, sem, is_pe=False):
        self.name = name
        self.h = h
        self.sem = sem
        self.n = 0
        self.seen = {}
        self.is_pe = is_pe
        self.prog = []


class FW:
    def __init__(self, nc, stack):
        self.nc = nc
        self.stack = stack
        mk = lambda nm, h, pe=False: Eng(nm, h, stack.enter_context(nc.semaphore("s_" + nm)), pe)
        self.pe = mk("pe", nc.tensor, True)
        self.act = mk("act", nc.scalar)
        self.dve = mk("dve", nc.vector)
        self.pool = mk("pool", nc.gpsimd)
        self.sp = mk("sp", nc.sync)
        self.dma_sems = {}
        self.nres = 0

    def res(self, name=None):
        self.nres += 1
        return Res(name or f"r{self.nres}")

    def _wait(self, eng, dep):
        key, (sem, idx, is_eng) = dep
        if is_eng and key == eng.name and eng.is_pe:
            return
        if eng.seen.get(key, 0) >= idx:
            return
        eng.prog.append(lambda h=eng.h, s=sem, v=idx: h.wait_ge(s, v))
        eng.seen[key] = idx

    def _deps(self, reads, writes):
        deps = []
        for r in reads:
            if r.w is not None:
                deps.append(r.w)
        for w in writes:
            if w.w is not None:
                deps.append(w.w)
            deps.extend(w.rs.items())
        return deps

    def _mark(self, me, reads, writes):
        key, val = me
        for r in reads:
            r.rs[key] = val
        for w in writes:
            w.w = me
            w.rs = {}

    def op(self, eng, fn, reads=(), writes=()):
        for d in self._deps(reads, writes):
            self._wait(eng, d)
        eng.n += 1
        eng.prog.append(lambda fn=fn, s=eng.sem: fn().then_inc(s, 1))
        me = (eng.name, (eng.sem, eng.n, True))
        self._mark(me, reads, writes)
        return me

    def dma(self, qeng, out, in_, reads=(), writes=(), key=None):
        for d in self._deps(reads, writes):
            self._wait(qeng, d)
        if key not in self.dma_sems:
            self.dma_sems[key] = [self.stack.enter_context(self.nc.semaphore("d_%d" % len(self.dma_sems))), 0]
        ent = self.dma_sems[key]
        ent[1] += 16
        qeng.prog.append(lambda h=qeng.h, o=out, i=in_, s=ent[0]: h.dma_start(out=o, in_=i).then_inc(s, 16))
        me = (("dma", key), (ent[0], ent[1], False))
        self._mark(me, reads, writes)
        return me

    def finish(self, eng):
        for key, (sem, val) in self.dma_sems.items():
            if val:
                eng.prog.append(lambda h=eng.h, s=sem, v=val: h.wait_ge(s, v))
        for e2 in (self.pe, self.act, self.dve, self.pool, self.sp):
            if e2 is not eng and e2.n:
                eng.prog.append(lambda h=eng.h, s=e2.sem, v=e2.n: h.wait_ge(s, v))

    def run(self):
        with self.nc.Block() as block:
            @block.tensor
            def _(e):
                for t in self.pe.prog:
                    t()

            @block.scalar
            def _(e):
                for t in self.act.prog:
                    t()

            @block.vector
            def _(e):
                for t in self.dve.prog:
                    t()

            @block.gpsimd
            def _(e):
                for t in self.pool.prog:
                    t()

            @block.sync
            def _(e):
                for t in self.sp.prog:
                    t()


def build(NL=4, ST=4096, NP=4, stages=None, dbg=False):
    NTOK = ST + NP * 256
    NCH = NTOK // 128
    nc = bass.Bass("TRN2", target_bir_lowering=False)
    din = lambda n, s, dt=F32: nc.dram_tensor(n, s, dt, kind="ExternalInput").ap()
    dout = lambda n, s: nc.dram_tensor(n, s, F32, kind="ExternalOutput").ap()
    dint = lambda n, s, dt=F32: nc.dram_tensor(n, s, dt, kind="Internal").ap()
    xin = din("xin", [NTOK, D])
    stf = din("stf", [NL, 4, 128, 128])
    stb = din("stb", [NL, 4, 128, 128])
    cond = din("cond", [2, D])
    w_ada = din("w_ada", [NL, D, 9 * D])
    b_ada = din("b_ada", [NL, 9 * D])
    ln_g = din("ln_g", [NL, 3, D])
    ln_b = din("ln_b", [NL, 3, D])
    ffn_w1 = din("ffn_w1", [NL, 2, D, 2 * DFF])
    ffn_w2 = din("ffn_w2", [NL, 2, DFF, D])
    w_in = din("w_in", [NL, D, 3072])
    w_out = din("w_out", [NL, D, D])
    conv_w = din("conv_w", [NL, CK, 512])
    conv_b = din("conv_b", [NL, 512])
    conv_ln_g = din("conv_ln_g", [NL, 512])
    conv_ln_b = din("conv_ln_b", [NL, 512])
    dlogit = din("dlogit", [NL, 8])
    yout = dout("yout", [NTOK, D])
    nsf = dout("nsf", [max(NP, 1), NL, 4, 128, 128])
    nsb = dout("nsb", [max(NP, 1), NL, 4, 128, 128])
    xs = dint("xs", [NTOK, D])
    mods = dint("mods", [NL, 2, 9 * D])
    rope = (dout if dbg else dint)("rope", [2, 128, max(ST, 256)])
    rbd = dint("rbd", [NCH, 128, 512], BF16)

    tiles = []
    for t in range(ST // 256):
        tiles.append(dict(kind="S", tok0=t * 256, cond=0, first=(t == 0), idx=len(tiles)))
    for n in range(NP):
        tiles.append(dict(kind="P", tok0=ST + n * 256, cond=1, seq=n, idx=len(tiles)))
    NT = len(tiles)

    with contextlib.ExitStack() as st, nc.allow_non_contiguous_dma(reason="small vector gathers"):
        fw = FW(nc, st)
        pe, act, dve, pool, sp = fw.pe, fw.act, fw.dve, fw.pool, fw.sp
        sbt = lambda n, s, dt=F32: st.enter_context(nc.sbuf_tensor(n, s, dt))
        R = fw.res

        arena = sbt("arena", [128, 33 * 1024])
        aux = sbt("aux", [128, 4864])
        xbuf = sbt("xbuf", [128, 2, 2, D])
        hT = sbt("hT", [128, 2, 8, 256], BF16)
        zb = sbt("zb", [128, 2, D])
        gate = sbt("gate", [128, 2, D])
        lng = sbt("lng", [128, D])
        lnb = sbt("lnb", [128, D])
        rbt = sbt("rbt", [128, 2, 2, 512], BF16)
        ident = sbt("ident", [128, 128])
        identb = sbt("identb", [128, 128], BF16)
        sc1 = sbt("sc1", [128, 2, 8])
        shf = sbt("shf", [128, 2, 8])
        stats = sbt("stats", [128, 4, 12])
        mv = sbt("mv", [128, 4, 2])
        rstd = sbt("rstd", [128, 4, 2])
        gst = sbt("gst", [128, 2, 24])
        gmv = sbt("gmv", [128, 2, 8])
        grs = sbt("grs", [128, 2, 8])
        lgt = sbt("lgt", [128, 8])
        kdf = sbt("kdf", [128, 4])
        kdb = sbt("kdb", [128, 4])
        gC = sbt("gC", [128, 8])
        piota = sbt("piota", [128, 1])
        cw = sbt("cw", [128, 4, CK])
        cvec = sbt("cvec", [128, 3, 4])
        scT = sbt("scT", [128, 8, 2], BF16)
        condT = sbt("condT", [128, 8, 2])
        tiny = sbt("tiny", [128, 8])
        negpi = sbt("negpi", [128, 1])
        epst = sbt("epst", [128, 1])
        dmask = sbt("dmask", [128, 32])

        slot = [R(f"slot{i}") for i in range(33)]
        ov = R("ov")
        Rx = [[R(f"x{b}{s}") for s in range(2)] for b in range(2)]
        RhT = [R("hT0"), R("hT1")]
        Rzb = [R("zb0"), R("zb1")]
        Rgate, Rlng, Rlnb, Rsc = R("gate"), R("lng"), R("lnb"), R("sc")
        Rident = R("ident")
        Rst = [R(f"st{i}") for i in range(4)]
        Rgst = [R("gst0"), R("gst1")]
        Rdec = R("dec")
        Rcw = R("cw")
        Rrbt = [R("rbt0"), R("rbt1")]
        Rmods = R("mods")
        Rrope = R("rope")
        Rxs = [R(f"xs{t}") for t in range(NT)]
        Rrbd = [R(f"rbd{c}") for c in range(NCH)]
        Rmisc = R("misc")

        banks = [st.enter_context(nc.psum_tensor(f"bank{i}", [128, 512], F32)) for i in range(8)]
        Rb = [R(f"bank{i}") for i in range(8)]

        def aview(off, n, dt, pat=None, **kw):
            v = arena[:, off:off + n]
            if dt is not F32:
                v = v.bitcast(dt)
            if pat:
                v = v.rearrange(pat, **kw)
            return v

        def xview(off, n, dt, pat=None, **kw):
            v = aux[:, off:off + n]
            if dt is not F32:
                v = v.bitcast(dt)
            if pat:
                v = v.rearrange(pat, **kw)
            return v

        uT = xview(0, 2816, BF16, "p (j t) -> p j t", t=256)
        silt = xview(2816, 1024, F32, "p (a t) -> p a t", t=256)
        Qf = xview(0, 1024, F32, "p (h t) -> p h t", t=256)
        Qb = xview(1024, 1024, F32, "p (h t) -> p h t", t=256)
        upb = xview(2048, 752, BF16)
        upS = xview(2048, 752, BF16, "p (c r w) -> p c r w", r=4, w=94)
        upP = xview(2048, 572, BF16, "p (c w) -> p c w", w=286)
        Dblk = xview(2800, 1984, BF16, "p (c k j) -> p c k j", c=4, k=CK)
        o = [20 * 1024]

        def oa(n, dt, pat=None, **kw):
            v = aview(o[0], n, dt, pat, **kw)
            o[0] += n
            assert o[0] <= 33 * 1024
            return v

        Mcomb = oa(512, F32)
        qT = oa(512, BF16, "p (h t) -> p h t", t=256)
        qfT = oa(512, BF16, "p (h t) -> p h t", t=256)
        qbT = oa(512, BF16, "p (h t) -> p h t", t=256)
        kT = oa(512, BF16, "p (h t) -> p h t", t=256)
        ropeCS = oa(1024, F32, "p (b c t) -> p b c t", b=2, c=2)
        rt1 = oa(512, F32, "p (b t) -> p b t", b=2)
        rt2 = oa(512, F32, "p (b t) -> p b t", b=2)
        vbf = oa(512, BF16, "p (s c) -> p s c", s=2)
        sg = oa(1024, F32, "p (s c) -> p s c", s=2)
        sgm = oa(512, F32, "p (b t) -> p b t", b=2)
        acc = oa(1024, F32, "p (c t) -> p c t", c=4)
        xn = oa(1024, F32, "p (s c) -> p s c", s=2)
        mixT = oa(1024, BF16, "p (k t) -> p k t", t=256)
        Pm = oa(512, BF16, "p (b c) -> p b c", b=2)
        kdT = oa(512, BF16, "p (b c) -> p b c", b=2)
        Rf = oa(512, F32)
        Rbs = oa(512, F32)
        Rfb = oa(512, BF16, "p (b c) -> p b c", b=2)
        on = oa(512, F32)
        og = oa(256, BF16)
        og2 = oa(256, BF16)
        on2 = sbt("on2", [128, 512])
        onb = [on, on2[:]]
        ogb = [og, og2]
        cwraw = xn

        RqT, RkT, Rcs, Rrt, Rvbf, Rsg, Rup, Rsgm, Racc, Rxn, Rmix = (R("qT"), R("kT"), [R("cs0"), R("cs1")],
            [R("rt0"), R("rt1")], R("vbf"), R("sg"), R("up"), [R("sgm0"), R("sgm1")], [R(f"acc{i}") for i in range(4)],
            [R("xn0"), R("xn1")], R("mix"))
        RPm, RkdT, RRf, RRbs, RRfb, Ron, Rog = [R("P0"), R("P1")], [R("kd0"), R("kd1")], R("Rf"), R("Rbs"), [R("Rfb0"), R("Rfb1")], [R("on0"), R("on1")], [R("og0"), R("og1")]
        RuT, Rsil = R("uT"), [R("sil0"), R("sil1"), R("sil2"), R("sil3")]
        Rdb = R("dblk")

        V, A, G = nc.vector, nc.scalar, nc.gpsimd
        T = nc.tensor

        fw.op(pool, lambda: G.memset(ident[:], 0.0), writes=[Rident])
        fw.op(pool, lambda: G.affine_select(out=ident[:], in_=ident[:], compare_op=ALU.not_equal, fill=1.0, base=0,
                                            pattern=[[-1, 128]], channel_multiplier=1), reads=[Rident], writes=[Rident])
        fw.op(dve, lambda: V.tensor_copy(out=identb[:], in_=ident[:]), reads=[Rident], writes=[Rmisc])
        fw.op(pool, lambda: G.iota(piota[:], pattern=[[0, 1]], base=0, channel_multiplier=1,
                                   allow_small_or_imprecise_dtypes=True), writes=[Rmisc])
        fw.op(pool, lambda: G.memset(negpi[:], -math.pi), writes=[Rmisc])
        fw.op(pool, lambda: G.memset(epst[:], EPS), writes=[Rmisc])
        Rdm = R("dmask")

        fw.op(dve, lambda: V.tensor_tensor(out=dmask[:], in0=ident[:, 0:32], in1=ident[:, 32:64], op=ALU.add), reads=[Rident], writes=[Rdm])
        fw.op(dve, lambda: V.tensor_tensor(out=dmask[:], in0=dmask[:], in1=ident[:, 64:96], op=ALU.add), reads=[Rident, Rdm], writes=[Rdm])
        fw.op(dve, lambda: V.tensor_tensor(out=dmask[:], in0=dmask[:], in1=ident[:, 96:128], op=ALU.add), reads=[Rident, Rdm], writes=[Rdm])

        for c_ in range(2):
            fw.dma(sp, condT[:, :, c_], cond[c_, :].rearrange("(k p) -> p k", p=128), writes=[Rmisc], key=("c0", c_))
        fw.op(act, lambda: A.activation(out=scT[:], in_=condT[:], func=AF.Silu), reads=[Rmisc], writes=[Rsc])
        wab = arena[:, 0:6144].bitcast(BF16).rearrange("p (b k c) -> p b k c", b=3, k=8)
        badat = arena[0:2, 6144:6656]
        modrow = arena[0:2, 7168:9216].rearrange("p (m c) -> p m c", m=2)[:, :, 0:512]
        Rwab = [slot[0], slot[2], slot[4]]
        Rwab2 = [slot[1], slot[3], slot[5]]
        Rmr = [slot[7], slot[8]]
        Rbad = slot[6]
        k = 0
        for l in range(NL):
            for j in range(18):
                b3 = k % 3
                fw.dma(pool, wab[:, b3], w_ada[l, :, j * 512:(j + 1) * 512].rearrange("(k p) c -> p k c", p=128),
                       writes=[Rwab[b3], Rwab2[b3]], key=("wab", b3))
                fw.dma(sp, badat[:], b_ada[l:l + 1, j * 512:(j + 1) * 512].partition_broadcast(2), writes=[Rbad], key="bad")
                bk = Rb[k % 2]

                def mm(b3=b3, bank=banks[k % 2]):
                    for kc in range(8):
                        i = T.matmul(bank[0:2, :], lhsT=scT[:, kc, :], rhs=wab[:, b3, kc, :], start=(kc == 0), stop=(kc == 7))
                    return i
                fw.op(pe, mm, reads=[Rwab[b3], Rwab2[b3], Rsc], writes=[bk])
                m2 = k % 2
                fw.op(dve, lambda m2=m2, bank=banks[k % 2]: V.tensor_tensor(out=modrow[:, m2, :], in0=bank[0:2, :], in1=badat[:], op=ALU.add),
                      reads=[bk, Rbad], writes=[Rmr[m2]])
                fw.dma(sp, mods[l, :, j * 512:(j + 1) * 512], modrow[:, m2, :], reads=[Rmr[m2]], writes=[Rmods], key=("mr", m2))
                k += 1

        if ST > 0:
            invf = sbt("invf", [128, 1])
            MAGIC = 12582912.0

            def mkf():
                G.iota(tiny[0:64, 0:1], pattern=[[0, 1]], base=0, channel_multiplier=1, allow_small_or_imprecise_dtypes=True)
                return G.iota(tiny[64:128, 0:1], pattern=[[0, 1]], base=0, channel_multiplier=1, allow_small_or_imprecise_dtypes=True)
            fw.op(pool, mkf, reads=[ov, Rmisc], writes=[Rmisc])
            fw.op(dve, lambda: V.tensor_scalar(out=tiny[:, 2:3], in0=tiny[:, 0:1], scalar1=32.0, scalar2=32.0, op0=ALU.is_ge, op1=ALU.mult), reads=[ov, Rmisc], writes=[Rmisc])
            fw.op(dve, lambda: V.tensor_tensor(out=tiny[:, 0:1], in0=tiny[:, 0:1], in1=tiny[:, 2:3], op=ALU.subtract), reads=[ov, Rmisc], writes=[Rmisc])
            fw.op(act, lambda: A.activation(out=invf[:], in_=tiny[:, 0:1], func=AF.Exp, scale=-math.log(10000.0) / 32.0), reads=[ov, Rmisc], writes=[Rmisc])
            fw.op(dve, lambda: V.tensor_single_scalar(out=invf[:], in_=invf[:], scalar=1.0 / TWO_PI, op=ALU.mult), reads=[ov, Rmisc], writes=[Rmisc])
            posv = sg
            pa = posv[:, 0, 0:256]
            pr_ = posv[:, 0, 256:512]
            ps_ = posv[:, 1, 0:256]
            pc_ = posv[:, 1, 256:512]
            SC = 6.283184
            for t in range(ST // 256):
                def mkpos(t=t):
                    G.iota(pa[0:64, :], pattern=[[1, 4], [0, 64]], base=4 * t, channel_multiplier=0, allow_small_or_imprecise_dtypes=True)
                    return G.iota(pa[64:128, :], pattern=[[0, 4], [1, 64]], base=0, channel_multiplier=0, allow_small_or_imprecise_dtypes=True)
                fw.op(pool, mkpos, reads=[ov], writes=[Rsg])
                fw.op(dve, lambda: V.tensor_scalar(out=pa, in0=pa, scalar1=invf[:, 0:1], scalar2=None, op0=ALU.mult), reads=[ov, Rsg, Rmisc], writes=[Rsg])
                for (dstv, off, rr) in ((ps_, 0.0, Rxn[0]), (pc_, 0.25, Rxn[1])):
                    fw.op(dve, lambda dstv=dstv, off=off: V.tensor_single_scalar(out=dstv, in_=pa, scalar=off, op=ALU.add), reads=[ov, Rsg], writes=[rr])
                    fw.op(dve, lambda dstv=dstv: V.tensor_single_scalar(out=pr_, in_=dstv, scalar=MAGIC, op=ALU.add), reads=[ov, rr], writes=[Rsgm[0]])
                    fw.op(dve, lambda: V.tensor_single_scalar(out=pr_, in_=pr_, scalar=MAGIC, op=ALU.subtract), reads=[ov, Rsgm[0]], writes=[Rsgm[0]])
                    fw.op(dve, lambda dstv=dstv: V.tensor_tensor(out=dstv, in0=dstv, in1=pr_, op=ALU.subtract), reads=[ov, rr, Rsgm[0]], writes=[rr])
                    fw.op(act, lambda dstv=dstv: A.activation(out=dstv, in_=dstv, func=AF.Sin, scale=SC), reads=[ov, rr], writes=[rr])
                fw.dma(sp, rope[1, :, t * 256:(t + 1) * 256], ps_, reads=[ov, Rxn[0]], writes=[Rrope], key="rp0")
                fw.dma(sp, rope[0, :, t * 256:(t + 1) * 256], pc_, reads=[ov, Rxn[1]], writes=[Rrope], key="rp1")

        def stage_consts(l, sl, half):
            for c in range(2):
                fw.dma(sp, gate[:, c, :], mods[l, c:c + 1, (3 * sl + 2) * D:(3 * sl + 3) * D].partition_broadcast(128),
                       reads=[Rmods], writes=[Rgate], key=("gate", c))
                fw.dma(sp, sc1[:, c, :], mods[l, c, (3 * sl + 1) * D:(3 * sl + 2) * D].rearrange("(k p) -> p k", p=128),
                       reads=[Rmods], writes=[Rsc], key=("sc1", c))
                fw.dma(sp, shf[:, c, :], mods[l, c, (3 * sl) * D:(3 * sl + 1) * D].rearrange("(k p) -> p k", p=128),
                       reads=[Rmods], writes=[Rsc], key=("shf", c))
            fw.dma(sp, lng[:], ln_g[l, sl:sl + 1, :].partition_broadcast(128), writes=[Rlng], key="lng")
            fw.dma(sp, lnb[:], ln_b[l, sl:sl + 1, :].partition_broadcast(128), writes=[Rlnb], key="lnb")
            fw.op(dve, lambda: V.tensor_single_scalar(out=sc1[:], in_=sc1[:], scalar=1.0, op=ALU.add), reads=[Rsc], writes=[Rsc])
            if half:
                fw.op(pool, lambda: G.tensor_single_scalar(out=gate[:], in_=gate[:], scalar=0.5, op=ALU.mult), reads=[Rgate], writes=[Rgate])

        def load_x(src, tl, b):
            for s in range(2):
                r0 = tl["tok0"] + s * 128
                fw.dma(sp, xbuf[:, b, s, :], src[r0:r0 + 128, :], reads=[Rxs[tl["idx"]]], writes=[Rx[b][s]], key=("x", b, s))

        def prologue(tl, b):
            c = tl["cond"]
            for pr in range(4):
                bk = pr % 2

                def tr(pr=pr, bk=bk):
                    for cc in range(2):
                        for s in range(2):
                            fc = pr * 2 + cc
                            i = T.transpose(out=banks[bk][:, cc * 256 + s * 128: cc * 256 + (s + 1) * 128],
                                            in_=xbuf[:, b, s, fc * 128:(fc + 1) * 128], identity=ident[:])
                    return i
                fw.op(pe, tr, reads=[Rx[b][0], Rx[b][1], Rident], writes=[Rb[bk]])

                def ev(pr=pr, bk=bk):
                    for cc in range(2):
                        fc = pr * 2 + cc
                        i = A.activation(out=hT[:, b, fc, :], in_=banks[bk][:, cc * 256:(cc + 1) * 256], func=AF.Identity,
                                         scale=sc1[:, c, fc:fc + 1], bias=shf[:, c, fc:fc + 1])
                    return i
                fw.op(act, ev, reads=[Rb[bk], Rsc], writes=[RhT[b]])

        def epilogue(tl, b, s, ybanks, dst, zi):
            c = tl["cond"]
            z = zb[:, zi, :]
            rz = Rzb[zi]
            for hf in range(2):
                fw.op(dve, lambda hf=hf: V.tensor_tensor(out=z[:, hf * 512:(hf + 1) * 512], in0=banks[ybanks[hf]][:, :],
                                                         in1=gate[:, c, hf * 512:(hf + 1) * 512], op=ALU.mult),
                      reads=[Rb[ybanks[hf]], Rgate], writes=[rz])
            fw.op(dve, lambda: V.scalar_tensor_tensor(out=z, in0=xbuf[:, b, s, :], scalar=ALPHA, in1=z, op0=ALU.mult, op1=ALU.add),
                  reads=[Rx[b][s], rz], writes=[rz])
            ln_rows(z, rz, zi, D)
            fw.op(act, lambda: A.activation(out=z, in_=z, func=AF.Identity, scale=rstd[:, zi, 0:1], bias=rstd[:, zi, 1:2]),
                  reads=[rz, Rst[zi]], writes=[rz])
            fw.op(pool, lambda: G.tensor_tensor(out=z, in0=z, in1=lng[:], op=ALU.mult), reads=[rz, Rlng], writes=[rz])
            fw.op(pool, lambda: G.tensor_tensor(out=z, in0=z, in1=lnb[:], op=ALU.add), reads=[rz, Rlnb], writes=[rz])
            r0 = tl["tok0"] + s * 128
            fw.dma(sp, dst[r0:r0 + 128, :], z, reads=[rz], writes=[Rxs[tl["idx"]]], key=("xo", zi))

        def ln_rows(src, rsrc, si, n):
            nchunk = n // 512

            def bs():
                for q in range(nchunk):
                    i = V.bn_stats(out=stats[:, si, q * 6:(q + 1) * 6], in_=src[:, q * 512:(q + 1) * 512])
                return i
            fw.op(dve, bs, reads=[rsrc], writes=[Rst[si]])
            fw.op(dve, lambda: V.bn_aggr(out=mv[:, si, :], in_=stats[:, si, 0:6 * nchunk]), reads=[Rst[si]], writes=[Rst[si]])
            fw.op(act, lambda: A.activation(out=rstd[:, si, 0:1], in_=mv[:, si, 1:2], func=AF.Sqrt, bias=epst[:, 0:1]), reads=[Rst[si], Rmisc], writes=[Rst[si]])
            fw.op(dve, lambda: V.reciprocal(out=rstd[:, si, 0:1], in_=rstd[:, si, 0:1]), reads=[Rst[si]], writes=[Rst[si]])
            fw.op(dve, lambda: V.scalar_tensor_tensor(out=rstd[:, si, 1:2], in0=mv[:, si, 0:1], scalar=-1.0, in1=rstd[:, si, 0:1], op0=ALU.mult, op1=ALU.mult),
                  reads=[Rst[si]], writes=[Rst[si]])

        claim_t = tiny

        def claim(extra=()):
            fw.op(pool, lambda: G.memset(claim_t[:, 4:5], 0.0), writes=[ov] + list(extra))

        def w1_dma(l, i, j):
            for part in range(2):
                dstv = aview(j * 2048, 2048, BF16, "p (k c) -> p k c", c=512)[:, :, part * 256:(part + 1) * 256]
                c0 = part * DFF + j * 256
                fw.dma(pool, dstv, ffn_w1[l, i, :, c0:c0 + 256].rearrange("(k p) c -> p k c", p=128),
                       reads=([ov] if 2 * j + 1 >= 20 else []), writes=[slot[2 * j], slot[2 * j + 1]], key=("w", 2 * j, part))

        def w2_dma(l, i, j):
            dstv = aview(22 * 1024 + j * 1024, 1024, BF16, "p (e c) -> p e c", e=2)
            fw.dma(pool, dstv, ffn_w2[l, i, j * 256:(j + 1) * 256, :].rearrange("(e p) c -> p e c", p=128),
                   reads=[ov], writes=[slot[22 + j]], key=("w", 22 + j, 0))

        def mixer_dma(l, m):
            if m < 6:
                fw.dma(pool, WIN(m), w_in[l, :, m * 512:(m + 1) * 512].rearrange("(k p) c -> p k c", p=128),
                       writes=[slot[2 * m], slot[2 * m + 1]], key=("w", 2 * m, 0))
            elif m in (8, 9):
                hf = m - 8
                fw.dma(pool, WIN(8 + hf), w_out[l, :, hf * 512:(hf + 1) * 512].rearrange("(k p) c -> p k c", p=128),
                       writes=[slot[16 + 2 * hf], slot[17 + 2 * hf]], key=("w", 16 + 2 * hf, 0))

        def stage_F(l, i, src, dst, have_w=False, nxt=None):
            sl = 0 if i == 0 else 2
            if not have_w:
                claim()
                for j in range(11):
                    w1_dma(l, i, j)
                for j in range(11):
                    w2_dma(l, i, j)
            stage_consts(l, sl, True)
            load_x(src, tiles[0], 0)
            zi = 0
            prologue(tiles[0], 0)
            for ti, tl in enumerate(tiles):
                b = ti % 2
                if ti + 1 < NT:
                    load_x(src, tiles[ti + 1], 1 - b)
                for hc in range(NHC):
                    j, e = hc // 2, hc % 2
                    w1v = aview(j * 2048, 2048, BF16, "p (k c) -> p k c", c=512)
                    bk = hc % 4

                    def mm(w1v=w1v, e=e, bk=bk, b=b):
                        for part in range(2):
                            for kc in range(8):
                                i_ = T.matmul(banks[bk][:, part * 256:(part + 1) * 256],
                                              lhsT=w1v[:, kc, part * 256 + e * 128: part * 256 + (e + 1) * 128],
                                              rhs=hT[:, b, kc, :], start=(kc == 0), stop=(kc == 7))
                        return i_
                    fw.op(pe, mm, reads=[slot[2 * j], slot[2 * j + 1], RhT[b]], writes=[Rb[bk]])
                    sb_ = hc % 4
                    fw.op(act, lambda bk=bk, sb_=sb_: A.activation(out=silt[:, sb_, :], in_=banks[bk][:, 0:256], func=AF.Silu),
                          reads=[Rb[bk], ov], writes=[Rsil[sb_]])
                    fw.op(dve, lambda bk=bk, sb_=sb_, hc=hc: V.tensor_tensor(out=uT[:, hc, :], in0=banks[bk][:, 256:512], in1=silt[:, sb_, :], op=ALU.mult),
                          reads=[Rb[bk], Rsil[sb_], ov], writes=[RuT])
                    if ti == NT - 1 and nxt is not None and e == 1:
                        if nxt[0] == "F":
                            w1_dma(nxt[1], nxt[2], j)
                        else:
                            mixer_dma(nxt[1], j)
                if ti + 1 < NT:
                    prologue(tiles[ti + 1], 1 - b)
                for s in range(2):
                    for hf in range(2):
                        bk = 4 + s * 2 + hf

                        def mm2(s=s, hf=hf, bk=bk):
                            for hc in range(NHC):
                                w2v = aview(22 * 1024 + (hc // 2) * 1024, 1024, BF16, "p (e c) -> p e c", e=2)
                                i_ = T.matmul(banks[bk][:, :], lhsT=uT[:, hc, s * 128:(s + 1) * 128],
                                              rhs=w2v[:, hc % 2, hf * 512:(hf + 1) * 512], start=(hc == 0), stop=(hc == NHC - 1))
                            return i_
                        fw.op(pe, mm2, reads=[RuT, ov] + [slot[22 + q] for q in range(11)], writes=[Rb[bk]])
                if ti == NT - 1 and nxt is not None and nxt[0] == "F":
                    for j in range(11):
                        w2_dma(nxt[1], nxt[2], j)
                for s in range(2):
                    epilogue(tl, b, s, (4 + s * 2, 5 + s * 2), dst, zi)
                    zi = 1 - zi

        def WIN(m):
            return aview(m * 2048, 2048, BF16, "p (k c) -> p k c", c=512)

        def load_mixer(l, have_w=False):
            if not have_w:
                for m in (0, 1, 2, 3, 4, 5, 8, 9):
                    mixer_dma(l, m)
            for m in (2, 3):
                srcv = aview(m * 2048, 2048, BF16, "p (a two f) -> p a two f", two=2, f=32)
                dstv = aview((m + 4) * 2048, 2048, BF16, "p (a two f) -> p a two f", two=2, f=32)
                fw.op(act, lambda srcv=srcv, dstv=dstv: A.mul(out=dstv[:, :, 0, :], in_=srcv[:, :, 1, :], mul=-1.0),
                      reads=[slot[2 * m], slot[2 * m + 1]], writes=[slot[2 * m + 8], slot[2 * m + 9]])
                fw.op(act, lambda srcv=srcv, dstv=dstv: A.copy(out=dstv[:, :, 1, :], in_=srcv[:, :, 0, :]),
                      reads=[slot[2 * m], slot[2 * m + 1]], writes=[slot[2 * m + 8], slot[2 * m + 9]])

        def layer_consts(l):
            fw.dma(sp, lgt[:], dlogit[l:l + 1, :].partition_broadcast(128), writes=[Rdec], key="lgt")
            fw.op(act, lambda: A.activation(out=lgt[:], in_=lgt[:], func=AF.Exp, scale=-1.0), reads=[Rdec], writes=[Rdec])
            fw.op(act, lambda: A.activation(out=lgt[:], in_=lgt[:], func=AF.Ln, bias=1.0), reads=[Rdec], writes=[Rdec])
            fw.op(act, lambda: A.mul(out=lgt[:], in_=lgt[:], mul=-1.0), reads=[Rdec], writes=[Rdec])
            fw.op(act, lambda: A.activation(out=gC[:], in_=lgt[:], func=AF.Exp, scale=128.0), reads=[Rdec], writes=[Rdec])
            fw.op(dve, lambda: V.tensor_scalar(out=tiny[:, 1:2], in0=piota[:], scalar1=-1.0, scalar2=127.0, op0=ALU.mult, op1=ALU.add),
                  reads=[Rmisc], writes=[Rmisc])
            fw.op(dve, lambda: V.tensor_scalar(out=kdf[:], in0=lgt[:, 0:4], scalar1=tiny[:, 1:2], scalar2=None, op0=ALU.mult), reads=[Rdec, Rmisc], writes=[Rdec])
            fw.op(dve, lambda: V.tensor_scalar(out=kdb[:], in0=lgt[:, 4:8], scalar1=piota[:, 0:1], scalar2=None, op0=ALU.mult), reads=[Rdec, Rmisc], writes=[Rdec])
            fw.op(act, lambda: A.activation(out=kdf[:], in_=kdf[:], func=AF.Exp), reads=[Rdec], writes=[Rdec])
            fw.op(act, lambda: A.activation(out=kdb[:], in_=kdb[:], func=AF.Exp), reads=[Rdec], writes=[Rdec])
            dist = xn[:, 0, 0:128]
            ci = xn[:, 0, 128:256]
            t1 = xn[:, 0, 256:384]
            t2 = xn[:, 0, 384:512]
            mk1 = xn[:, 1, 0:128]
            fw.op(pool, lambda: G.iota(dist, pattern=[[1, 128]], base=0, channel_multiplier=-1, allow_small_or_imprecise_dtypes=True),
                  reads=[ov], writes=[Rxn[0]])
            fw.op(pool, lambda: G.iota(ci, pattern=[[1, 128]], base=0, channel_multiplier=0, allow_small_or_imprecise_dtypes=True),
                  reads=[ov], writes=[Rxn[0]])
            for h in range(4):
                fw.op(dve, lambda: V.tensor_single_scalar(out=t1, in_=dist, scalar=0.0, op=ALU.max), reads=[Rxn[0], ov], writes=[Rxn[0]])
                fw.op(act, lambda h=h: A.activation(out=t1, in_=t1, func=AF.Exp, scale=lgt[:, h:h + 1]), reads=[Rxn[0], Rdec, ov], writes=[Rxn[0]])
                fw.op(dve, lambda: V.tensor_single_scalar(out=mk1, in_=dist, scalar=0.0, op=ALU.is_ge), reads=[Rxn[0], ov], writes=[Rxn[1]])
                fw.op(dve, lambda: V.tensor_tensor(out=t1, in0=t1, in1=mk1, op=ALU.mult), reads=[Rxn[0], Rxn[1], ov], writes=[Rxn[0]])
                fw.op(dve, lambda: V.tensor_scalar(out=t2, in0=dist, scalar1=-1.0, scalar2=0.0, op0=ALU.mult, op1=ALU.max), reads=[Rxn[0], ov], writes=[Rxn[0]])
                fw.op(act, lambda h=h: A.activation(out=t2, in_=t2, func=AF.Exp, scale=lgt[:, 4 + h:5 + h]), reads=[Rxn[0], Rdec, ov], writes=[Rxn[0]])
                fw.op(dve, lambda: V.tensor_single_scalar(out=mk1, in_=dist, scalar=0.0, op=ALU.is_le), reads=[Rxn[0], ov], writes=[Rxn[1]])
                fw.op(dve, lambda: V.tensor_tensor(out=t2, in0=t2, in1=mk1, op=ALU.mult), reads=[Rxn[0], Rxn[1], ov], writes=[Rxn[0]])
                fw.op(dve, lambda h=h: V.scalar_tensor_tensor(out=Mcomb[:, h * 128:(h + 1) * 128], in0=t1, scalar=1.0, in1=t2, op0=ALU.mult, op1=ALU.add),
                      reads=[Rxn[0], ov], writes=[Rdec])
                fw.op(dve, lambda: V.tensor_single_scalar(out=t1, in_=ci, scalar=1.0, op=ALU.add), reads=[Rxn[0], ov], writes=[Rxn[0]])
                fw.op(act, lambda h=h: A.activation(out=Qf[:, h, 0:128], in_=t1, func=AF.Exp, scale=lgt[:, h:h + 1]), reads=[Rxn[0], Rdec, ov], writes=[Rdec])
                fw.op(dve, lambda: V.tensor_scalar(out=t2, in0=ci, scalar1=-1.0, scalar2=128.0, op0=ALU.mult, op1=ALU.add), reads=[Rxn[0], ov], writes=[Rxn[0]])
                fw.op(act, lambda h=h: A.activation(out=Qb[:, h, 0:128], in_=t2, func=AF.Exp, scale=lgt[:, 4 + h:5 + h]), reads=[Rxn[0], Rdec, ov], writes=[Rdec])
            fw.op(dve, lambda: V.tensor_single_scalar(out=Mcomb, in_=Mcomb, scalar=QSCALE, op=ALU.mult), reads=[Rdec, ov], writes=[Rdec])
            fw.op(dve, lambda: V.tensor_single_scalar(out=Qf[:, :, 0:128], in_=Qf[:, :, 0:128], scalar=QSCALE, op=ALU.mult), reads=[Rdec, ov], writes=[Rdec])
            fw.op(dve, lambda: V.tensor_single_scalar(out=Qb[:, :, 0:128], in_=Qb[:, :, 0:128], scalar=QSCALE, op=ALU.mult), reads=[Rdec, ov], writes=[Rdec])
            fw.op(dve, lambda: V.tensor_copy(out=Qf[:, :, 128:256], in_=Qf[:, :, 0:128]), reads=[Rdec, ov], writes=[Rdec])
            fw.op(dve, lambda: V.tensor_copy(out=Qb[:, :, 128:256], in_=Qb[:, :, 0:128]), reads=[Rdec, ov], writes=[Rdec])
            fw.dma(sp, cwraw[0:CK, 1, :], conv_w[l, :, :], reads=[ov], writes=[Rxn[1]], key="cwr")

            def trw():
                for cc in range(4):
                    i_ = T.transpose(out=banks[0][:, cc * 32:cc * 32 + CK], in_=cwraw[0:CK, 1, cc * 128:(cc + 1) * 128], identity=ident[0:CK, 0:CK])
                return i_
            fw.op(pe, trw, reads=[Rxn[1], Rident, ov], writes=[Rb[0]])
            fw.op(dve, lambda: V.tensor_copy(out=cw[:], in_=banks[0][:, 0:128].rearrange("p (c k) -> p c k", k=32)[:, :, 0:CK]), reads=[Rb[0]], writes=[Rcw])
            for vi, vec in enumerate((conv_b, conv_ln_g, conv_ln_b)):
                fw.dma(sp, cvec[:, vi, :], vec[l, :].rearrange("(c p) -> p c", p=128), writes=[Rcw], key=("cv", vi))
            fw.op(pool, lambda: G.memset(upb, 0.0), reads=[ov], writes=[Rup])

            def mkd():
                for cc in range(4):
                    for kk in range(CK):
                        i_ = V.tensor_scalar(out=Dblk[:, cc, kk, :], in0=dmask[:], scalar1=cw[:, cc, kk:kk + 1], scalar2=None, op0=ALU.mult)
                return i_
            fw.op(dve, mkd, reads=[Rcw, Rdm, ov], writes=[Rdb])

        def proj_fm(tl, m, outT, Rout, versions=None, bk0=2):
            isS = tl["kind"] == "S"
            b = tl["idx"] % 2
            for h in range(4):
                bk = bk0 + (h % 2)
                nparts = 2 if isS else 1

                def mm(h=h, bk=bk, nparts=nparts):
                    for part in range(nparts):
                        wv = WIN(m if part == 0 else m + 4)
                        for kc in range(8):
                            i_ = T.matmul(banks[bk][:, part * 256:(part + 1) * 256], lhsT=wv[:, kc, h * 128:(h + 1) * 128], rhs=hT[:, b, kc, :],
                                          start=(kc == 0), stop=(kc == 7))
                    return i_
                rd = [slot[2 * m], slot[2 * m + 1], RhT[b]] + ([slot[2 * m + 8], slot[2 * m + 9]] if isS else [])
                fw.op(pe, mm, reads=rd, writes=[Rb[bk]])
                r = h % 2
                if isS:
                    fw.op(dve, lambda bk=bk, r=r: V.tensor_tensor(out=rt1[:, r, :], in0=banks[bk][:, 0:256], in1=ropeCS[:, b, 0, :], op=ALU.mult),
                          reads=[Rb[bk], Rcs[b], ov], writes=[Rrt[r]])
                    fw.op(dve, lambda bk=bk, r=r: V.tensor_tensor(out=rt2[:, r, :], in0=banks[bk][:, 256:512], in1=ropeCS[:, b, 1, :], op=ALU.mult),
                          reads=[Rb[bk], Rcs[b], ov], writes=[Rrt[r]])
                    fw.op(pool, lambda r=r: G.tensor_tensor(out=rt1[:, r, :], in0=rt1[:, r, :], in1=rt2[:, r, :], op=ALU.add),
                          reads=[Rrt[r], ov], writes=[Rrt[r]])
                    srcv, rsrc = rt1[:, r, :], Rrt[r]
                else:
                    srcv, rsrc = banks[bk][:, 0:256], Rb[bk]
                fw.op(act, lambda h=h, srcv=srcv: A.copy(out=outT[:, h, :], in_=srcv), reads=[rsrc, ov], writes=[Rout])
                if versions:
                    for (Qt, dstT) in versions:
                        if isS:
                            fw.op(pool, lambda h=h, srcv=srcv, Qt=Qt, dstT=dstT: G.tensor_tensor(out=dstT[:, h, :], in0=srcv, in1=Qt[:, h, :], op=ALU.mult),
                                  reads=[rsrc, Rdec, ov], writes=[Rout])
                        else:
                            fw.op(dve, lambda h=h, srcv=srcv, Qt=Qt, dstT=dstT: V.tensor_tensor(out=dstT[:, h, :], in0=srcv, in1=Qt[:, h, :], op=ALU.mult),
                                  reads=[rsrc, Rdec, ov], writes=[Rout])

        def proj_tm(s, m, bk, b):
            def mm():
                for kc in range(8):
                    i_ = T.matmul(banks[bk][:, :], lhsT=hT[:, b, kc, s * 128:(s + 1) * 128], rhs=WIN(m)[:, kc, :], start=(kc == 0), stop=(kc == 7))
                return i_
            fw.op(pe, mm, reads=[slot[2 * m], slot[2 * m + 1], RhT[b]], writes=[Rb[bk]])

        def k_tm(s, kd, bk, ki):
            pb = banks[bk][:, 0:256].bitcast(BF16)

            def tr():
                for h in range(4):
                    i_ = T.transpose(out=pb[:, h * 128:(h + 1) * 128], in_=kT[:, h, s * 128:(s + 1) * 128], identity=identb[:])
                return i_
            fw.op(pe, tr, reads=[RkT, Rmisc, ov], writes=[Rb[bk]])

            def ev():
                for h in range(4):
                    i_ = A.activation(out=kdT[:, ki, h * 128:(h + 1) * 128], in_=pb[:, h * 128:(h + 1) * 128], func=AF.Copy, scale=kd[:, h:h + 1])
                return i_
            fw.op(act, ev, reads=[Rb[bk], Rdec, ov], writes=[RkdT[ki]])

        def kv_mm(s, bk, ki):
            def mm():
                for h in range(4):
                    i_ = T.matmul(banks[bk][:, h * 128:(h + 1) * 128], lhsT=kdT[:, ki, h * 128:(h + 1) * 128], rhs=vbf[:, s, h * 128:(h + 1) * 128],
                                  start=True, stop=True)
                return i_
            fw.op(pe, mm, reads=[RkdT[ki], Rvbf, ov], writes=[Rb[bk]])

        def state_update(Rt, RRt, bk, goff):
            def up():
                for h in range(4):
                    i_ = V.scalar_tensor_tensor(out=Rt[:, h * 128:(h + 1) * 128], in0=Rt[:, h * 128:(h + 1) * 128], scalar=gC[:, goff + h:goff + h + 1],
                                                in1=banks[bk][:, h * 128:(h + 1) * 128], op0=ALU.mult, op1=ALU.add)
                return i_
            fw.op(dve, up, reads=[Rb[bk], Rdec, ov, RRt], writes=[RRt])

        def load_cs(tl):
            if tl["kind"] == "S":
                b = tl["idx"] % 2
                for c in range(2):
                    fw.dma(sp, ropeCS[:, b, c, :], rope[c, :, tl["tok0"]:tl["tok0"] + 256], reads=[Rrope, ov], writes=[Rcs[b]], key=("cs", b, c))

        def stage_KB(l, src, have_w=False):
            claim(extra=[slot[i] for i in range(20, 33)])
            load_mixer(l, have_w)
            layer_consts(l)
            stage_consts(l, 1, False)
            order = list(reversed(tiles))

            def kb_pk(tl):
                proj_fm(tl, 3, kT, RkT)

            def kb_pv(tl):
                b = tl["idx"] % 2
                for s in range(2):
                    proj_tm(s, 4, 4 + s, b)
                    fw.op(act, lambda s=s: A.copy(out=vbf[:, s, :], in_=banks[4 + s][:, :]), reads=[Rb[4 + s], ov], writes=[Rvbf])

            def kb_ld(tl):
                load_x(src, tl, tl["idx"] % 2)
                load_cs(tl)

            kb_ld(order[0])
            if NT > 1:
                kb_ld(order[1])
            prologue(order[0], order[0]["idx"] % 2)
            kb_pk(order[0])
            kb_pv(order[0])
            if NT > 1:
                prologue(order[1], order[1]["idx"] % 2)
            for oi, tl in enumerate(order):
                b = tl["idx"] % 2
                isS = tl["kind"] == "S"
                if oi + 2 < NT:
                    kb_ld(order[oi + 2])
                k_tm(1, kdb, 7, 1)
                k_tm(0, kdb, 6, 0)
                if oi + 1 < NT:
                    kb_pk(order[oi + 1])
                last_tile_of_seq = (not isS) or (tl["idx"] == ST // 256 - 1)
                if last_tile_of_seq:
                    if isS:
                        fw.dma(sp, Rbs.rearrange("p (h v) -> p h v", h=4), stb[l].rearrange("h d v -> d h v"), reads=[ov], writes=[RRbs], key="rbs")
                    else:
                        fw.op(pool, lambda: G.memset(Rbs, 0.0), reads=[ov], writes=[RRbs])
                for s in (1, 0):
                    ch = tl["tok0"] // 128 + s
                    kv_mm(s, 6 + s, s)
                    ri = ch % 2
                    fw.op(act, lambda ri=ri: A.copy(out=rbt[:, ri, 0, :], in_=Rbs), reads=[RRbs, ov], writes=[Rrbt[ri]])
                    fw.dma(sp, rbd[ch], rbt[:, ri, 0, :], reads=[Rrbt[ri]], writes=[Rrbd[ch]], key=("rbo", ri))
                    state_update(Rbs, RRbs, 6 + s, 4)
                if not isS:
                    fw.dma(sp, nsb[tl["seq"], l].rearrange("h d v -> d h v"), Rbs.rearrange("p (h v) -> p h v", h=4), reads=[RRbs, ov], key="nsb")
                if oi + 1 < NT:
                    kb_pv(order[oi + 1])
                if oi + 2 < NT:
                    prologue(order[oi + 2], order[oi + 2]["idx"] % 2)

        def stage_M(l, src, dst):
            stage_consts(l, 1, False)
            load_x(src, tiles[0], 0)
            load_cs(tiles[0])
            prologue(tiles[0], 0)
            zi = [0]

            def conv_in(tl, b):
                isS = tl["kind"] == "S"
                for cc in range(4):
                    bk = 6 + cc % 2

                    def mmc(cc=cc, bk=bk, b=b):
                        for part in range(2):
                            for kc in range(8):
                                i_ = T.matmul(banks[bk][:, part * 256:(part + 1) * 256], lhsT=WIN(part)[:, kc, cc * 128:(cc + 1) * 128], rhs=hT[:, b, kc, :],
                                              start=(kc == 0), stop=(kc == 7))
                        return i_
                    fw.op(pe, mmc, reads=[slot[0], slot[1], slot[2], slot[3], RhT[b]], writes=[Rb[bk]])
                    r = cc % 2
                    fw.op(act, lambda bk=bk, r=r: A.activation(out=sgm[:, r, :], in_=banks[bk][:, 256:512], func=AF.Sigmoid), reads=[Rb[bk], ov], writes=[Rsgm[r]])
                    if isS:
                        outv = upS[:, cc, :, 15:79]
                        in0 = banks[bk][:, 0:256].rearrange("p (r w) -> p r w", w=64)
                        in1 = sgm[:, r, :].rearrange("p (r w) -> p r w", w=64)
                    else:
                        outv = upP[:, cc, 15:271]
                        in0 = banks[bk][:, 0:256]
                        in1 = sgm[:, r, :]
                    fw.op(dve, lambda outv=outv, in0=in0, in1=in1: V.tensor_tensor(out=outv, in0=in0, in1=in1, op=ALU.mult),
                          reads=[Rb[bk], Rsgm[r], ov], writes=[Rup])

            CBK = [0, 1, 6, 7]

            def conv_mm(tl, ccs):
                isS = tl["kind"] == "S"

                def mmcv():
                    for cc in ccs:
                        for kk in range(CK):
                            for g in range(4):
                                ps = slice(32 * g, 32 * g + 32)
                                bkc = CBK[(g + cc) % 4]
                                if isS:
                                    ov_ = banks[bkc][ps, 0:256].rearrange("p (r w) -> p r w", w=64)
                                    rh = upS[ps, cc, :, kk:kk + 64]
                                else:
                                    ov_ = banks[bkc][ps, 0:256]
                                    rh = upP[ps, cc, kk:kk + 256]
                                i_ = T.matmul(ov_, lhsT=Dblk[ps, cc, kk, :], rhs=rh, start=(kk == 0), stop=(kk == CK - 1),
                                              tile_position=(32 * g, 32 * g))
                    return i_
                fw.op(pe, mmcv, reads=[Rup, Rdb, ov], writes=[Rb[q_] for q_ in CBK])

            def conv_evac():
                for cc in range(4):
                    def evc(cc=cc):
                        for g in range(4):
                            ps = slice(32 * g, 32 * g + 32)
                            i_ = V.tensor_scalar(out=acc[ps, cc, :], in0=banks[CBK[(g + cc) % 4]][ps, 0:256], scalar1=cvec[ps, 0, cc:cc + 1], scalar2=None, op0=ALU.add)
                        return i_
                    fw.op(dve, evc, reads=[Rb[q_] for q_ in CBK] + [Rcw, ov], writes=[Racc[cc]])

            def conv_ln_a():
                for s in range(2):
                    bk = 4 + s

                    def trc(s=s, bk=bk):
                        for cc in range(4):
                            i_ = T.transpose(out=banks[bk][:, cc * 128:(cc + 1) * 128], in_=acc[:, cc, s * 128:(s + 1) * 128], identity=ident[:])
                        return i_
                    fw.op(pe, trc, reads=Racc + [Rident, ov], writes=[Rb[bk]])
                for s in range(2):
                    bk = 4 + s
                    ln_rows(banks[bk], Rb[bk], 2 + s, 512)
                    fw.op(act, lambda s=s, bk=bk: A.activation(out=xn[:, s, :], in_=banks[bk][:, :], func=AF.Identity, scale=rstd[:, 2 + s, 0:1], bias=rstd[:, 2 + s, 1:2]),
                          reads=[Rb[bk], Rst[2 + s], ov], writes=[Rxn[s]])

            def conv_ln_b():
                for pr in range(2):
                    bk = 4 + pr

                    def trb(pr=pr, bk=bk):
                        for c2 in range(2):
                            for s in range(2):
                                cc = pr * 2 + c2
                                i_ = T.transpose(out=banks[bk][:, c2 * 256 + s * 128:c2 * 256 + (s + 1) * 128], in_=xn[:, s, cc * 128:(cc + 1) * 128], identity=ident[:])
                        return i_
                    fw.op(pe, trb, reads=[Rxn[0], Rxn[1], Rident, ov], writes=[Rb[bk]])

                    def evb(pr=pr, bk=bk):
                        for c2 in range(2):
                            cc = pr * 2 + c2
                            i_ = A.activation(out=mixT[:, cc, :], in_=banks[bk][:, c2 * 256:(c2 + 1) * 256], func=AF.Silu, scale=cvec[:, 1, cc:cc + 1], bias=cvec[:, 2, cc:cc + 1])
                        return i_
                    fw.op(act, evb, reads=[Rb[bk], Rcw, ov], writes=[Rmix])

            def ret_rfb(tl, s):
                fi = (tl["tok0"] // 128 + s) % 2
                fw.op(act, lambda fi=fi: A.copy(out=Rfb[:, fi, :], in_=Rf), reads=[RRf, ov], writes=[RRfb[fi]])

            def ret_scores(s, bk):
                def mms(s=s, bk=bk):
                    for h in range(4):
                        i_ = T.matmul(banks[bk][:, h * 128:(h + 1) * 128], lhsT=kT[:, h, s * 128:(s + 1) * 128], rhs=qT[:, h, s * 128:(s + 1) * 128], start=True, stop=True)
                    return i_
                fw.op(pe, mms, reads=[RkT, RqT, ov], writes=[Rb[bk]])
                fw.op(dve, lambda s=s, bk=bk: V.tensor_tensor(out=Pm[:, s, :], in0=banks[bk][:, :], in1=Mcomb, op=ALU.mult), reads=[Rb[bk], Rdec, ov], writes=[RPm[s]])

            def ret_o(tl, b, s, bk):
                fi = (tl["tok0"] // 128 + s) % 2

                def mmo(s=s, fi=fi, b=b, bk=bk):
                    for h in range(4):
                        hs = slice(h * 128, (h + 1) * 128)
                        T.matmul(banks[bk][:, hs], lhsT=Pm[:, s, hs], rhs=vbf[:, s, hs], start=True, stop=False)
                        T.matmul(banks[bk][:, hs], lhsT=qfT[:, h, s * 128:(s + 1) * 128], rhs=Rfb[:, fi, hs], start=False, stop=False)
                        i_ = T.matmul(banks[bk][:, hs], lhsT=qbT[:, h, s * 128:(s + 1) * 128], rhs=rbt[:, b, s, hs], start=False, stop=True)
                    return i_
                fw.op(pe, mmo, reads=[RPm[s], Rvbf, RqT, RRfb[fi], Rrbt[b], ov], writes=[Rb[bk]])

            def gn_pre(s, bko):
                gi = s

                def gbs(gi=gi):
                    for h in range(4):
                        i_ = V.bn_stats(out=gst[:, gi, h * 6:(h + 1) * 6], in_=banks[bko][:, h * 128:(h + 1) * 128])
                    return i_
                fw.op(dve, gbs, reads=[Rb[bko]], writes=[Rgst[gi]])

                def gag(gi=gi):
                    for h in range(4):
                        i_ = V.bn_aggr(out=gmv[:, gi, h * 2:(h + 1) * 2], in_=gst[:, gi, h * 6:(h + 1) * 6])
                    return i_
                fw.op(dve, gag, reads=[Rgst[gi]], writes=[Rgst[gi]])
                gm = gmv[:, gi, :].rearrange("p (h two) -> p h two", two=2)
                fw.op(act, lambda gi=gi, gm=gm: A.activation(out=grs[:, gi, 0:4], in_=gm[:, :, 1], func=AF.Sqrt, bias=epst[:, 0:1]), reads=[Rgst[gi], Rmisc], writes=[Rgst[gi]])
                fw.op(dve, lambda gi=gi: V.reciprocal(out=grs[:, gi, 0:4], in_=grs[:, gi, 0:4]), reads=[Rgst[gi]], writes=[Rgst[gi]])
                fw.op(dve, lambda gi=gi, gm=gm: V.scalar_tensor_tensor(out=grs[:, gi, 4:8], in0=gm[:, :, 0], scalar=-1.0, in1=grs[:, gi, 0:4], op0=ALU.mult, op1=ALU.mult),
                      reads=[Rgst[gi]], writes=[Rgst[gi]])
                onv, ogv = onb[s], ogb[s]

                def gev(gi=gi, onv=onv):
                    for h in range(4):
                        i_ = A.activation(out=onv[:, h * 128:(h + 1) * 128], in_=banks[bko][:, h * 128:(h + 1) * 128], func=AF.Identity,
                                          scale=grs[:, gi, h:h + 1], bias=grs[:, gi, 4 + h:5 + h])
                    return i_
                fw.op(act, gev, reads=[Rb[bko], Rgst[gi], ov], writes=[Ron[s]])
                fw.op(pool, lambda s=s, onv=onv, ogv=ogv: G.tensor_tensor(out=ogv, in0=onv, in1=sg[:, s, :], op=ALU.mult), reads=[Ron[s], Rsg, ov], writes=[Rog[s]])

            def gn_post(s, bkt):
                ogv = ogb[s]
                pb = banks[bkt][:, 0:256].bitcast(BF16)

                def tro(pb=pb, ogv=ogv):
                    for h in range(4):
                        i_ = T.transpose(out=pb[:, h * 128:(h + 1) * 128], in_=ogv[:, h * 128:(h + 1) * 128], identity=identb[:])
                    return i_
                fw.op(pe, tro, reads=[Rog[s], Rmisc, ov], writes=[Rb[bkt]])
                fw.op(act, lambda s=s, pb=pb: A.copy(out=mixT[:, 4:8, s * 128:(s + 1) * 128], in_=pb.rearrange("p (h t) -> p h t", t=128)),
                      reads=[Rb[bkt], ov], writes=[Rmix])

            def head1(tl):
                b = tl["idx"] % 2
                isS = tl["kind"] == "S"
                if (not isS) and tiles[tl["idx"] - 1]["kind"] == "S":
                    fw.op(pool, lambda: G.memset(upb, 0.0), reads=[ov, Rup], writes=[Rup])
                conv_in(tl, b)
                proj_fm(tl, 2, qT, RqT, versions=[(Qf, qfT), (Qb, qbT)], bk0=0)
                proj_fm(tl, 3, kT, RkT, bk0=0)

            def head2(tl):
                b = tl["idx"] % 2
                for s in range(2):
                    proj_tm(s, 4, 0, b)
                    fw.op(act, lambda s=s: A.copy(out=vbf[:, s, :], in_=banks[0][:, :]), reads=[Rb[0], ov], writes=[Rvbf])
                    proj_tm(s, 5, 1, b)
                    fw.op(act, lambda s=s: A.activation(out=sg[:, s, :], in_=banks[1][:, :], func=AF.Silu), reads=[Rb[1], ov], writes=[Rsg])
                conv_mm(tl, (0, 1, 2, 3))
                conv_evac()

            def rbt_load(tl):
                b = tl["idx"] % 2
                for s in range(2):
                    ch = tl["tok0"] // 128 + s
                    fw.dma(sp, rbt[:, b, s, :], rbd[ch], reads=[Rrbd[ch]], writes=[Rrbt[b]], key=("rbi", b, s))

            def tailA1(tl):
                isS = tl["kind"] == "S"
                if isS and tl["first"]:
                    fw.dma(sp, Rf.rearrange("p (h v) -> p h v", h=4), stf[l].rearrange("h d v -> d h v"), reads=[ov], writes=[RRf], key="rfs")
                elif not isS:
                    fw.op(pool, lambda: G.memset(Rf, 0.0), reads=[ov], writes=[RRf])
                ret_rfb(tl, 0)
                ret_scores(0, 2)
                ret_scores(1, 3)
                k_tm(0, kdf, 4, 0)
                k_tm(1, kdf, 5, 1)

            def warm(bk, n):
                def f(bk=bk, n=n):
                    for _ in range(n):
                        i_ = T.matmul(banks[bk][:, :], lhsT=identb[:], rhs=kT[:, 0:2, :], start=True, stop=True)
                    return i_
                fw.op(pe, f, reads=[RkT, Rmisc, ov], writes=[Rb[bk]])

            rbt_load(tiles[0])
            if NT > 1:
                load_x(src, tiles[1], 1)
                load_cs(tiles[1])
            head1(tiles[0])
            head2(tiles[0])
            tailA1(tiles[0])
            for ti, tl in enumerate(tiles):
                b = ti % 2
                isS = tl["kind"] == "S"
                if ti + 1 < NT:
                    rbt_load(tiles[ti + 1])
                warm(2, 28)
                kv_mm(0, 4, 0)
                kv_mm(1, 5, 1)
                ret_o(tl, b, 0, 2)
                state_update(Rf, RRf, 4, 0)
                ret_rfb(tl, 1)
                warm(3, 16)
                ret_o(tl, b, 1, 3)
                state_update(Rf, RRf, 5, 0)
                if not isS:
                    fw.dma(sp, nsf[tl["seq"], l].rearrange("h d v -> d h v"), Rf.rearrange("p (h v) -> p h v", h=4), reads=[RRf, ov], key="nsf")
                conv_ln_a()
                gn_pre(0, 2)
                gn_pre(1, 3)
                if ti + 1 < NT:
                    prologue(tiles[ti + 1], 1 - b)
                    head1(tiles[ti + 1])
                    head2(tiles[ti + 1])
                conv_ln_b()
                gn_post(0, 2)
                gn_post(1, 3)
                if ti + 1 < NT:
                    tailA1(tiles[ti + 1])
                for s in range(2):
                    yb = (6, 7) if s == 0 else (0, 1)
                    for hf in range(2):
                        bk = yb[hf]

                        def mmw(s=s, hf=hf, bk=bk):
                            for kc in range(8):
                                i_ = T.matmul(banks[bk][:, :], lhsT=mixT[:, kc, s * 128:(s + 1) * 128], rhs=WIN(8 + hf)[:, kc, :], start=(kc == 0), stop=(kc == 7))
                            return i_
                        fw.op(pe, mmw, reads=[Rmix, ov, slot[16 + 2 * hf], slot[17 + 2 * hf]], writes=[Rb[bk]])
                for s in range(2):
                    yb = (6, 7) if s == 0 else (0, 1)
                    epilogue(tl, b, s, yb, dst, zi[0])
                    zi[0] = 1 - zi[0]
                if ti + 2 < NT:
                    load_x(src, tiles[ti + 2], b)
                    load_cs(tiles[ti + 2])

        seq = []
        for l in range(NL):
            seq += [(l, "F0"), (l, "KB"), (l, "M"), (l, "F1")]
        if stages is not None:
            seq = [s_ for s_ in seq if s_ in stages]
        n_xstage = sum(1 for s_ in seq if s_[1] != "KB")
        xi = 0
        cur = xin
        have = False
        for si, (l, nm) in enumerate(seq):
            nx = seq[si + 1] if si + 1 < len(seq) else None
            if nm == "KB":
                stage_KB(l, cur, have_w=have)
                have = False
                continue
            xi += 1
            dst = yout if xi == n_xstage else xs
            if nm in ("F0", "F1"):
                nxt = None
                if PREFETCH and nx is not None:
                    if nx[1] == "KB":
                        nxt = ("M", nx[0])
                    elif nx[1] in ("F0", "F1"):
                        nxt = ("F", nx[0], 0 if nx[1] == "F0" else 1)
                stage_F(l, 0 if nm == "F0" else 1, cur, dst, have_w=have, nxt=nxt)
                have = nxt is not None
            else:
                stage_M(l, cur, dst)
                have = False
            cur = dst
        fw.finish(sp)
        fw.run()
    return nc


_W_KEYS = ["w_ada", "b_ada", "ln_g", "ln_b", "ffn_w1", "ffn_w2", "w_in", "w_out", "conv_w", "conv_b", "conv_ln_g", "conv_ln_b"]


def kernel(x_prompt, x_sample, state_ret_fwd, state_ret_bwd, c, c_ctx, w_ada, b_ada, ln_g, ln_b,
           ffn_w1, ffn_w2, w_in, w_out, conv_w, conv_b, conv_ln_g, conv_ln_b, ret_decay_logit):
    f = lambda a: np.ascontiguousarray(np.asarray(a, dtype=np.float32))
    NCORE = 8
    NL = w_in.shape[0]
    B, S = x_prompt.shape[0], x_prompt.shape[1]
    DB, ST = x_sample.shape[0], x_sample.shape[1]
    NP = B // NCORE
    assert DB == NCORE and S == 256
    nc = build(NL=NL, ST=ST, NP=NP)
    wts = dict(w_ada=f(w_ada), b_ada=f(b_ada), ln_g=f(ln_g), ln_b=f(ln_b), ffn_w1=f(ffn_w1), ffn_w2=f(ffn_w2), w_in=f(w_in),
               w_out=f(w_out), conv_w=f(conv_w), conv_b=f(conv_b), conv_ln_g=f(conv_ln_g), conv_ln_b=f(conv_ln_b),
               dlogit=f(ret_decay_logit).reshape(NL, 8))
    in_maps = []
    for i in range(NCORE):
        m = dict(wts)
        m["xin"] = np.concatenate([f(x_sample[i]), f(x_prompt[i * NP:(i + 1) * NP]).reshape(NP * S, D)], axis=0)
        m["stf"] = f(state_ret_fwd[i])
        m["stb"] = f(state_ret_bwd[i])
        m["cond"] = np.stack([f(c[i]), f(c_ctx)], axis=0)
        in_maps.append(m)
    res = run_bass_kernel_spmd(nc, in_maps, core_ids=list(range(NCORE)))
    outs = res.results
    y_sample = np.stack([outs[i]["yout"][:ST] for i in range(NCORE)], axis=0)
    y_prompt = np.concatenate([outs[i]["yout"][ST:].reshape(NP, S, D) for i in range(NCORE)], axis=0)
    nf = np.concatenate([outs[i]["nsf"] for i in range(NCORE)], axis=0)
    nb = np.concatenate([outs[i]["nsb"] for i in range(NCORE)], axis=0)
    return (y_prompt.astype(np.float32), y_sample.astype(np.float32), nf.astype(np.float32), nb.astype(np.float32))
```

```python
import contextlib
import math
import numpy as np
import concourse.bass as bass
import concourse.mybir as mybir
from concourse.bass_utils import run_bass_kernel_spmd

F32 = mybir.dt.float32
BF16 = mybir.dt.bfloat16
AF = mybir.ActivationFunctionType
ALU = mybir.AluOpType

D = 1024
DFF = 2816
NHC = 22
DEPTH = 4
ALPHA = (2.0 * DEPTH) ** 0.25
EPS = 1e-5
CK = 31
QSCALE = 128.0 ** -0.5
TWO_PI = 2.0 * math.pi
PREFETCH = True


class Res:
    __slots__ = ("name", "w", "rs")

    def __init__(self, name):
        self.name = name
        self.w = None
        self.rs = {}


class Eng:
    def __init__(self, name, h, sem, is_pe=False):
        self.name = name
        self.h = h
        self.sem = sem
        self.n = 0
        self.seen = {}
        self.is_pe = is_pe
        self.prog = []


class FW:
    def __init__(self, nc, stack):
        self.nc = nc
        self.stack = stack
        mk = lambda nm, h, pe=False: Eng(nm, h, stack.enter_context(nc.semaphore("s_" + nm)), pe)
        self.pe = mk("pe", nc.tensor, True)
        self.act = mk("act", nc.scalar)
        self.dve = mk("dve", nc.vector)
        self.pool = mk("pool", nc.gpsimd)
        self.sp = mk("sp", nc.sync)
        self.dma_sems = {}
        self.nres = 0

    def res(self, name=None):
        self.nres += 1
        return Res(name or f"r{self.nres}")

    def _wait(self, eng, dep):
        key, (sem, idx, is_eng) = dep
        if is_eng and key == eng.name and eng.is_pe:
            return
        if eng.seen.get(key, 0) >= idx:
            return
        eng.prog.append(lambda h=eng.h, s=sem, v=idx: h.wait_ge(s, v))
        eng.seen[key] = idx

    def _deps(self, reads, writes):
        deps = []
        for r in reads:
            if r.w is not None:
                deps.append(r.w)
        for w in writes:
            if w.w is not None:
                deps.append(w.w)
            deps.extend(w.rs.items())
        return deps

    def _mark(self, me, reads, writes):
        key, val = me
        for r in reads:
            r.rs[key] = val
        for w in writes:
            w.w = me
            w.rs = {}

    def op(self, eng, fn, reads=(), writes=()):
        for d in self._deps(reads, writes):
            self._wait(eng, d)
        eng.n += 1
        eng.prog.append(lambda fn=fn, s=eng.sem: fn().then_inc(s, 1))
        me = (eng.name, (eng.sem, eng.n, True))
        self._mark(me, reads, writes)
        return me

    def dma(self, qeng, out, in_, reads=(), writes=(), key=None):
        for d in self._deps(reads, writes):
            self._wait(qeng, d)
        if key not in self.dma_sems:
            self.dma_sems[key] = [self.stack.enter_context(self.nc.semaphore("d_%d" % len(self.dma_sems))), 0]
        ent = self.dma_sems[key]
        ent[1] += 16
        qeng.prog.append(lambda h=qeng.h, o=out, i=in_, s=ent[0]: h.dma_start(out=o, in_=i).then_inc(s, 16))
        me = (("dma", key), (ent[0], ent[1], False))
        self._mark(me, reads, writes)
        return me

    def finish(self, eng):
        for key, (sem, val) in self.dma_sems.items():
            if val:
                eng.prog.append(lambda h=eng.h, s=sem, v=val: h.wait_ge(s, v))
        for e2 in (self.pe, self.act, self.dve, self.pool, self.sp):
            if e2 is not eng and e2.n:
                eng.prog.append(lambda h=eng.h, s=e2.sem, v=e2.n: h.wait_ge(s, v))

    def run(self):
        with self.nc.Block() as block:
            @block.tensor
            def _(e):
                for t in self.pe.prog:
                    t()

            @block.scalar
            def _(e):
                for t in self.act.prog:
                    t()

            @block.vector
            def _(e):
                for t in self.dve.prog:
                    t()

            @block.gpsimd
            def _(e):
                for t in self.pool.prog:
                    t()

            @block.sync
            def _(e):
                for t in self.sp.prog:
                    t()


def build(NL=4, ST=4096, NP=4, stages=None, dbg=False):
    NTOK = ST + NP * 256
    NCH = NTOK // 128
    nc = bass.Bass("TRN2", target_bir_lowering=False)
    din = lambda n, s, dt=F32: nc.dram_tensor(n, s, dt, kind="ExternalInput").ap()
    dout = lambda n, s: nc.dram_tensor(n, s, F32, kind="ExternalOutput").ap()
    dint = lambda n, s, dt=F32: nc.dram_tensor(n, s, dt, kind="Internal").ap()
    xin = din("xin", [NTOK, D])
    stf = din("stf", [NL, 4, 128, 128])
    stb = din("stb", [NL, 4, 128, 128])
    cond = din("cond", [2, D])
    w_ada = din("w_ada", [NL, D, 9 * D])
    b_ada = din("b_ada", [NL, 9 * D])
    ln_g = din("ln_g", [NL, 3, D])
    ln_b = din("ln_b", [NL, 3, D])
    ffn_w1 = din("ffn_w1", [NL, 2, D, 2 * DFF])
    ffn_w2 = din("ffn_w2", [NL, 2, DFF, D])
    w_in = din("w_in", [NL, D, 3072])
    w_out = din("w_out", [NL, D, D])
    conv_w = din("conv_w", [NL, CK, 512])
    conv_b = din("conv_b", [NL, 512])
    conv_ln_g = din("conv_ln_g", [NL, 512])
    conv_ln_b = din("conv_ln_b", [NL, 512])
    dlogit = din("dlogit", [NL, 8])
    yout = dout("yout", [NTOK, D])
    nsf = dout("nsf", [max(NP, 1), NL, 4, 128, 128])
    nsb = dout("nsb", [max(NP, 1), NL, 4, 128, 128])
    xs = dint("xs", [NTOK, D])
    mods = dint("mods", [NL, 2, 9 * D])
    rope = (dout if dbg else dint)("rope", [2, 128, max(ST, 256)])
    rbd = dint("rbd", [NCH, 128, 512], BF16)

    tiles = []
    for t in range(ST // 256):
        tiles.append(dict(kind="S", tok0=t * 256, cond=0, first=(t == 0), idx=len(tiles)))
    for n in range(NP):
        tiles.append(dict(kind="P", tok0=ST + n * 256, cond=1, seq=n, idx=len(tiles)))
    NT = len(tiles)

    with contextlib.ExitStack() as st, nc.allow_non_contiguous_dma(reason="small vector gathers"):
        fw = FW(nc, st)
        pe, act, dve, pool, sp = fw.pe, fw.act, fw.dve, fw.pool, fw.sp
        sbt = lambda n, s, dt=F32: st.enter_context(nc.sbuf_tensor(n, s, dt))
        R = fw.res

        arena = sbt("arena", [128, 33 * 1024])
        aux = sbt("aux", [128, 4864])
        xbuf = sbt("xbuf", [128, 2, 2, D])
        hT = sbt("hT", [128, 2, 8, 256], BF16)
        zb = sbt("zb", [128, 2, D])
        gate = sbt("gate", [128, 2, D])
        lng = sbt("lng", [128, D])
        lnb = sbt("lnb", [128, D])
        rbt = sbt("rbt", [128, 2, 2, 512], BF16)
        ident = sbt("ident", [128, 128])
        identb = sbt("identb", [128, 128], BF16)
        sc1 = sbt("sc1", [128, 2, 8])
        shf = sbt("shf", [128, 2, 8])
        stats = sbt("stats", [128, 4, 12])
        mv = sbt("mv", [128, 4, 2])
        rstd = sbt("rstd", [128, 4, 2])
        gst = sbt("gst", [128, 2, 24])
        gmv = sbt("gmv", [128, 2, 8])
        grs = sbt("grs", [128, 2, 8])
        lgt = sbt("lgt", [128, 8])
        kdf = sbt("kdf", [128, 4])
        kdb = sbt("kdb", [128, 4])
        gC = sbt("gC", [128, 8])
        piota = sbt("piota", [128, 1])
        cw = sbt("cw", [128, 4, CK])
        cvec = sbt("cvec", [128, 3, 4])
        scT = sbt("scT", [128, 8, 2], BF16)
        condT = sbt("condT", [128, 8, 2])
        tiny = sbt("tiny", [128, 8])
        negpi = sbt("negpi", [128, 1])
        epst = sbt("epst", [128, 1])
        dmask = sbt("dmask", [128, 32])

        slot = [R(f"slot{i}") for i in range(33)]
        ov = R("ov")
        Rx = [[R(f"x{b}{s}") for s in range(2)] for b in range(2)]
        RhT = [R("hT0"), R("hT1")]
        Rzb = [R("zb0"), R("zb1")]
        Rgate, Rlng, Rlnb, Rsc = R("gate"), R("lng"), R("lnb"), R("sc")
        Rident = R("ident")
        Rst = [R(f"st{i}") for i in range(4)]
        Rgst = [R("gst0"), R("gst1")]
        Rdec = R("dec")
        Rcw = R("cw")
        Rrbt = [R("rbt0"), R("rbt1")]
        Rmods = R("mods")
        Rrope = R("rope")
        Rxs = [R(f"xs{t}") for t in range(NT)]
        Rrbd = [R(f"rbd{c}") for c in range(NCH)]
        Rmisc = R("misc")

        banks = [st.enter_context(nc.psum_tensor(f"bank{i}", [128, 512], F32)) for i in range(8)]
        Rb = [R(f"bank{i}") for i in range(8)]

        def aview(off, n, dt, pat=None, **kw):
            v = arena[:, off:off + n]
            if dt is not F32:
                v = v.bitcast(dt)
            if pat:
                v = v.rearrange(pat, **kw)
            return v

        def xview(off, n, dt, pat=None, **kw):
            v = aux[:, off:off + n]
            if dt is not F32:
                v = v.bitcast(dt)
            if pat:
                v = v.rearrange(pat, **kw)
            return v

        uT = xview(0, 2816, BF16, "p (j t) -> p j t", t=256)
        silt = xview(2816, 1024, F32, "p (a t) -> p a t", t=256)
        Qf = xview(0, 1024, F32, "p (h t) -> p h t", t=256)
        Qb = xview(1024, 1024, F32, "p (h t) -> p h t", t=256)
        upb = xview(2048, 752, BF16)
        upS = xview(2048, 752, BF16, "p (c r w) -> p c r w", r=4, w=94)
        upP = xview(2048, 572, BF16, "p (c w) -> p c w", w=286)
        Dblk = xview(2800, 1984, BF16, "p (c k j) -> p c k j", c=4, k=CK)
        o = [20 * 1024]

        def oa(n, dt, pat=None, **kw):
            v = aview(o[0], n, dt, pat, **kw)
            o[0] += n
            assert o[0] <= 33 * 1024
            return v

        Mcomb = oa(512, F32)
        qT = oa(512, BF16, "p (h t) -> p h t", t=256)
        qfT = oa(512, BF16, "p (h t) -> p h t", t=256)
        qbT = oa(512, BF16, "p (h t) -> p h t", t=256)
        kT = oa(512, BF16, "p (h t) -> p h t", t=256)
        ropeCS = oa(1024, F32, "p (b c t) -> p b c t", b=2, c=2)
        rt1 = oa(512, F32, "p (b t) -> p b t", b=2)
        rt2 = oa(512, F32, "p (b t) -> p b t", b=2)
        vbf = oa(512, BF16, "p (s c) -> p s c", s=2)
        sg = oa(1024, F32, "p (s c) -> p s c", s=2)
        sgm = oa(512, F32, "p (b t) -> p b t", b=2)
        acc = oa(1024, F32, "p (c t) -> p c t", c=4)
        xn = oa(1024, F32, "p (s c) -> p s c", s=2)
        mixT = oa(1024, BF16, "p (k t) -> p k t", t=256)
        Pm = oa(512, BF16, "p (b c) -> p b c", b=2)
        kdT = oa(512, BF16, "p (b c) -> p b c", b=2)
        Rf = oa(512, F32)
        Rbs = oa(512, F32)
        Rfb = oa(512, BF16, "p (b c) -> p b c", b=2)
        on = oa(512, F32)
        og = oa(256, BF16)
        og2 = oa(256, BF16)
        on2 = sbt("on2", [128, 512])
        onb = [on, on2[:]]
        ogb = [og, og2]
        cwraw = xn

        RqT, RkT, Rcs, Rrt, Rvbf, Rsg, Rup, Rsgm, Racc, Rxn, Rmix = (R("qT"), R("kT"), [R("cs0"), R("cs1")],
            [R("rt0"), R("rt1")], R("vbf"), R("sg"), R("up"), [R("sgm0"), R("sgm1")], [R(f"acc{i}") for i in range(4)],
            [R("xn0"), R("xn1")], R("mix"))
        RPm, RkdT, RRf, RRbs, RRfb, Ron, Rog = [R("P0"), R("P1")], [R("kd0"), R("kd1")], R("Rf"), R("Rbs"), [R("Rfb0"), R("Rfb1")], [R("on0"), R("on1")], [R("og0"), R("og1")]
        RuT, Rsil = R("uT"), [R("sil0"), R("sil1"), R("sil2"), R("sil3")]
        Rdb = R("dblk")

        V, A, G = nc.vector, nc.scalar, nc.gpsimd
        T = nc.tensor

        fw.op(pool, lambda: G.memset(ident[:], 0.0), writes=[Rident])
        fw.op(pool, lambda: G.affine_select(out=ident[:], in_=ident[:], compare_op=ALU.not_equal, fill=1.0, base=0,
                                            pattern=[[-1, 128]], channel_multiplier=1), reads=[Rident], writes=[Rident])
        fw.op(dve, lambda: V.tensor_copy(out=identb[:], in_=ident[:]), reads=[Rident], writes=[Rmisc])
        fw.op(pool, lambda: G.iota(piota[:], pattern=[[0, 1]], base=0, channel_multiplier=1,
                                   allow_small_or_imprecise_dtypes=True), writes=[Rmisc])
        fw.op(pool, lambda: G.memset(negpi[:], -math.pi), writes=[Rmisc])
        fw.op(pool, lambda: G.memset(epst[:], EPS), writes=[Rmisc])
        Rdm = R("dmask")

        fw.op(dve, lambda: V.tensor_tensor(out=dmask[:], in0=ident[:, 0:32], in1=ident[:, 32:64], op=ALU.add), reads=[Rident], writes=[Rdm])
        fw.op(dve, lambda: V.tensor_tensor(out=dmask[:], in0=dmask[:], in1=ident[:, 64:96], op=ALU.add), reads=[Rident, Rdm], writes=[Rdm])
        fw.op(dve, lambda: V.tensor_tensor(out=dmask[:], in0=dmask[:], in1=ident[:, 96:128], op=ALU.add), reads=[Rident, Rdm], writes=[Rdm])

        for c_ in range(2):
            fw.dma(sp, condT[:, :, c_], cond[c_, :].rearrange("(k p) -> p k", p=128), writes=[Rmisc], key=("c0", c_))
        fw.op(act, lambda: A.activation(out=scT[:], in_=condT[:], func=AF.Silu), reads=[Rmisc], writes=[Rsc])
        wab = arena[:, 0:6144].bitcast(BF16).rearrange("p (b k c) -> p b k c", b=3, k=8)
        badat = arena[0:2, 6144:6656]
        modrow = arena[0:2, 7168:9216].rearrange("p (m c) -> p m c", m=2)[:, :, 0:512]
        Rwab = [slot[0], slot[2], slot[4]]
        Rwab2 = [slot[1], slot[3], slot[5]]
        Rmr = [slot[7], slot[8]]
        Rbad = slot[6]
        k = 0
        for l in range(NL):
            for j in range(18):
                b3 = k % 3
                fw.dma(pool, wab[:, b3], w_ada[l, :, j * 512:(j + 1) * 512].rearrange("(k p) c -> p k c", p=128),
                       writes=[Rwab[b3], Rwab2[b3]], key=("wab", b3))
                fw.dma(sp, badat[:], b_ada[l:l + 1, j * 512:(j + 1) * 512].partition_broadcast(2), writes=[Rbad], key="bad")
                bk = Rb[k % 2]

                def mm(b3=b3, bank=banks[k % 2]):
                    for kc in range(8):
                        i = T.matmul(bank[0:2, :], lhsT=scT[:, kc, :], rhs=wab[:, b3, kc, :], start=(kc == 0), stop=(kc == 7))
                    return i
                fw.op(pe, mm, reads=[Rwab[b3], Rwab2[b3], Rsc], writes=[bk])
                m2 = k % 2
                fw.op(dve, lambda m2=m2, bank=banks[k % 2]: V.tensor_tensor(out=modrow[:, m2, :], in0=bank[0:2, :], in1=badat[:], op=ALU.add),
                      reads=[bk, Rbad], writes=[Rmr[m2]])
                fw.dma(sp, mods[l, :, j * 512:(j + 1) * 512], modrow[:, m2, :], reads=[Rmr[m2]], writes=[Rmods], key=("mr", m2))
                k += 1

        if ST > 0:
            invf = sbt("invf", [128, 1])
            MAGIC = 12582912.0

            def mkf():
                G.iota(tiny[0:64, 0:1], pattern=[[0, 1]], base=0, channel_multiplier=1, allow_small_or_imprecise_dtypes=True)
                return G.iota(tiny[64:128, 0:1], pattern=[[0, 1]], base=0, channel_multiplier=1, allow_small_or_imprecise_dtypes=True)
            fw.op(pool, mkf, reads=[ov, Rmisc], writes=[Rmisc])
            fw.op(dve, lambda: V.tensor_scalar(out=tiny[:, 2:3], in0=tiny[:, 0:1], scalar1=32.0, scalar2=32.0, op0=ALU.is_ge, op1=ALU.mult), reads=[ov, Rmisc], writes=[Rmisc])
            fw.op(dve, lambda: V.tensor_tensor(out=tiny[:, 0:1], in0=tiny[:, 0:1], in1=tiny[:, 2:3], op=ALU.subtract), reads=[ov, Rmisc], writes=[Rmisc])
            fw.op(act, lambda: A.activation(out=invf[:], in_=tiny[:, 0:1], func=AF.Exp, scale=-math.log(10000.0) / 32.0), reads=[ov, Rmisc], writes=[Rmisc])
            fw.op(dve, lambda: V.tensor_single_scalar(out=invf[:], in_=invf[:], scalar=1.0 / TWO_PI, op=ALU.mult), reads=[ov, Rmisc], writes=[Rmisc])
            posv = sg
            pa = posv[:, 0, 0:256]
            pr_ = posv[:, 0, 256:512]
            ps_ = posv[:, 1, 0:256]
            pc_ = posv[:, 1, 256:512]
            SC = 6.283184
            for t in range(ST // 256):
                def mkpos(t=t):
                    G.iota(pa[0:64, :], pattern=[[1, 4], [0, 64]], base=4 * t, channel_multiplier=0, allow_small_or_imprecise_dtypes=True)
                    return G.iota(pa[64:128, :], pattern=[[0, 4], [1, 64]], base=0, channel_multiplier=0, allow_small_or_imprecise_dtypes=True)
                fw.op(pool, mkpos, reads=[ov], writes=[Rsg])
                fw.op(dve, lambda: V.tensor_scalar(out=pa, in0=pa, scalar1=invf[:, 0:1], scalar2=None, op0=ALU.mult), reads=[ov, Rsg, Rmisc], writes=[Rsg])
                for (dstv, off, rr) in ((ps_, 0.0, Rxn[0]), (pc_, 0.25, Rxn[1])):
                    fw.op(dve, lambda dstv=dstv, off=off: V.tensor_single_scalar(out=dstv, in_=pa, scalar=off, op=ALU.add), reads=[ov, Rsg], writes=[rr])
                    fw.op(dve, lambda dstv=dstv: V.tensor_single_scalar(out=pr_, in_=dstv, scalar=MAGIC, op=ALU.add), reads=[ov, rr], writes=[Rsgm[0]])
                    fw.op(dve, lambda: V.tensor_single_scalar(out=pr_, in_=pr_, scalar=MAGIC, op=ALU.subtract), reads=[ov, Rsgm[0]], writes=[Rsgm[0]])
                    fw.op(dve, lambda dstv=dstv: V.tensor_tensor(out=dstv, in0=dstv, in1=pr_, op=ALU.subtract), reads=[ov, rr, Rsgm[0]], writes=[rr])
                    fw.op(act, lambda dstv=dstv: A.activation(out=dstv, in_=dstv, func=AF.Sin, scale=SC), reads=[ov, rr], writes=[rr])
                fw.dma(sp, rope[1, :, t * 256:(t + 1) * 256], ps_, reads=[ov, Rxn[0]], writes=[Rrope], key="rp0")
                fw.dma(sp, rope[0, :, t * 256:(t + 1) * 256], pc_, reads=[ov, Rxn[1]], writes=[Rrope], key="rp1")

        def stage_consts(l, sl, half):
            for c in range(2):
                fw.dma(sp, gate[:, c, :], mods[l, c:c + 1, (3 * sl + 2) * D:(3 * sl + 3) * D].partition_broadcast(128),
                       reads=[Rmods], writes=[Rgate], key=("gate", c))
                fw.dma(sp, sc1[:, c, :], mods[l, c, (3 * sl + 1) * D:(3 * sl + 2) * D].rearrange("(k p) -> p k", p=128),
                       reads=[Rmods], writes=[Rsc], key=("sc1", c))
                fw.dma(sp, shf[:, c, :], mods[l, c, (3 * sl) * D:(3 * sl + 1) * D].rearrange("(k p) -> p k", p=128),
                       reads=[Rmods], writes=[Rsc], key=("shf", c))
            fw.dma(sp, lng[:], ln_g[l, sl:sl + 1, :].partition_broadcast(128), writes=[Rlng], key="lng")
            fw.dma(sp, lnb[:], ln_b[l, sl:sl + 1, :].partition_broadcast(128), writes=[Rlnb], key="lnb")
            fw.op(dve, lambda: V.tensor_single_scalar(out=sc1[:], in_=sc1[:], scalar=1.0, op=ALU.add), reads=[Rsc], writes=[Rsc])
            if half:
                fw.op(pool, lambda: G.tensor_single_scalar(out=gate[:], in_=gate[:], scalar=0.5, op=ALU.mult), reads=[Rgate], writes=[Rgate])

        def load_x(src, tl, b):
            for s in range(2):
                r0 = tl["tok0"] + s * 128
                fw.dma(sp, xbuf[:, b, s, :], src[r0:r0 + 128, :], reads=[Rxs[tl["idx"]]], writes=[Rx[b][s]], key=("x", b, s))

        def prologue(tl, b):
            c = tl["cond"]
            for pr in range(4):
                bk = pr % 2

                def tr(pr=pr, bk=bk):
                    for cc in range(2):
                        for s in range(2):
                            fc = pr * 2 + cc
                            i = T.transpose(out=banks[bk][:, cc * 256 + s * 128: cc * 256 + (s + 1) * 128],
                                            in_=xbuf[:, b, s, fc * 128:(fc + 1) * 128], identity=ident[:])
                    return i
                fw.op(pe, tr, reads=[Rx[b][0], Rx[b][1], Rident], writes=[Rb[bk]])

                def ev(pr=pr, bk=bk):
                    for cc in range(2):
                        fc = pr * 2 + cc
                        i = A.activation(out=hT[:, b, fc, :], in_=banks[bk][:, cc * 256:(cc + 1) * 256], func=AF.Identity,
                                         scale=sc1[:, c, fc:fc + 1], bias=shf[:, c, fc:fc + 1])
                    return i
                fw.op(act, ev, reads=[Rb[bk], Rsc], writes=[RhT[b]])

        def epilogue(tl, b, s, ybanks, dst, zi):
            c = tl["cond"]
            z = zb[:, zi, :]
            rz = Rzb[zi]
            for hf in range(2):
                fw.op(dve, lambda hf=hf: V.tensor_tensor(out=z[:, hf * 512:(hf + 1) * 512], in0=banks[ybanks[hf]][:, :],
                                                         in1=gate[:, c, hf * 512:(hf + 1) * 512], op=ALU.mult),
                      reads=[Rb[ybanks[hf]], Rgate], writes=[rz])
            fw.op(dve, lambda: V.scalar_tensor_tensor(out=z, in0=xbuf[:, b, s, :], scalar=ALPHA, in1=z, op0=ALU.mult, op1=ALU.add),
                  reads=[Rx[b][s], rz], writes=[rz])
            ln_rows(z, rz, zi, D)
            fw.op(act, lambda: A.activation(out=z, in_=z, func=AF.Identity, scale=rstd[:, zi, 0:1], bias=rstd[:, zi, 1:2]),
                  reads=[rz, Rst[zi]], writes=[rz])
            fw.op(pool, lambda: G.tensor_tensor(out=z, in0=z, in1=lng[:], op=ALU.mult), reads=[rz, Rlng], writes=[rz])
            fw.op(pool, lambda: G.tensor_tensor(out=z, in0=z, in1=lnb[:], op=ALU.add), reads=[rz, Rlnb], writes=[rz])
            r0 = tl["tok0"] + s * 128
            fw.dma(sp, dst[r0:r0 + 128, :], z, reads=[rz], writes=[Rxs[tl["idx"]]], key=("xo", zi))

        def ln_rows(src, rsrc, si, n):
            nchunk = n // 512

            def bs():
                for q in range(nchunk):
                    i = V.bn_stats(out=stats[:, si, q * 6:(q + 1) * 6], in_=src[:, q * 512:(q + 1) * 512])
                return i
            fw.op(dve, bs, reads=[rsrc], writes=[Rst[si]])
            fw.op(dve, lambda: V.bn_aggr(out=mv[:, si, :], in_=stats[:, si, 0:6 * nchunk]), reads=[Rst[si]], writes=[Rst[si]])
            fw.op(act, lambda: A.activation(out=rstd[:, si, 0:1], in_=mv[:, si, 1:2], func=AF.Sqrt, bias=epst[:, 0:1]), reads=[Rst[si], Rmisc], writes=[Rst[si]])
            fw.op(dve, lambda: V.reciprocal(out=rstd[:, si, 0:1], in_=rstd[:, si, 0:1]), reads=[Rst[si]], writes=[Rst[si]])
            fw.op(dve, lambda: V.scalar_tensor_tensor(out=rstd[:, si, 1:2], in0=mv[:, si, 0:1], scalar=-1.0, in1=rstd[:, si, 0:1], op0=ALU.mult, op1=ALU.mult),
                  reads=[Rst[si]], writes=[Rst[si]])

        claim_t = tiny

        def claim(extra=()):
            fw.op(pool, lambda: G.memset(claim_t[:, 4:5], 0.0), writes=[ov] + list(extra))

        def w1_dma(l, i, j):
            for part in range(2):
                dstv = aview(j * 2048, 2048, BF16, "p (k c) -> p k c", c=512)[:, :, part * 256:(part + 1) * 256]
                c0 = part * DFF + j * 256
                fw.dma(pool, dstv, ffn_w1[l, i, :, c0:c0 + 256].rearrange("(k p) c -> p k c", p=128),
                       reads=([ov] if 2 * j + 1 >= 20 else []), writes=[slot[2 * j], slot[2 * j + 1]], key=("w", 2 * j, part))

        def w2_dma(l, i, j):
            dstv = aview(22 * 1024 + j * 1024, 1024, BF16, "p (e c) -> p e c", e=2)
            fw.dma(pool, dstv, ffn_w2[l, i, j * 256:(j + 1) * 256, :].rearrange("(e p) c -> p e c", p=128),
                   reads=[ov], writes=[slot[22 + j]], key=("w", 22 + j, 0))

        def mixer_dma(l, m):
            if m < 6:
                fw.dma(pool, WIN(m), w_in[l, :, m * 512:(m + 1) * 512].rearrange("(k p) c -> p k c", p=128),
                       writes=[slot[2 * m], slot[2 * m + 1]], key=("w", 2 * m, 0))
            elif m in (8, 9):
                hf = m - 8
                fw.dma(pool, WIN(8 + hf), w_out[l, :, hf * 512:(hf + 1) * 512].rearrange("(k p) c -> p k c", p=128),
                       writes=[slot[16 + 2 * hf], slot[17 + 2 * hf]], key=("w", 16 + 2 * hf, 0))

        def stage_F(l, i, src, dst, have_w=False, nxt=None):
            sl = 0 if i == 0 else 2
            if not have_w:
                claim()
                for j in range(11):
                    w1_dma(l, i, j)
                for j in range(11):
                    w2_dma(l, i, j)
            stage_consts(l, sl, True)
            load_x(src, tiles[0], 0)
            zi = 0
            prologue(tiles[0], 0)
            for ti, tl in enumerate(tiles):
                b = ti % 2
                if ti + 1 < NT:
                    load_x(src, tiles[ti + 1], 1 - b)
                for hc in range(NHC):
                    j, e = hc // 2, hc % 2
                    w1v = aview(j * 2048, 2048, BF16, "p (k c) -> p k c", c=512)
                    bk = hc % 4

                    def mm(w1v=w1v, e=e, bk=bk, b=b):
                        for part in range(2):
                            for kc in range(8):
                                i_ = T.matmul(banks[bk][:, part * 256:(part + 1) * 256],
                                              lhsT=w1v[:, kc, part * 256 + e * 128: part * 256 + (e + 1) * 128],
                                              rhs=hT[:, b, kc, :], start=(kc == 0), stop=(kc == 7))
                        return i_
                    fw.op(pe, mm, reads=[slot[2 * j], slot[2 * j + 1], RhT[b]], writes=[Rb[bk]])
                    sb_ = hc % 4
                    fw.op(act, lambda bk=bk, sb_=sb_: A.activation(out=silt[:, sb_, :], in_=banks[bk][:, 0:256], func=AF.Silu),
                          reads=[Rb[bk], ov], writes=[Rsil[sb_]])
                    fw.op(dve, lambda bk=bk, sb_=sb_, hc=hc: V.tensor_tensor(out=uT[:, hc, :], in0=banks[bk][:, 256:512], in1=silt[:, sb_, :], op=ALU.mult),
                          reads=[Rb[bk], Rsil[sb_], ov], writes=[RuT])
                    if ti == NT - 1 and nxt is not None and e == 1:
                        if nxt[0] == "F":
                            w1_dma(nxt[1], nxt[2], j)
                        else:
                            mixer_dma(nxt[1], j)
                if ti + 1 < NT:
                    prologue(tiles[ti + 1], 1 - b)
                for s in range(2):
                    for hf in range(2):
                        bk = 4 + s * 2 + hf

                        def mm2(s=s, hf=hf, bk=bk):
                            for hc in range(NHC):
                                w2v = aview(22 * 1024 + (hc // 2) * 1024, 1024, BF16, "p (e c) -> p e c", e=2)
                                i_ = T.matmul(banks[bk][:, :], lhsT=uT[:, hc, s * 128:(s + 1) * 128],
                                              rhs=w2v[:, hc % 2, hf * 512:(hf + 1) * 512], start=(hc == 0), stop=(hc == NHC - 1))
                            return i_
                        fw.op(pe, mm2, reads=[RuT, ov] + [slot[22 + q] for q in range(11)], writes=[Rb[bk]])
                if ti == NT - 1 and nxt is not None and nxt[0] == "F":
                    for j in range(11):
                        w2_dma(nxt[1], nxt[2], j)
                for s in range(2):
                    epilogue(tl, b, s, (4 + s * 2, 5 + s * 2), dst, zi)
                    zi = 1 - zi

        def WIN(m):
            return aview(m * 2048, 2048, BF16, "p (k c) -> p k c", c=512)

        def load_mixer(l, have_w=False):
            if not have_w:
                for m in (0, 1, 2, 3, 4, 5, 8, 9):
                    mixer_dma(l, m)
            for m in (2, 3):
                srcv = aview(m * 2048, 2048, BF16, "p (a two f) -> p a two f", two=2, f=32)
                dstv = aview((m + 4) * 2048, 2048, BF16, "p (a two f) -> p a two f", two=2, f=32)
                fw.op(act, lambda srcv=srcv, dstv=dstv: A.mul(out=dstv[:, :, 0, :], in_=srcv[:, :, 1, :], mul=-1.0),
                      reads=[slot[2 * m], slot[2 * m + 1]], writes=[slot[2 * m + 8], slot[2 * m + 9]])
                fw.op(act, lambda srcv=srcv, dstv=dstv: A.copy(out=dstv[:, :, 1, :], in_=srcv[:, :, 0, :]),
                      reads=[slot[2 * m], slot[2 * m + 1]], writes=[slot[2 * m + 8], slot[2 * m + 9]])

        def layer_consts(l):
            fw.dma(sp, lgt[:], dlogit[l:l + 1, :].partition_broadcast(128), writes=[Rdec], key="lgt")
            fw.op(act, lambda: A.activation(out=lgt[:], in_=lgt[:], func=AF.Exp, scale=-1.0), reads=[Rdec], writes=[Rdec])
            fw.op(act, lambda: A.activation(out=lgt[:], in_=lgt[:], func=AF.Ln, bias=1.0), reads=[Rdec], writes=[Rdec])
            fw.op(act, lambda: A.mul(out=lgt[:], in_=lgt[:], mul=-1.0), reads=[Rdec], writes=[Rdec])
            fw.op(act, lambda: A.activation(out=gC[:], in_=lgt[:], func=AF.Exp, scale=128.0), reads=[Rdec], writes=[Rdec])
            fw.op(dve, lambda: V.tensor_scalar(out=tiny[:, 1:2], in0=piota[:], scalar1=-1.0, scalar2=127.0, op0=ALU.mult, op1=ALU.add),
                  reads=[Rmisc], writes=[Rmisc])
            fw.op(dve, lambda: V.tensor_scalar(out=kdf[:], in0=lgt[:, 0:4], scalar1=tiny[:, 1:2], scalar2=None, op0=ALU.mult), reads=[Rdec, Rmisc], writes=[Rdec])
            fw.op(dve, lambda: V.tensor_scalar(out=kdb[:], in0=lgt[:, 4:8], scalar1=piota[:, 0:1], scalar2=None, op0=ALU.mult), reads=[Rdec, Rmisc], writes=[Rdec])
            fw.op(act, lambda: A.activation(out=kdf[:], in_=kdf[:], func=AF.Exp), reads=[Rdec], writes=[Rdec])
            fw.op(act, lambda: A.activation(out=kdb[:], in_=kdb[:], func=AF.Exp), reads=[Rdec], writes=[Rdec])
            dist = xn[:, 0, 0:128]
            ci = xn[:, 0, 128:256]
            t1 = xn[:, 0, 256:384]
            t2 = xn[:, 0, 384:512]
            mk1 = xn[:, 1, 0:128]
            fw.op(pool, lambda: G.iota(dist, pattern=[[1, 128]], base=0, channel_multiplier=-1, allow_small_or_imprecise_dtypes=True),
                  reads=[ov], writes=[Rxn[0]])
            fw.op(pool, lambda: G.iota(ci, pattern=[[1, 128]], base=0, channel_multiplier=0, allow_small_or_imprecise_dtypes=True),
                  reads=[ov], writes=[Rxn[0]])
            for h in range(4):
                fw.op(dve, lambda: V.tensor_single_scalar(out=t1, in_=dist, scalar=0.0, op=ALU.max), reads=[Rxn[0], ov], writes=[Rxn[0]])
                fw.op(act, lambda h=h: A.activation(out=t1, in_=t1, func=AF.Exp, scale=lgt[:, h:h + 1]), reads=[Rxn[0], Rdec, ov], writes=[Rxn[0]])
                fw.op(dve, lambda: V.tensor_single_scalar(out=mk1, in_=dist, scalar=0.0, op=ALU.is_ge), reads=[Rxn[0], ov], writes=[Rxn[1]])
                fw.op(dve, lambda: V.tensor_tensor(out=t1, in0=t1, in1=mk1, op=ALU.mult), reads=[Rxn[0], Rxn[1], ov], writes=[Rxn[0]])
                fw.op(dve, lambda: V.tensor_scalar(out=t2, in0=dist, scalar1=-1.0, scalar2=0.0, op0=ALU.mult, op1=ALU.max), reads=[Rxn[0], ov], writes=[Rxn[0]])
                fw.op(act, lambda h=h: A.activation(out=t2, in_=t2, func=AF.Exp, scale=lgt[:, 4 + h:5 + h]), reads=[Rxn[0], Rdec, ov], writes=[Rxn[0]])
                fw.op(dve, lambda: V.tensor_single_scalar(out=mk1, in_=dist, scalar=0.0, op=ALU.is_le), reads=[Rxn[0], ov], writes=[Rxn[1]])
                fw.op(dve, lambda: V.tensor_tensor(out=t2, in0=t2, in1=mk1, op=ALU.mult), reads=[Rxn[0], Rxn[1], ov], writes=[Rxn[0]])
                fw.op(dve, lambda h=h: V.scalar_tensor_tensor(out=Mcomb[:, h * 128:(h + 1) * 128], in0=t1, scalar=1.0, in1=t2, op0=ALU.mult, op1=ALU.add),
                      reads=[Rxn[0], ov], writes=[Rdec])
                fw.op(dve, lambda: V.tensor_single_scalar(out=t1, in_=ci, scalar=1.0, op=ALU.add), reads=[Rxn[0], ov], writes=[Rxn[0]])
                fw.op(act, lambda h=h: A.activation(out=Qf[:, h, 0:128], in_=t1, func=AF.Exp, scale=lgt[:, h:h + 1]), reads=[Rxn[0], Rdec, ov], writes=[Rdec])
                fw.op(dve, lambda: V.tensor_scalar(out=t2, in0=ci, scalar1=-1.0, scalar2=128.0, op0=ALU.mult, op1=ALU.add), reads=[Rxn[0], ov], writes=[Rxn[0]])
                fw.op(act, lambda h=h: A.activation(out=Qb[:, h, 0:128], in_=t2, func=AF.Exp, scale=lgt[:, 4 + h:5 + h]), reads=[Rxn[0], Rdec, ov], writes=[Rdec])
            fw.op(dve, lambda: V.tensor_single_scalar(out=Mcomb, in_=Mcomb, scalar=QSCALE, op=ALU.mult), reads=[Rdec, ov], writes=[Rdec])
            fw.op(dve, lambda: V.tensor_single_scalar(out=Qf[:, :, 0:128], in_=Qf[:, :, 0:128], scalar=QSCALE, op=ALU.mult), reads=[Rdec, ov], writes=[Rdec])
            fw.op(dve, lambda: V.tensor_single_scalar(out=Qb[:, :, 0:128], in_=Qb[:, :, 0:128], scalar=QSCALE, op=ALU.mult), reads=[Rdec, ov], writes=[Rdec])
            fw.op(dve, lambda: V.tensor_copy(out=Qf[:, :, 128:256], in_=Qf[:, :, 0:128]), reads=[Rdec, ov], writes=[Rdec])
            fw.op(dve, lambda: V.tensor_copy(out=Qb[:, :, 128:256], in_=Qb[:, :, 0:128]), reads=[Rdec, ov], writes=[Rdec])
            fw.dma(sp, cwraw[0:CK, 1, :], conv_w[l, :, :], reads=[ov], writes=[Rxn[1]], key="cwr")

            def trw():
                for cc in range(4):
                    i_ = T.transpose(out=banks[0][:, cc * 32:cc * 32 + CK], in_=cwraw[0:CK, 1, cc * 128:(cc + 1) * 128], identity=ident[0:CK, 0:CK])
                return i_
            fw.op(pe, trw, reads=[Rxn[1], Rident, ov], writes=[Rb[0]])
            fw.op(dve, lambda: V.tensor_copy(out=cw[:], in_=banks[0][:, 0:128].rearrange("p (c k) -> p c k", k=32)[:, :, 0:CK]), reads=[Rb[0]], writes=[Rcw])
            for vi, vec in enumerate((conv_b, conv_ln_g, conv_ln_b)):
                fw.dma(sp, cvec[:, vi, :], vec[l, :].rearrange("(c p) -> p c", p=128), writes=[Rcw], key=("cv", vi))
            fw.op(pool, lambda: G.memset(upb, 0.0), reads=[ov], writes=[Rup])

            def mkd():
                for cc in range(4):
                    for kk in range(CK):
                        i_ = V.tensor_scalar(out=Dblk[:, cc, kk, :], in0=dmask[:], scalar1=cw[:, cc, kk:kk + 1], scalar2=None, op0=ALU.mult)
                return i_
            fw.op(dve, mkd, reads=[Rcw, Rdm, ov], writes=[Rdb])

        def proj_fm(tl, m, outT, Rout, versions=None, bk0=2):
            isS = tl["kind"] == "S"
            b = tl["idx"] % 2
            for h in range(4):
                bk = bk0 + (h % 2)
                nparts = 2 if isS else 1

                def mm(h=h, bk=bk, nparts=nparts):
                    for part in range(nparts):
                        wv = WIN(m if part == 0 else m + 4)
                        for kc in range(8):
                            i_ = T.matmul(banks[bk][:, part * 256:(part + 1) * 256], lhsT=wv[:, kc, h * 128:(h + 1) * 128], rhs=hT[:, b, kc, :],
                                          start=(kc == 0), stop=(kc == 7))
                    return i_
                rd = [slot[2 * m], slot[2 * m + 1], RhT[b]] + ([slot[2 * m + 8], slot[2 * m + 9]] if isS else [])
                fw.op(pe, mm, reads=rd, writes=[Rb[bk]])
                r = h % 2
                if isS:
                    fw.op(dve, lambda bk=bk, r=r: V.tensor_tensor(out=rt1[:, r, :], in0=banks[bk][:, 0:256], in1=ropeCS[:, b, 0, :], op=ALU.mult),
                          reads=[Rb[bk], Rcs[b], ov], writes=[Rrt[r]])
                    fw.op(dve, lambda bk=bk, r=r: V.tensor_tensor(out=rt2[:, r, :], in0=banks[bk][:, 256:512], in1=ropeCS[:, b, 1, :], op=ALU.mult),
                          reads=[Rb[bk], Rcs[b], ov], writes=[Rrt[r]])
                    fw.op(pool, lambda r=r: G.tensor_tensor(out=rt1[:, r, :], in0=rt1[:, r, :], in1=rt2[:, r, :], op=ALU.add),
                          reads=[Rrt[r], ov], writes=[Rrt[r]])
                    srcv, rsrc = rt1[:, r, :], Rrt[r]
                else:
                    srcv, rsrc = banks[bk][:, 0:256], Rb[bk]
                fw.op(act, lambda h=h, srcv=srcv: A.copy(out=outT[:, h, :], in_=srcv), reads=[rsrc, ov], writes=[Rout])
                if versions:
                    for (Qt, dstT) in versions:
                        if isS:
                            fw.op(pool, lambda h=h, srcv=srcv, Qt=Qt, dstT=dstT: G.tensor_tensor(out=dstT[:, h, :], in0=srcv, in1=Qt[:, h, :], op=ALU.mult),
                                  reads=[rsrc, Rdec, ov], writes=[Rout])
                        else:
                            fw.op(dve, lambda h=h, srcv=srcv, Qt=Qt, dstT=dstT: V.tensor_tensor(out=dstT[:, h, :], in0=srcv, in1=Qt[:, h, :], op=ALU.mult),
                                  reads=[rsrc, Rdec, ov], writes=[Rout])

        def proj_tm(s, m, bk, b):
            def mm():
                for kc in range(8):
                    i_ = T.matmul(banks[bk][:, :], lhsT=hT[:, b, kc, s * 128:(s + 1) * 128], rhs=WIN(m)[:, kc, :], start=(kc == 0), stop=(kc == 7))
                return i_
            fw.op(pe, mm, reads=[slot[2 * m], slot[2 * m + 1], RhT[b]], writes=[Rb[bk]])

        def k_tm(s, kd, bk, ki):
            pb = banks[bk][:, 0:256].bitcast(BF16)

            def tr():
                for h in range(4):
                    i_ = T.transpose(out=pb[:, h * 128:(h + 1) * 128], in_=kT[:, h, s * 128:(s + 1) * 128], identity=identb[:])
                return i_
            fw.op(pe, tr, reads=[RkT, Rmisc, ov], writes=[Rb[bk]])

            def ev():
                for h in range(4):
                    i_ = A.activation(out=kdT[:, ki, h * 128:(h + 1) * 128], in_=pb[:, h * 128:(h + 1) * 128], func=AF.Copy, scale=kd[:, h:h + 1])
                return i_
            fw.op(act, ev, reads=[Rb[bk], Rdec, ov], writes=[RkdT[ki]])

        def kv_mm(s, bk, ki):
            def mm():
                for h in range(4):
                    i_ = T.matmul(banks[bk][:, h * 128:(h + 1) * 128], lhsT=kdT[:, ki, h * 128:(h + 1) * 128], rhs=vbf[:, s, h * 128:(h + 1) * 128],
                                  start=True, stop=True)
                return i_
            fw.op(pe, mm, reads=[RkdT[ki], Rvbf, ov], writes=[Rb[bk]])

        def state_update(Rt, RRt, bk, goff):
            def up():
                for h in range(4):
                    i_ = V.scalar_tensor_tensor(out=Rt[:, h * 128:(h + 1) * 128], in0=Rt[:, h * 128:(h + 1) * 128], scalar=gC[:, goff + h:goff + h + 1],
                                                in1=banks[bk][:, h * 128:(h + 1) * 128], op0=ALU.mult, op1=ALU.add)
                return i_
            fw.op(dve, up, reads=[Rb[bk], Rdec, ov, RRt], writes=[RRt])

        def load_cs(tl):
            if tl["kind"] == "S":
                b = tl["idx"] % 2
                for c in range(2):
                    fw.dma(sp, ropeCS[:, b, c, :], rope[c, :, tl["tok0"]:tl["tok0"] + 256], reads=[Rrope, ov], writes=[Rcs[b]], key=("cs", b, c))

        def stage_KB(l, src, have_w=False):
            claim(extra=[slot[i] for i in range(20, 33)])
            load_mixer(l, have_w)
            layer_consts(l)
            stage_consts(l, 1, False)
            order = list(reversed(tiles))

            def kb_pk(tl):
                proj_fm(tl, 3, kT, RkT)

            def kb_pv(tl):
                b = tl["idx"] % 2
                for s in range(2):
                    proj_tm(s, 4, 4 + s, b)
                    fw.op(act, lambda s=s: A.copy(out=vbf[:, s, :], in_=banks[4 + s][:, :]), reads=[Rb[4 + s], ov], writes=[Rvbf])

            def kb_ld(tl):
                load_x(src, tl, tl["idx"] % 2)
                load_cs(tl)

            kb_ld(order[0])
            if NT > 1:
                kb_ld(order[1])
            prologue(order[0], order[0]["idx"] % 2)
            kb_pk(order[0])
            kb_pv(order[0])
            if NT > 1:
                prologue(order[1], order[1]["idx"] % 2)
            for oi, tl in enumerate(order):
                b = tl["idx"] % 2
                isS = tl["kind"] == "S"
                if oi + 2 < NT:
                    kb_ld(order[oi + 2])
                k_tm(1, kdb, 7, 1)
                k_tm(0, kdb, 6, 0)
                if oi + 1 < NT:
                    kb_pk(order[oi + 1])
                last_tile_of_seq = (not isS) or (tl["idx"] == ST // 256 - 1)
                if last_tile_of_seq:
                    if isS:
                        fw.dma(sp, Rbs.rearrange("p (h v) -> p h v", h=4), stb[l].rearrange("h d v -> d h v"), reads=[ov], writes=[RRbs], key="rbs")
                    else:
                        fw.op(pool, lambda: G.memset(Rbs, 0.0), reads=[ov], writes=[RRbs])
                for s in (1, 0):
                    ch = tl["tok0"] // 128 + s
                    kv_mm(s, 6 + s, s)
                    ri = ch % 2
                    fw.op(act, lambda ri=ri: A.copy(out=rbt[:, ri, 0, :], in_=Rbs), reads=[RRbs, ov], writes=[Rrbt[ri]])
                    fw.dma(sp, rbd[ch], rbt[:, ri, 0, :], reads=[Rrbt[ri]], writes=[Rrbd[ch]], key=("rbo", ri))
                    state_update(Rbs, RRbs, 6 + s, 4)
                if not isS:
                    fw.dma(sp, nsb[tl["seq"], l].rearrange("h d v -> d h v"), Rbs.rearrange("p (h v) -> p h v", h=4), reads=[RRbs, ov], key="nsb")
                if oi + 1 < NT:
                    kb_pv(order[oi + 1])
                if oi + 2 < NT:
                    prologue(order[oi + 2], order[oi + 2]["idx"] % 2)

        def stage_M(l, src, dst):
            stage_consts(l, 1, False)
            load_x(src, tiles[0], 0)
            load_cs(tiles[0])
            prologue(tiles[0], 0)
            zi = [0]

            def conv_in(tl, b):
                isS = tl["kind"] == "S"
                for cc in range(4):
                    bk = 6 + cc % 2

                    def mmc(cc=cc, bk=bk, b=b):
                        for part in range(2):
                            for kc in range(8):
                                i_ = T.matmul(banks[bk][:, part * 256:(part + 1) * 256], lhsT=WIN(part)[:, kc, cc * 128:(cc + 1) * 128], rhs=hT[:, b, kc, :],
                                              start=(kc == 0), stop=(kc == 7))
                        return i_
                    fw.op(pe, mmc, reads=[slot[0], slot[1], slot[2], slot[3], RhT[b]], writes=[Rb[bk]])
                    r = cc % 2
                    fw.op(act, lambda bk=bk, r=r: A.activation(out=sgm[:, r, :], in_=banks[bk][:, 256:512], func=AF.Sigmoid), reads=[Rb[bk], ov], writes=[Rsgm[r]])
                    if isS:
                        outv = upS[:, cc, :, 15:79]
                        in0 = banks[bk][:, 0:256].rearrange("p (r w) -> p r w", w=64)
                        in1 = sgm[:, r, :].rearrange("p (r w) -> p r w", w=64)
                    else:
                        outv = upP[:, cc, 15:271]
                        in0 = banks[bk][:, 0:256]
                        in1 = sgm[:, r, :]
                    fw.op(dve, lambda outv=outv, in0=in0, in1=in1: V.tensor_tensor(out=outv, in0=in0, in1=in1, op=ALU.mult),
                          reads=[Rb[bk], Rsgm[r], ov], writes=[Rup])

            CBK = [0, 1, 6, 7]

            def conv_mm(tl, ccs):
                isS = tl["kind"] == "S"

                def mmcv():
                    for cc in ccs:
                        for kk in range(CK):
                            for g in range(4):
                                ps = slice(32 * g, 32 * g + 32)
                                bkc = CBK[(g + cc) % 4]
                                if isS:
                                    ov_ = banks[bkc][ps, 0:256].rearrange("p (r w) -> p r w", w=64)
                                    rh = upS[ps, cc, :, kk:kk + 64]
                                else:
                                    ov_ = banks[bkc][ps, 0:256]
                                    rh = upP[ps, cc, kk:kk + 256]
                                i_ = T.matmul(ov_, lhsT=Dblk[ps, cc, kk, :], rhs=rh, start=(kk == 0), stop=(kk == CK - 1),
                                              tile_position=(32 * g, 32 * g))
                    return i_
                fw.op(pe, mmcv, reads=[Rup, Rdb, ov], writes=[Rb[q_] for q_ in CBK])

            def conv_evac():
                for cc in range(4):
                    def evc(cc=cc):
                        for g in range(4):
                            ps = slice(32 * g, 32 * g + 32)
                            i_ = V.tensor_scalar(out=acc[ps, cc, :], in0=banks[CBK[(g + cc) % 4]][ps, 0:256], scalar1=cvec[ps, 0, cc:cc + 1], scalar2=None, op0=ALU.add)
                        return i_
                    fw.op(dve, evc, reads=[Rb[q_] for q_ in CBK] + [Rcw, ov], writes=[Racc[cc]])

            def conv_ln_a():
                for s in range(2):
                    bk = 4 + s

                    def trc(s=s, bk=bk):
                        for cc in range(4):
                            i_ = T.transpose(out=banks[bk][:, cc * 128:(cc + 1) * 128], in_=acc[:, cc, s * 128:(s + 1) * 128], identity=ident[:])
                        return i_
                    fw.op(pe, trc, reads=Racc + [Rident, ov], writes=[Rb[bk]])
                for s in range(2):
                    bk = 4 + s
                    ln_rows(banks[bk], Rb[bk], 2 + s, 512)
                    fw.op(act, lambda s=s, bk=bk: A.activation(out=xn[:, s, :], in_=banks[bk][:, :], func=AF.Identity, scale=rstd[:, 2 + s, 0:1], bias=rstd[:, 2 + s, 1:2]),
                          reads=[Rb[bk], Rst[2 + s], ov], writes=[Rxn[s]])

            def conv_ln_b():
                for pr in range(2):
                    bk = 4 + pr

                    def trb(pr=pr, bk=bk):
                        for c2 in range(2):
                            for s in range(2):
                                cc = pr * 2 + c2
                                i_ = T.transpose(out=banks[bk][:, c2 * 256 + s * 128:c2 * 256 + (s + 1) * 128], in_=xn[:, s, cc * 128:(cc + 1) * 128], identity=ident[:])
                        return i_
                    fw.op(pe, trb, reads=[Rxn[0], Rxn[1], Rident, ov], writes=[Rb[bk]])

                    def evb(pr=pr, bk=bk):
                        for c2 in range(2):
                            cc = pr * 2 + c2
                            i_ = A.activation(out=mixT[:, cc, :], in_=banks[bk][:, c2 * 256:(c2 + 1) * 256], func=AF.Silu, scale=cvec[:, 1, cc:cc + 1], bias=cvec[:, 2, cc:cc + 1])
                        return i_
                    fw.op(act, evb, reads=[Rb[bk], Rcw, ov], writes=[Rmix])

            def ret_rfb(tl, s):
                fi = (tl["tok0"] // 128 + s) % 2
                fw.op(act, lambda fi=fi: A.copy(out=Rfb[:, fi, :], in_=Rf), reads=[RRf, ov], writes=[RRfb[fi]])

            def ret_scores(s, bk):
                def mms(s=s, bk=bk):
                    for h in range(4):
                        i_ = T.matmul(banks[bk][:, h * 128:(h + 1) * 128], lhsT=kT[:, h, s * 128:(s + 1) * 128], rhs=qT[:, h, s * 128:(s + 1) * 128], start=True, stop=True)
                    return i_
                fw.op(pe, mms, reads=[RkT, RqT, ov], writes=[Rb[bk]])
                fw.op(dve, lambda s=s, bk=bk: V.tensor_tensor(out=Pm[:, s, :], in0=banks[bk][:, :], in1=Mcomb, op=ALU.mult), reads=[Rb[bk], Rdec, ov], writes=[RPm[s]])

            def ret_o(tl, b, s, bk):
                fi = (tl["tok0"] // 128 + s) % 2

                def mmo(s=s, fi=fi, b=b, bk=bk):
                    for h in range(4):
                        hs = slice(h * 128, (h + 1) * 128)
                        T.matmul(banks[bk][:, hs], lhsT=Pm[:, s, hs], rhs=vbf[:, s, hs], start=True, stop=False)
                        T.matmul(banks[bk][:, hs], lhsT=qfT[:, h, s * 128:(s + 1) * 128], rhs=Rfb[:, fi, hs], start=False, stop=False)
                        i_ = T.matmul(banks[bk][:, hs], lhsT=qbT[:, h, s * 128:(s + 1) * 128], rhs=rbt[:, b, s, hs], start=False, stop=True)
                    return i_
                fw.op(pe, mmo, reads=[RPm[s], Rvbf, RqT, RRfb[fi], Rrbt[b], ov], writes=[Rb[bk]])

            def gn_pre(s, bko):
                gi = s

                def gbs(gi=gi):
                    for h in range(4):
                        i_ = V.bn_stats(out=gst[:, gi, h * 6:(h + 1) * 6], in_=banks[bko][:, h * 128:(h + 1) * 128])
                    return i_
                fw.op(dve, gbs, reads=[Rb[bko]], writes=[Rgst[gi]])

                def gag(gi=gi):
                    for h in range(4):
                        i_ = V.bn_aggr(out=gmv[:, gi, h * 2:(h + 1) * 2], in_=gst[:, gi, h * 6:(h + 1) * 6])
                    return i_
                fw.op(dve, gag, reads=[Rgst[gi]], writes=[Rgst[gi]])
                gm = gmv[:, gi, :].rearrange("p (h two) -> p h two", two=2)
                fw.op(act, lambda gi=gi, gm=gm: A.activation(out=grs[:, gi, 0:4], in_=gm[:, :, 1], func=AF.Sqrt, bias=epst[:, 0:1]), reads=[Rgst[gi], Rmisc], writes=[Rgst[gi]])
                fw.op(dve, lambda gi=gi: V.reciprocal(out=grs[:, gi, 0:4], in_=grs[:, gi, 0:4]), reads=[Rgst[gi]], writes=[Rgst[gi]])
                fw.op(dve, lambda gi=gi, gm=gm: V.scalar_tensor_tensor(out=grs[:, gi, 4:8], in0=gm[:, :, 0], scalar=-1.0, in1=grs[:, gi, 0:4], op0=ALU.mult, op1=ALU.mult),
                      reads=[Rgst[gi]], writes=[Rgst[gi]])
                onv, ogv = onb[s], ogb[s]

                def gev(gi=gi, onv=onv):
                    for h in range(4):
                        i_ = A.activation(out=onv[:, h * 128:(h + 1) * 128], in_=banks[bko][:, h * 128:(h + 1) * 128], func=AF.Identity,
                                          scale=grs[:, gi, h:h + 1], bias=grs[:, gi, 4 + h:5 + h])
                    return i_
                fw.op(act, gev, reads=[Rb[bko], Rgst[gi], ov], writes=[Ron[s]])
                fw.op(pool, lambda s=s, onv=onv, ogv=ogv: G.tensor_tensor(out=ogv, in0=onv, in1=sg[:, s, :], op=ALU.mult), reads=[Ron[s], Rsg, ov], writes=[Rog[s]])

            def gn_post(s, bkt):
                ogv = ogb[s]
                pb = banks[bkt][:, 0:256].bitcast(BF16)

                def tro(pb=pb, ogv=ogv):
                    for h in range(4):
                        i_ = T.transpose(out=pb[:, h * 128:(h + 1) * 128], in_=ogv[:, h * 128:(h + 1) * 128], identity=identb[:])
                    return i_
                fw.op(pe, tro, reads=[Rog[s], Rmisc, ov], writes=[Rb[bkt]])
                fw.op(act, lambda s=s, pb=pb: A.copy(out=mixT[:, 4:8, s * 128:(s + 1) * 128], in_=pb.rearrange("p (h t) -> p h t", t=128)),
                      reads=[Rb[bkt], ov], writes=[Rmix])

            def head1(tl):
                b = tl["idx"] % 2
                isS = tl["kind"] == "S"
                if (not isS) and tiles[tl["idx"] - 1]["kind"] == "S":
                    fw.op(pool, lambda: G.memset(upb, 0.0), reads=[ov, Rup], writes=[Rup])
                conv_in(tl, b)
                proj_fm(tl, 2, qT, RqT, versions=[(Qf, qfT), (Qb, qbT)], bk0=0)
                proj_fm(tl, 3, kT, RkT, bk0=0)

            def head2(tl):
                b = tl["idx"] % 2
                for s in range(2):
                    proj_tm(s, 4, 0, b)
                    fw.op(act, lambda s=s: A.copy(out=vbf[:, s, :], in_=banks[0][:, :]), reads=[Rb[0], ov], writes=[Rvbf])
                    proj_tm(s, 5, 1, b)
                    fw.op(act, lambda s=s: A.activation(out=sg[:, s, :], in_=banks[1][:, :], func=AF.Silu), reads=[Rb[1], ov], writes=[Rsg])
                conv_mm(tl, (0, 1, 2, 3))
                conv_evac()

            def rbt_load(tl):
                b = tl["idx"] % 2
                for s in range(2):
                    ch = tl["tok0"] // 128 + s
                    fw.dma(sp, rbt[:, b, s, :], rbd[ch], reads=[Rrbd[ch]], writes=[Rrbt[b]], key=("rbi", b, s))

            def tailA1(tl):
                isS = tl["kind"] == "S"
                if isS and tl["first"]:
                    fw.dma(sp, Rf.rearrange("p (h v) -> p h v", h=4), stf[l].rearrange("h d v -> d h v"), reads=[ov], writes=[RRf], key="rfs")
                elif not isS:
                    fw.op(pool, lambda: G.memset(Rf, 0.0), reads=[ov], writes=[RRf])
                ret_rfb(tl, 0)
                ret_scores(0, 2)
                ret_scores(1, 3)
                k_tm(0, kdf, 4, 0)
                k_tm(1, kdf, 5, 1)
                kv_mm(0, 4, 0)
                kv_mm(1, 5, 1)

            rbt_load(tiles[0])
            if NT > 1:
                load_x(src, tiles[1], 1)
                load_cs(tiles[1])
            head1(tiles[0])
            head2(tiles[0])
            tailA1(tiles[0])
            for ti, tl in enumerate(tiles):
                b = ti % 2
                isS = tl["kind"] == "S"
                if ti + 1 < NT:
                    rbt_load(tiles[ti + 1])
                ret_o(tl, b, 0, 2)
                state_update(Rf, RRf, 4, 0)
                ret_rfb(tl, 1)
                ret_o(tl, b, 1, 3)
                state_update(Rf, RRf, 5, 0)
                if not isS:
                    fw.dma(sp, nsf[tl["seq"], l].rearrange("h d v -> d h v"), Rf.rearrange("p (h v) -> p h v", h=4), reads=[RRf, ov], key="nsf")
                conv_ln_a()
                gn_pre(0, 2)
                gn_pre(1, 3)
                if ti + 1 < NT:
                    prologue(tiles[ti + 1], 1 - b)
                    head1(tiles[ti + 1])
                    head2(tiles[ti + 1])
                conv_ln_b()
                gn_post(0, 2)
                gn_post(1, 3)
                if ti + 1 < NT:
                    tailA1(tiles[ti + 1])
                for s in range(2):
                    yb = (6, 7) if s == 0 else (0, 1)
                    for hf in range(2):
                        bk = yb[hf]

                        def mmw(s=s, hf=hf, bk=bk):
                            for kc in range(8):
                                i_ = T.matmul(banks[bk][:, :], lhsT=mixT[:, kc, s * 128:(s + 1) * 128], rhs=WIN(8 + hf)[:, kc, :], start=(kc == 0), stop=(kc == 7))
                            return i_
                        fw.op(pe, mmw, reads=[Rmix, ov, slot[16 + 2 * hf], slot[17 + 2 * hf]], writes=[Rb[bk]])
                for s in range(2):
                    yb = (6, 7) if s == 0 else (0, 1)
                    epilogue(tl, b, s, yb, dst, zi[0])
                    zi[0] = 1 - zi[0]
                if ti + 2 < NT:
                    load_x(src, tiles[ti + 2], b)
                    load_cs(tiles[ti + 2])

        seq = []
        for l in range(NL):
            seq += [(l, "F0"), (l, "KB"), (l, "M"), (l, "F1")]
        if stages is not None:
            seq = [s_ for s_ in seq if s_ in stages]
        n_xstage = sum(1 for s_ in seq if s_[1] != "KB")
        xi = 0
        cur = xin
        have = False
        for si, (l, nm) in enumerate(seq):
            nx = seq[si + 1] if si + 1 < len(seq) else None
            if nm == "KB":
                stage_KB(l, cur, have_w=have)
                have = False
                continue
            xi += 1
            dst = yout if xi == n_xstage else xs
            if nm in ("F0", "F1"):
                nxt = None
                if PREFETCH and nx is not None:
                    if nx[1] == "KB":
                        nxt = ("M", nx[0])
                    elif nx[1] in ("F0", "F1"):
                        nxt = ("F", nx[0], 0 if nx[1] == "F0" else 1)
                stage_F(l, 0 if nm == "F0" else 1, cur, dst, have_w=have, nxt=nxt)
                have = nxt is not None
            else:
                stage_M(l, cur, dst)
                have = False
            cur = dst
        fw.finish(sp)
        fw.run()
    return nc


_W_KEYS = ["w_ada", "b_ada", "ln_g", "ln_b", "ffn_w1", "ffn_w2", "w_in", "w_out", "conv_w", "conv_b", "conv_ln_g", "conv_ln_b"]


def kernel(x_prompt, x_sample, state_ret_fwd, state_ret_bwd, c, c_ctx, w_ada, b_ada, ln_g, ln_b,
           ffn_w1, ffn_w2, w_in, w_out, conv_w, conv_b, conv_ln_g, conv_ln_b, ret_decay_logit):
    f = lambda a: np.ascontiguousarray(np.asarray(a, dtype=np.float32))
    NCORE = 8
    NL = w_in.shape[0]
    B, S = x_prompt.shape[0], x_prompt.shape[1]
    DB, ST = x_sample.shape[0], x_sample.shape[1]
    NP = B // NCORE
    assert DB == NCORE and S == 256
    nc = build(NL=NL, ST=ST, NP=NP)
    wts = dict(w_ada=f(w_ada), b_ada=f(b_ada), ln_g=f(ln_g), ln_b=f(ln_b), ffn_w1=f(ffn_w1), ffn_w2=f(ffn_w2), w_in=f(w_in),
               w_out=f(w_out), conv_w=f(conv_w), conv_b=f(conv_b), conv_ln_g=f(conv_ln_g), conv_ln_b=f(conv_ln_b),
               dlogit=f(ret_decay_logit).reshape(NL, 8))
    in_maps = []
    for i in range(NCORE):
        m = dict(wts)
        m["xin"] = np.concatenate([f(x_sample[i]), f(x_prompt[i * NP:(i + 1) * NP]).reshape(NP * S, D)], axis=0)
        m["stf"] = f(state_ret_fwd[i])
        m["stb"] = f(state_ret_bwd[i])
        m["cond"] = np.stack([f(c[i]), f(c_ctx)], axis=0)
        in_maps.append(m)
    res = run_bass_kernel_spmd(nc, in_maps, core_ids=list(range(NCORE)))
    outs = res.results
    y_sample = np.stack([outs[i]["yout"][:ST] for i in range(NCORE)], axis=0)
    y_prompt = np.concatenate([outs[i]["yout"][ST:].reshape(NP, S, D) for i in range(NCORE)], axis=0)
    nf = np.concatenate([outs[i]["nsf"] for i in range(NCORE)], axis=0)
    nb = np.concatenate([outs[i]["nsb"] for i in range(NCORE)], axis=0)
    return (y_prompt.astype(np.float32), y_sample.astype(np.float32), nf.astype(np.float32), nb.astype(np.float32))
```
